# Optimizing a Trainium2 kernel written in Bass

```python
import math
import jax, jax.numpy as jnp
from jax import lax
import numpy as np

D_MODEL = 2048
BATCH = 2
SEQ = 4096
DEPTH = 1

CHUNK = 64
D_MIX = D_MODEL
D_CONV = D_MIX // 2
D_SSM = D_MIX - D_CONV
CONV_WIDTH = 31
SSM_GROUP = 16
SSM_GROUPS = D_SSM // SSM_GROUP
SSM_STATE = 64
N_MEM = 256
MEM_HEADS = 4
MEM_HEAD_DIM = D_MODEL // MEM_HEADS
PEER_HEADS = 8
PEER_KEYS = 128
PEER_EXPERTS = PEER_KEYS * PEER_KEYS
PEER_DKEY = 128
PEER_HALF = PEER_DKEY // 2
PEER_TOPK = 16
PEER_BLOCK = 128
EPS = 1e-6

kernel_name = "hybrid_conv_s5_peer_memxattn"

F32 = jnp.float32


def rms_norm(x, g):
    xf = x.astype(F32)
    y = xf * lax.rsqrt(jnp.mean(xf * xf, axis=-1, keepdims=True) + EPS)
    return (y * g.astype(F32)).astype(x.dtype)


def layer_norm(x, g, b):
    xf = x.astype(F32)
    mu = jnp.mean(xf, axis=-1, keepdims=True)
    xc = xf - mu
    y = xc * lax.rsqrt(jnp.mean(xc * xc, axis=-1, keepdims=True) + EPS)
    return (y * g.astype(F32) + b.astype(F32)).astype(x.dtype)


def causal_depthwise_conv(x, w, b):
    k = w.shape[0]
    y = lax.conv_general_dilated(
        x, w[:, None, :].astype(x.dtype), window_strides=(1,), padding=[(k - 1, 0)],
        dimension_numbers=("NWC", "WIO", "NWC"), feature_group_count=x.shape[-1])
    return y + b.astype(x.dtype)


def conformer_conv_group(a, gate, conv_w, conv_b, ln_g, ln_b):
    c = a * jax.nn.sigmoid(gate)
    c = causal_depthwise_conv(c, conv_w, conv_b)
    c = layer_norm(c, ln_g, ln_b)
    return c * jax.nn.sigmoid(c)


def s5_group(u, lam_re, lam_im, log_dt, b_re, b_im, c_re, c_im, d, glu_w):
    bsz, length, _ = u.shape
    uf = u.astype(F32).reshape(bsz, length, SSM_GROUPS, SSM_GROUP)
    lam = lax.complex(lam_re.astype(F32), lam_im.astype(F32))
    dt = jnp.exp(log_dt.astype(F32))[:, None]
    lam_bar = jnp.exp(lam * dt)
    b = lax.complex(b_re.astype(F32), b_im.astype(F32))
    b_bar = ((lam_bar - 1.0) / lam)[..., None] * b
    bu = jnp.einsum("blgh,gph->blgp", uf.astype(jnp.complex64), b_bar)
    a = jnp.broadcast_to(lam_bar, bu.shape)

    def combine(left, right):
        a_l, s_l = left
        a_r, s_r = right
        return a_r * a_l, a_r * s_l + s_r

    _, states = lax.associative_scan(combine, (a, bu), axis=1)
    c = lax.complex(c_re.astype(F32), c_im.astype(F32))
    y = jnp.real(jnp.einsum("blgp,ghp->blgh", states, c))
    y = y + d.astype(F32).reshape(SSM_GROUPS, SSM_GROUP) * uf
    y = y.reshape(bsz, length, D_SSM).astype(u.dtype)
    z = jax.nn.gelu(y)
    return z * jax.nn.sigmoid(z @ glu_w)


def mem_cross_attention(h, mem_n, w_q, w_kv, w_o):
    bsz, length, _ = h.shape
    q = (h @ w_q).reshape(bsz, length, MEM_HEADS, MEM_HEAD_DIM)
    kv = (mem_n @ w_kv).reshape(bsz, mem_n.shape[1], 2, MEM_HEADS, MEM_HEAD_DIM)
    k, v = kv[:, :, 0], kv[:, :, 1]
    s = jnp.einsum("blhd,bmhd->bhlm", q, k).astype(F32) * (1.0 / math.sqrt(MEM_HEAD_DIM))
    p = jax.nn.softmax(s, axis=-1).astype(h.dtype)
    o = jnp.einsum("bhlm,bmhd->blhd", p, v).reshape(bsz, length, D_MODEL)
    return o @ w_o


def peer(h, w_q, sub_keys, u_tab, v_tab):
    bsz, length, dm = h.shape
    q = (h @ w_q).astype(F32).reshape(bsz, length, PEER_HEADS, 2, PEER_HALF)
    s = jnp.einsum("blhsk,hsnk->blhsn", q, sub_keys.astype(F32))
    s1, i1 = lax.top_k(s[..., 0, :], PEER_TOPK)
    s2, i2 = lax.top_k(s[..., 1, :], PEER_TOPK)
    cand_s = (s1[..., :, None] + s2[..., None, :]).reshape(bsz, length, PEER_HEADS, PEER_TOPK * PEER_TOPK)
    cand_i = (i1[..., :, None] * PEER_KEYS + i2[..., None, :]).reshape(bsz, length, PEER_HEADS, PEER_TOPK * PEER_TOPK)
    top_s, pos = lax.top_k(cand_s, PEER_TOPK)
    idx = jnp.take_along_axis(cand_i, pos, axis=-1)
    g = jax.nn.softmax(top_s, axis=-1)
    n_tok = bsz * length
    nb = n_tok // PEER_BLOCK
    xb = h.reshape(nb, PEER_BLOCK, dm)
    ib = idx.reshape(nb, PEER_BLOCK, PEER_HEADS * PEER_TOPK)
    gb = g.astype(h.dtype).reshape(nb, PEER_BLOCK, PEER_HEADS * PEER_TOPK)

    def block(args):
        xt, it, gt = args
        u = jnp.take(u_tab, it, axis=0)
        act = jax.nn.gelu(jnp.einsum("td,tkd->tk", xt, u))
        v = jnp.take(v_tab, it, axis=0)
        return jnp.einsum("tk,tkd->td", gt * act, v)

    out = lax.map(block, (xb, ib, gb))
    return out.reshape(bsz, length, dm)


def setup_inputs(seed: int = 0) -> dict:
    key = jax.random.key(seed)
    ks = iter(jax.random.split(key, 40))
    nrm = lambda shape, scale: jax.random.normal(next(ks), shape, F32) * scale
    gain = lambda shape: 1.0 + 0.05 * jax.random.normal(next(ks), shape, F32)
    L = DEPTH
    n_in = 2 * D_CONV + D_SSM
    lam_re = -0.5 + 0.01 * jax.random.normal(next(ks), (L, SSM_GROUPS, SSM_STATE), F32)
    lam_im = (math.pi * jnp.arange(SSM_STATE, dtype=F32))[None, None, :] + 0.01 * jax.random.normal(next(ks), (L, SSM_GROUPS, SSM_STATE), F32)
    log_dt = jax.random.uniform(next(ks), (L, SSM_GROUPS), F32, math.log(1e-3), math.log(1e-1))
    return {
        "x": nrm((BATCH, SEQ, D_MODEL), 1.0),
        "mem": nrm((BATCH, N_MEM, D_MODEL), 1.0),
        "norm_mix_g": gain((L, D_MODEL)),
        "w_in": nrm((L, D_MODEL, n_in), D_MODEL ** -0.5),
        "conv_w": nrm((L, CONV_WIDTH, D_CONV), CONV_WIDTH ** -0.5),
        "conv_b": nrm((L, D_CONV), 0.02),
        "conv_ln_g": gain((L, D_CONV)),
        "conv_ln_b": nrm((L, D_CONV), 0.02),
        "ssm_lam_re": lam_re,
        "ssm_lam_im": lam_im,
        "ssm_log_dt": log_dt,
        "ssm_b_re": nrm((L, SSM_GROUPS, SSM_STATE, SSM_GROUP), (2.0 * SSM_GROUP) ** -0.5),
        "ssm_b_im": nrm((L, SSM_GROUPS, SSM_STATE, SSM_GROUP), (2.0 * SSM_GROUP) ** -0.5),
        "ssm_c_re": nrm((L, SSM_GROUPS, SSM_GROUP, SSM_STATE), (2.0 * SSM_STATE) ** -0.5),
        "ssm_c_im": nrm((L, SSM_GROUPS, SSM_GROUP, SSM_STATE), (2.0 * SSM_STATE) ** -0.5),
        "ssm_d": nrm((L, D_SSM), 1.0),
        "ssm_glu_w": nrm((L, D_SSM, D_SSM), D_SSM ** -0.5),
        "grp_norm_conv_g": gain((L, D_CONV)),
        "grp_norm_ssm_g": gain((L, D_SSM)),
        "w_out": nrm((L, D_MIX, D_MODEL), D_MIX ** -0.5),
        "norm_mem_g": gain((L, D_MODEL)),
        "mem_norm_g": gain((L, D_MODEL)),
        "w_q_mem": nrm((L, D_MODEL, D_MODEL), D_MODEL ** -0.5),
        "w_kv_mem": nrm((L, D_MODEL, 2 * D_MODEL), D_MODEL ** -0.5),
        "w_o_mem": nrm((L, D_MODEL, D_MODEL), D_MODEL ** -0.5),
        "norm_ffn_g": gain((L, D_MODEL)),
        "peer_w_q": nrm((L, D_MODEL, PEER_HEADS * PEER_DKEY), D_MODEL ** -0.5),
        "peer_keys": nrm((L, PEER_HEADS, 2, PEER_KEYS, PEER_HALF), PEER_HALF ** -0.5),
        "peer_u": nrm((L, PEER_EXPERTS, D_MODEL), D_MODEL ** -0.5),
        "peer_v": nrm((L, PEER_EXPERTS, D_MODEL), PEER_TOPK ** -0.5),
        "final_norm_g": gain((D_MODEL,)),
    }


def reference(x, mem, norm_mix_g, w_in, conv_w, conv_b, conv_ln_g, conv_ln_b,
              ssm_lam_re, ssm_lam_im, ssm_log_dt, ssm_b_re, ssm_b_im, ssm_c_re, ssm_c_im,
              ssm_d, ssm_glu_w, grp_norm_conv_g, grp_norm_ssm_g, w_out,
              norm_mem_g, mem_norm_g, w_q_mem, w_kv_mem, w_o_mem,
              norm_ffn_g, peer_w_q, peer_keys, peer_u, peer_v, final_norm_g):
    h = x
    for i in range(DEPTH):
        hn = rms_norm(h, norm_mix_g[i])
        proj = hn @ w_in[i]
        conv_a = proj[..., :D_CONV]
        conv_gate = proj[..., D_CONV:2 * D_CONV]
        ssm_u = proj[..., 2 * D_CONV:]
        yc = conformer_conv_group(conv_a, conv_gate, conv_w[i], conv_b[i], conv_ln_g[i], conv_ln_b[i])
        ys = s5_group(ssm_u, ssm_lam_re[i], ssm_lam_im[i], ssm_log_dt[i], ssm_b_re[i], ssm_b_im[i],
                      ssm_c_re[i], ssm_c_im[i], ssm_d[i], ssm_glu_w[i])
        y = jnp.concatenate([rms_norm(yc, grp_norm_conv_g[i]), rms_norm(ys, grp_norm_ssm_g[i])], axis=-1)
        h = h + y @ w_out[i]
        hn = rms_norm(h, norm_mem_g[i])
        mem_n = rms_norm(mem, mem_norm_g[i])
        h = h + mem_cross_attention(hn, mem_n, w_q_mem[i], w_kv_mem[i], w_o_mem[i])
        hn = rms_norm(h, norm_ffn_g[i])
        h = h + peer(hn, peer_w_q[i], peer_keys[i], peer_u[i], peer_v[i])
    return rms_norm(h, final_norm_g)
```

```python
import math
import numpy as np
import concourse.bass as bass
import concourse.mybir as mybir
from concourse.bass_utils import run_bass_kernel_spmd

F32 = mybir.dt.float32
F32R = mybir.dt.float32r
BF16 = mybir.dt.bfloat16
U32 = mybir.dt.uint32
I32 = mybir.dt.int32
ALU = mybir.AluOpType
AF = mybir.ActivationFunctionType
AX = mybir.AxisListType

NCORES = 8
D = 2048
KT = 16
TOWN = 1024
TALL = 4096
BLK = 512
NBLK = TALL // BLK
EPS = 1e-6
TWO_PI_S = 6.28318


class Sem:
    def __init__(self, nc, name, step):
        self.h = nc.semaphore(name).__enter__()
        self.name = name
        self.cnt = 0
        self.step = step


class Buf:
    def __init__(self, name, t=None):
        self.name = name
        self.t = t
        self.last_w = None
        self.readers = {}

    def __getitem__(self, k):
        return self.t[k]

    def ap(self, offset, dims):
        full = self.t[:]
        pstep = full.ap[0][0]
        return bass.AP(tensor=self.t, offset=offset, ap=[[pstep, full.ap[0][1]]] + [list(d) for d in dims])

    def app(self, p0, pn, offset, dims):
        full = self.t[:]
        pstep = full.ap[0][0]
        return bass.AP(tensor=self.t, offset=p0 * pstep + offset, ap=[[pstep, pn]] + [list(d) for d in dims])


class K:
    def __init__(self, nc):
        self.nc = nc
        self.eng = {"pe": nc.tensor, "act": nc.scalar, "dve": nc.vector,
                    "pool": nc.gpsimd, "sp": nc.sync}
        self.qsem = {q: Sem(nc, "q_" + q, 1) for q in self.eng}
        self.seen = {q: {} for q in self.eng}
        self.n_inst = 0
        self.n_wait = 0
        self.dsems = []
        self._stack = []
        self._rstack = []
        self.fence = {}

    def sbuf(self, name, shape, dt=F32, side="left"):
        self._uid = getattr(self, "_uid", 0) + 1
        cm = self.nc.sbuf_tensor(f"sb{self._uid}_{name}", list(shape), dt, side=side)
        t = cm.__enter__()
        b = Buf(name, t)
        b.readers = dict(self.fence)
        (self._stack if side == "left" else self._rstack).append((cm, b))
        return b

    def psum(self, name, shape, dt=F32):
        cm = self.nc.psum_tensor(name, list(shape), dt)
        t = cm.__enter__()
        b = Buf(name, t)
        self._stack.append((cm, b))
        return b

    def mark(self, side="left"):
        return len(self._stack if side == "left" else self._rstack)

    def release(self, mark, side="left"):
        st = self._stack if side == "left" else self._rstack
        while len(st) > mark:
            cm, b = st.pop()
            deps = list(b.readers.items()) + ([b.last_w] if b.last_w else [])
            for sm, tick in deps:
                if self.fence.get(sm, 0) < tick:
                    self.fence[sm] = tick
            cm.__exit__(None, None, None)

    def dsem(self, name):
        s = Sem(self.nc, "d_" + name, 16)
        self.dsems.append(s)
        return s

    def _waits(self, q, reads, writes, skip=None):
        need = {}
        myq = self.qsem[q]

        def add(dep, same_ok):
            if dep is None:
                return
            s, tick = dep
            if s is skip:
                return
            if s is myq and not same_ok:
                return
            if need.get(s, 0) < tick:
                need[s] = tick

        same = q != "pe"
        for b in reads:
            add(b.last_w, same)
        for b in writes:
            add(b.last_w, same)
            for s, tick in b.readers.items():
                add((s, tick), same)
        eng = self.eng[q]
        seen = self.seen[q]
        for s, tick in need.items():
            if seen.get(s, 0) >= tick:
                continue
            eng.wait_ge(s.h, tick)
            self.n_wait += 1
            seen[s] = tick

    def _record(self, sem, reads, writes):
        tick = sem.cnt
        for b in reads:
            if b.readers.get(sem, 0) < tick:
                b.readers[sem] = tick
        for b in writes:
            b.last_w = (sem, tick)
            b.readers = {}

    def op(self, q, fn, reads=(), writes=()):
        self._waits(q, reads, writes)
        ins = fn(self.eng[q])
        s = self.qsem[q]
        s.cnt += 1
        ins.then_inc(s.h, 1)
        self.n_inst += 1
        self._record(s, reads, writes)
        return ins

    def dma(self, q, out, in_, dsem, reads=(), writes=(), **kw):
        self._waits(q, reads, writes, skip=dsem)
        ins = self.eng[q].dma_start(out=out, in_=in_, **kw)
        dsem.cnt += 16
        ins.then_inc(dsem.h, 16)
        self.n_inst += 1
        self._record(dsem, reads, writes)
        return ins

    def dma_custom(self, q, fn, dsem, reads=(), writes=()):
        self._waits(q, reads, writes)
        ins = fn(self.eng[q])
        dsem.cnt += 16
        ins.then_inc(dsem.h, 16)
        self.n_inst += 1
        self._record(dsem, reads, writes)
        return ins

    def group_end(self, dsem, bufs):
        for b in bufs:
            b.last_w = (dsem, dsem.cnt)

    def finish(self, q="sp"):
        eng = self.eng[q]
        for s in list(self.qsem.values()) + self.dsems:
            if s.cnt > 0 and s is not self.qsem[q]:
                eng.wait_ge(s.h, s.cnt)

    def mm(self, out, lhsT, rhs, start, stop, reads, writes):
        return self.op("pe", lambda e: e.matmul(out, lhsT=lhsT, rhs=rhs, start=start, stop=stop),
                       reads=reads, writes=writes)

    def tr(self, out, in_, ident, reads, writes):
        return self.op("pe", lambda e: e.transpose(out, in_, ident), reads=reads, writes=writes)

    def act(self, out, in_, func, reads, writes, **kw):
        return self.op("act", lambda e: e.activation(out=out, in_=in_, func=func, **kw), reads=reads, writes=writes)

    def tt(self, out, in0, in1, op, reads, writes, q="dve"):
        return self.op(q, lambda e: e.tensor_tensor(out=out, in0=in0, in1=in1, op=op), reads=reads, writes=writes)

    def ts(self, out, in0, s1, s2, op0, op1, reads, writes, q="dve"):
        if s2 is None:
            return self.op(q, lambda e: e.tensor_scalar(out, in0, s1, None, op0=op0), reads=reads, writes=writes)
        return self.op(q, lambda e: e.tensor_scalar(out, in0, s1, s2, op0=op0, op1=op1), reads=reads, writes=writes)

    def stt(self, out, in0, scalar, in1, op0, op1, reads, writes):
        return self.op("dve", lambda e: e.scalar_tensor_tensor(out=out, in0=in0, scalar=scalar, in1=in1,
                                                               op0=op0, op1=op1), reads=reads, writes=writes)

    def copy(self, out, in_, reads, writes, q="dve"):
        return self.op(q, lambda e: e.tensor_copy(out, in_), reads=reads, writes=writes)

    def memset(self, ap, val, writes, q="dve"):
        return self.op(q, lambda e: e.memset(ap, val), writes=writes)


def build_program(stage="full"):
    nc = bass.Bass("TRN2", target_bir_lowering=False)

    def din(name, shape, dt=F32):
        return nc.dram_tensor(name, list(shape), dt, kind="ExternalInput").ap()

    xs_d = din("xs", [TALL, D])
    mem_d = din("mem", [256, D])
    vecs_d = din("vecs", [112, 128])
    convw_d = din("conv_w", [31, 1024])
    lre_d = din("lam_re", [32, 128])
    lim_d = din("lam_im", [32, 128])
    ldt_d = din("log_dt", [32, 2])
    bre_d = din("b_re", [32, 2048])
    bim_d = din("b_im", [32, 2048])
    cre_d = din("c_re", [32, 2048])
    cim_d = din("c_im", [32, 2048])
    w_in_d = din("w_in", [D, 3072])
    glu_d = din("glu_w", [1024, 1024])
    w_out_d = din("w_out", [D, D])
    w_q_d = din("w_q", [D, D])
    w_kv_d = din("w_kv", [D, 2 * D])
    w_o_d = din("w_o", [D, D])
    pwq_d = din("peer_w_q", [D, 1024])
    pkeys_d = din("peer_keys", [8, 2, 128, 64])
    pu_d = din("peer_u", [16384, D])
    pv_d = din("peer_v", [16384, D])
    fng_d = din("final_g", [1, D])
    y_d = nc.dram_tensor("y", [TOWN, D], F32, kind="ExternalOutput").ap()
    dbg_d = None
    if stage != "full":
        dbg_d = nc.dram_tensor("dbg", [TOWN, D], F32, kind="ExternalOutput").ap()

    puv_bf = nc.dram_tensor("puv_bf", [16384, 2 * D], BF16, kind="Internal").ap()
    tblU = Buf("tblUV")

    k = K(nc)

    ident_f = k.sbuf("ident_f", [128, 128], F32, side="right")
    ident_b = k.sbuf("ident_b", [128, 128], BF16, side="right")
    ones_r = k.sbuf("ones_r", [128, 128], F32R, side="right")
    iot = k.sbuf("iot", [128, 128], F32, side="right")
    vT = k.sbuf("vT", [128, 112], F32, side="right")
    pidx = k.sbuf("pidx", [128, 1], F32, side="right")
    negh = k.sbuf("negh", [128, 1], F32, side="right")
    psd = [k.psum(f"psd{i}", [128, 1024], F32) for i in range(4)]
    ps = [Buf(f"ps{i}", psd[i // 2][:, (i % 2) * 512:(i % 2 + 1) * 512]) for i in range(8)]
    d_setup = k.dsem("setup")
    d_out = k.dsem("out")

    k.op("pool", lambda e: e.iota(iot[:], pattern=[[1, 128]], base=0, channel_multiplier=-1,
                                  allow_small_or_imprecise_dtypes=True), writes=[iot])
    k.ts(ident_f[:], iot[:], 0.0, None, ALU.is_equal, None, [iot], [ident_f])
    k.copy(ident_b[:], ident_f[:], [ident_f], [ident_b])
    k.memset(iot[:], 1.0, [iot])
    k.copy(ones_r[:], iot[:], [iot], [ones_r])
    k.op("pool", lambda e: e.iota(pidx[:], pattern=[[0, 1]], base=0, channel_multiplier=1,
                                  allow_small_or_imprecise_dtypes=True), writes=[pidx])

    k.memset(negh[:], -0.5, [negh])
    m0 = k.mark()
    vraw = k.sbuf("vraw", [112, 128], F32)
    k.dma("sp", vraw[:], vecs_d, d_setup, writes=[vraw])
    k.tr(ps[0][:, 0:112], vraw[:], ident_f[0:112, 0:112], [vraw, ident_f], [ps[0]])
    k.copy(vT[:], ps[0][:, 0:112], [ps[0]], [vT])
    k.release(m0)
    G_MIX, G_MEM, G_MEMN, G_FFN = 0, 16, 32, 48
    V_CB, V_LNG, V_LNB, V_GC, V_GS, V_SD = 64, 72, 80, 88, 96, 104

    NB = {}
    NXB = 4
    d_x = [k.dsem(f"x{i}") for i in range(NXB)]
    cnt = {"x": 0}

    def alloc_norm_bufs(with_x=True):
        if with_x:
            NB["xtile"] = [k.sbuf(f"xtile{i}", [128, D], F32) for i in range(NXB)]
        NB["xsb"] = [k.sbuf(f"xsb{i}", [128, D], BF16) for i in range(2)]
        NB["junk"] = k.sbuf("junk", [128, D], BF16)
        NB["stat"] = [k.sbuf(f"stat{i}", [128, 4], F32) for i in range(2)]

    def load_tok_tile(src_ap):
        i = cnt["x"] % 2
        cnt["x"] += 1
        k.dma("sp", NB["xtile"][i][:], src_ap, d_x[i], writes=[NB["xtile"][i]])
        return i

    def pool_rsqrt(buf, out_ap, in_ap, scale, shape):
        k.ts(out_ap, in_ap, scale, EPS, ALU.mult, ALU.add, [buf], [buf], q="pool")
        nh = negh[:, 0:1] if shape[1] == 1 else negh[:, 0:1].to_broadcast(list(shape))
        k.tt(out_ap, out_ap, nh, ALU.pow, [buf, negh], [buf], q="pool")

    def rstd_of(src_buf, src_ap, st, n):
        junk = NB["junk"]
        k.act(junk[:, 0:n], src_ap, AF.Square, [src_buf], [junk, st], accum_out=st[:, 0:1])
        pool_rsqrt(st, st[:, 1:2], st[:, 0:1], 1.0 / n, [128, 1])

    def fold_g(wb, g_off, ncols):
        for kt in range(KT):
            k.ts(wb[:, kt, 0:ncols], wb[:, kt, 0:ncols], vT[:, g_off + kt:g_off + kt + 1], None, ALU.mult, None,
                 [wb, vT], [wb])

    def norm_A(src_buf, src_ap, i):
        st = NB["stat"][i]
        rstd_of(src_buf, src_ap, st, D)
        k.ts(NB["xsb"][i][:], src_ap, st[:, 1:2], None, ALU.mult, None, [src_buf, st], [NB["xsb"][i]])

    def norm_B(dstT_buf, dst_ap_fn, pbank, i):
        xsb = NB["xsb"]
        for half in range(2):
            pb = ps[pbank + half]
            pv = pb[:].bitcast(BF16)
            for j in range(8):
                kt = half * 8 + j
                k.tr(pv[:, j * 128:(j + 1) * 128], xsb[i][:, kt * 128:(kt + 1) * 128], ident_b[:],
                     [xsb[i], ident_b], [pb])
            pin = pv.rearrange("p (j t) -> p j t", j=8)
            if half == 0:
                k.act(dst_ap_fn(half), pin, AF.Copy, [pb], [dstT_buf])
            else:
                k.copy(dst_ap_fn(half), pin, [pb], [dstT_buf])

    def norm_pipeline(items, post=None):
        n = len(items)

        def is_dram(t):
            return not isinstance(items[t][0], tuple)

        def issue_load(t):
            if is_dram(t):
                i = t % NXB
                k.dma("sp", NB["xtile"][i][:], items[t][0], d_x[i], writes=[NB["xtile"][i]])

        def doA(t):
            if is_dram(t):
                xb = NB["xtile"][t % NXB]
                norm_A(xb, xb[:], t % 2)
            else:
                norm_A(items[t][0][0], items[t][0][1], t % 2)
        for t in range(min(NXB - 1, n)):
            issue_load(t)
        doA(0)
        for t in range(n):
            if t + NXB - 1 < n:
                issue_load(t + NXB - 1)
            if t + 1 < n:
                doA(t + 1)
            norm_B(items[t][1], items[t][2], 0, t % 2)
            if post is not None:
                post(t)

    uT = k.sbuf("uT", [128, 8, TALL], BF16)
    mA = k.mark()
    alloc_norm_bufs()
    xtile = NB["xtile"]
    w_ssm = k.sbuf("w_ssm", [128, KT, 1024], BF16)
    d_w = k.dsem("w0")
    for kt in range(KT):
        k.dma("pool", w_ssm[:, kt, :], w_in_d[kt * 128:(kt + 1) * 128, 2048:3072], d_w, writes=[w_ssm])
    k.group_end(d_w, [w_ssm])
    fold_g(w_ssm, G_MIX, 1024)
    xnTs = [k.sbuf(f"xnT{i}", [128, KT, BLK], BF16) for i in range(2)]
    def a1_src(r0):
        def f():
            i = load_tok_tile(xs_d[r0:r0 + 128, :])
            return xtile[i], xtile[i][:]
        return f

    def a1_post(t):
        if t % 4 != 3:
            return
        b = t // 4
        xnT = xnTs[b % 2]
        for ct in range(8):
            pb = ps[2 + ct % 2]
            for kt in range(KT):
                k.mm(pb[:], w_ssm[:, kt, ct * 128:(ct + 1) * 128], xnT[:, kt, :], kt == 0, kt == KT - 1,
                     [w_ssm, xnT], [pb])
            k.act(uT[:, ct, b * BLK:(b + 1) * BLK], pb[:], AF.Copy, [pb], [uT])

    items = []
    for b in range(NBLK):
        for tl in range(4):
            xnT = xnTs[b % 2]
            r0 = b * BLK + tl * 128
            items.append((xs_d[r0:r0 + 128, :], xnT,
                          lambda half, tl=tl, xnT=xnT: xnT.ap(half * 8 * BLK + tl * 128, [[BLK, 8], [1, 128]])))
    norm_pipeline(items, a1_post)
    k.release(mA)

    mS = k.mark()
    s5w = {nm: k.sbuf(nm, [128, 32, 128], BF16) for nm in ("Bre_pad", "Bim_pad", "WA", "NWA", "NWB")}
    parT = k.sbuf("parT", [128, 18, 32], F32)
    jvec = k.sbuf("jvec", [128, BLK], F32)
    k.op("pool", lambda e: e.iota(jvec[:], pattern=[[1, BLK]], base=0, channel_multiplier=0,
                                  allow_small_or_imprecise_dtypes=True), writes=[jvec])
    jrev = k.sbuf("jrev", [128, BLK], F32)
    k.op("pool", lambda e: e.iota(jrev[:], pattern=[[-1, BLK]], base=BLK - 1, channel_multiplier=0,
                                  allow_small_or_imprecise_dtypes=True), writes=[jrev])
    mP = k.mark()
    la = {nm: k.sbuf("la_" + nm, [32, 128], F32) for nm in
          ("lre", "lim", "dtb", "ar", "th", "rho", "phi", "f", "fa", "sinv", "cosv", "lbr", "lbi", "nr", "den",
           "t1", "t2", "cr", "ci", "f512", "g512", "ga512")}
    lai = k.sbuf("la_i", [32, 128], I32)
    par = k.sbuf("par", [32, 18, 128], F32)
    ldt = k.sbuf("ldt", [32, 2], F32)
    dt2 = k.sbuf("dt2", [32, 2], F32)
    k.dma("sp", la["lre"][:], lre_d, d_setup, writes=[la["lre"]])
    k.dma("sp", la["lim"][:], lim_d, d_setup, writes=[la["lim"]])
    k.dma("sp", ldt[:], ldt_d, d_setup, writes=[ldt])
    k.group_end(d_setup, [la["lre"], la["lim"], ldt, vraw])
    k.act(dt2[:], ldt[:], AF.Exp, [ldt], [dt2])
    k.copy(la["dtb"].ap(0, [[64, 2], [1, 64]]), dt2.ap(0, [[1, 2], [0, 64]]), [dt2], [la["dtb"]])

    def L(n):
        return la[n][:]
    k.tt(L("ar"), L("lre"), L("dtb"), ALU.mult, [la["lre"], la["dtb"]], [la["ar"]])
    k.tt(L("th"), L("lim"), L("dtb"), ALU.mult, [la["lim"], la["dtb"]], [la["th"]])
    k.act(L("rho"), L("ar"), AF.Exp, [la["ar"]], [la["rho"]])
    k.ts(L("phi"), L("th"), 1.0 / (2 * math.pi), None, ALU.mult, None, [la["th"]], [la["phi"]])

    def frac(dst, src):
        k.copy(lai[:], la[src][:], [la[src]], [lai])
        k.tt(la[dst][:], la[src][:], lai[:], ALU.subtract, [la[src], lai], [la[dst]])

    def sincos(sin_dst, cos_dst, fsrc, tmp):
        k.act(la[sin_dst][:], la[fsrc][:], AF.Sin, [la[fsrc]], [la[sin_dst]], scale=TWO_PI_S)
        k.act(la[tmp][:], la[fsrc][:], AF.Abs, [la[fsrc]], [la[tmp]])
        k.act(la[cos_dst][:], la[tmp][:], AF.Sin, [la[tmp]], [la[cos_dst]], scale=-TWO_PI_S, bias=math.pi / 2)

    frac("f", "phi")
    sincos("sinv", "cosv", "f", "fa")
    k.tt(L("lbr"), L("rho"), L("cosv"), ALU.mult, [la["rho"], la["cosv"]], [la["lbr"]])
    k.tt(L("lbi"), L("rho"), L("sinv"), ALU.mult, [la["rho"], la["sinv"]], [la["lbi"]])
    k.ts(L("nr"), L("lbr"), -1.0, None, ALU.add, None, [la["lbr"]], [la["nr"]])
    k.tt(L("t1"), L("lre"), L("lre"), ALU.mult, [la["lre"]], [la["t1"]])
    k.tt(L("t2"), L("lim"), L("lim"), ALU.mult, [la["lim"]], [la["t2"]])
    k.tt(L("den"), L("t1"), L("t2"), ALU.add, [la["t1"], la["t2"]], [la["den"]])
    k.op("dve", lambda e: e.reciprocal(L("den"), L("den")), reads=[la["den"]], writes=[la["den"]])
    k.tt(L("t1"), L("nr"), L("lre"), ALU.mult, [la["nr"], la["lre"]], [la["t1"]])
    k.tt(L("t2"), L("lbi"), L("lim"), ALU.mult, [la["lbi"], la["lim"]], [la["t2"]])
    k.tt(L("cr"), L("t1"), L("t2"), ALU.add, [la["t1"], la["t2"]], [la["cr"]])
    k.tt(L("cr"), L("cr"), L("den"), ALU.mult, [la["cr"], la["den"]], [la["cr"]])
    k.tt(L("t1"), L("lbi"), L("lre"), ALU.mult, [la["lbi"], la["lre"]], [la["t1"]])
    k.tt(L("t2"), L("nr"), L("lim"), ALU.mult, [la["nr"], la["lim"]], [la["t2"]])
    k.tt(L("ci"), L("t1"), L("t2"), ALU.subtract, [la["t1"], la["t2"]], [la["ci"]])
    k.tt(L("ci"), L("ci"), L("den"), ALU.mult, [la["ci"], la["den"]], [la["ci"]])
    k.ts(L("f512"), L("f"), float(BLK), None, ALU.mult, None, [la["f"]], [la["f512"]])
    frac("g512", "f512")
    k.copy(par[:, 0, :], L("f"), [la["f"]], [par])
    k.copy(par[:, 1, :], L("rho"), [la["rho"]], [par])
    sincos("t1", "t2", "g512", "ga512")
    k.copy(par[:, 2, :], L("t2"), [la["t2"]], [par])
    k.copy(par[:, 3, :], L("t1"), [la["t1"]], [par])
    k.ts(par[:, 4, :], L("t1"), -1.0, None, ALU.mult, None, [la["t1"]], [par])
    k.copy(par[:, 5, :], L("ar"), [la["ar"]], [par])
    for b_ in range(6):
        kk = 5 - b_
        k.ts(L("f512"), L("g512"), float(kk), None, ALU.mult, None, [la["g512"]], [la["f512"]])
        frac("ga512", "f512")
        sincos("t1", "t2", "ga512", "fa")
        k.act(L("lbr"), L("ar"), AF.Exp, [la["ar"]], [la["lbr"]], scale=float(BLK * kk))
        k.tt(par[:, 6 + b_, :], L("lbr"), L("t2"), ALU.mult, [la["lbr"], la["t2"]], [par])
        k.tt(par[:, 12 + b_, :], L("lbr"), L("t1"), ALU.mult, [la["lbr"], la["t1"]], [par])
    for i in range(18):
        pb_ = ps[i // 16]
        k.tr(pb_[:, (i % 16) * 32:(i % 16 + 1) * 32], par[:, i, :], ident_f[0:32, 0:32], [par, ident_f], [pb_])
    k.copy(parT[:, 0:16, :].rearrange("p a b -> p (a b)"), ps[0][:, 0:512], [ps[0]], [parT])
    k.copy(parT[:, 16:18, :].rearrange("p a b -> p (a b)"), ps[1][:, 0:64], [ps[1]], [parT])

    XT = {nm: k.sbuf("XT_" + nm, [128, 32, 16], F32) for nm in ("bre", "bim", "cre", "cim")}
    mP1 = k.mark()
    raw_re = k.sbuf("raw_re", [32, 2048], F32)
    raw_im = k.sbuf("raw_im", [32, 2048], F32)
    Bre = k.sbuf("Bre", [32, 2048], F32)
    Bim = k.sbuf("Bim", [32, 2048], F32)
    tb = k.sbuf("tb", [32, 2048], F32)
    d_raw = k.dsem("raw")
    k.dma("sp", raw_re[:], bre_d, d_raw, writes=[raw_re])
    k.dma("sp", raw_im[:], bim_d, d_raw, writes=[raw_im])
    k.group_end(d_raw, [raw_re, raw_im])

    def v3(buf):
        return buf.ap(0, [[16, 128], [1, 16]])

    def bc(buf):
        return buf.ap(0, [[1, 128], [0, 16]])
    k.tt(v3(Bre), v3(raw_re), bc(la["cr"]), ALU.mult, [raw_re, la["cr"]], [Bre])
    k.tt(v3(tb), v3(raw_im), bc(la["ci"]), ALU.mult, [raw_im, la["ci"]], [tb])
    k.tt(Bre[:], Bre[:], tb[:], ALU.subtract, [Bre, tb], [Bre])
    k.tt(v3(Bim), v3(raw_im), bc(la["cr"]), ALU.mult, [raw_im, la["cr"]], [Bim])
    k.tt(v3(tb), v3(raw_re), bc(la["ci"]), ALU.mult, [raw_re, la["ci"]], [tb])
    k.tt(Bim[:], Bim[:], tb[:], ALU.add, [Bim, tb], [Bim])

    def to_XT(nm, in_fn, rd):
        pb = ps[1]
        for h in range(16):
            k.tr(pb[:, h * 32:(h + 1) * 32], in_fn(h), ident_f[0:32, 0:32], rd + [ident_f], [pb])
        k.copy(XT[nm].ap(0, [[1, 16], [16, 32]]), pb[:].rearrange("p (h a) -> p h a", h=16), [pb], [XT[nm]])

    to_XT("bre", lambda h: Bre.ap(h, [[16, 128]]), [Bre])
    to_XT("bim", lambda h: Bim.ap(h, [[16, 128]]), [Bim])
    k.dma("sp", raw_re[:], cre_d, d_raw, writes=[raw_re])
    k.dma("sp", raw_im[:], cim_d, d_raw, writes=[raw_im])
    k.group_end(d_raw, [raw_re, raw_im])
    for nm, src in (("cre", raw_re), ("cim", raw_im)):
        k.copy(tb.ap(0, [[128, 16], [64, 2], [1, 64]]), src.ap(0, [[64, 16], [1024, 2], [1, 64]]), [src], [tb])
        to_XT(nm, lambda h: tb[:, h * 128:(h + 1) * 128], [tb])
    k.release(mP1)

    mask2 = k.sbuf("mask2", [128, 4], F32)
    k.ts(mask2[:, 0:1], pidx[:], 64.0, None, ALU.is_lt, None, [pidx], [mask2])
    k.ts(mask2[:, 1:2], pidx[:], 64.0, None, ALU.is_ge, None, [pidx], [mask2])
    k.ts(mask2[:, 2:4], mask2[:, 0:2], -1.0, None, ALU.mult, None, [mask2], [mask2])

    padf = k.sbuf("padf", [128, 32, 128], F32)

    def build_padded(dst_buf, src, neg, dt_direct):
        tgt = dst_buf if dt_direct else padf
        k.memset(tgt[:], 0.0, [tgt])
        for g2 in range(2):
            for b4 in range(4):
                o_ap = tgt.ap(b4 * 128 + b4 * 32 + 16 * g2, [[512, 8], [1, 16]])
                i_ap = src.ap(b4 * 16, [[64, 8], [1, 16]])
                mcol = mask2[:, (2 if neg else 0) + g2:(2 if neg else 0) + g2 + 1]
                k.ts(o_ap, i_ap, mcol, None, ALU.mult, None, [src, mask2], [tgt])

    build_padded(s5w["WA"], XT["cre"], False, True)
    build_padded(s5w["NWA"], XT["cre"], True, True)
    build_padded(s5w["NWB"], XT["cim"], True, True)
    for nm, dst in (("bre", "Bre_pad"), ("bim", "Bim_pad")):
        build_padded(None, XT[nm], False, False)
        for a in range(8):
            pb = ps[2 + a % 2]
            for b4 in range(4):
                k.tr(pb[:, b4 * 128:(b4 + 1) * 128], padf[:, a * 4 + b4, :], ident_f[:], [padf, ident_f], [pb])
            k.copy(s5w[dst][:, a * 4:(a + 1) * 4, :].rearrange("p a b -> p (a b)"), pb[:], [pb], [s5w[dst]])
    k.release(mP)


    mark_yn = k.mark("right")
    yn = k.sbuf("yn", [128, 16, TOWN], BF16, side="right")
    mZ = k.mark("right")
    zb = k.sbuf("zb", [128, 8, TOWN], BF16, side="right")
    tabC = [k.sbuf(f"tabC{i}", [128, BLK], F32) for i in range(2)]
    tabS = [k.sbuf(f"tabS{i}", [128, BLK], F32) for i in range(2)]
    tabWC = [k.sbuf(f"tWC{i}", [128, BLK], F32) for i in range(2)]
    tabWS = [k.sbuf(f"tWS{i}", [128, BLK], F32) for i in range(2)]
    g_ang = k.sbuf("g_ang", [128, BLK], F32)
    g_fr = k.sbuf("g_fr", [128, BLK], F32)
    g_fab = g_ang
    g_wt = g_fr
    pacc = k.sbuf("pacc", [128, 6, 4], F32)
    psr = k.sbuf("psr", [128, 2, 6], F32)
    pcc = k.sbuf("pcc", [128, 8], F32)
    j6 = k.sbuf("j6", [128, 6], F32)
    print("S5 loop: sbuf bytes remaining", nc.sbuf_bytes_remaining)
    angi = k.sbuf("angi", [128, BLK], I32)
    tmp = [k.sbuf(f"s5t{i}", [128, BLK], F32) for i in range(4)]
    ang, fr, fab = tmp[0], tmp[1], tmp[2]
    bh0 = [k.sbuf(f"bh{j}", [128, BLK], F32) for j in range(2)]
    zz0 = [k.sbuf(f"zz{j}", [128, BLK], F32) for j in range(2)]
    bh = [bh0, bh0]
    zz = [zz0, zz0]
    pp = [[k.sbuf(f"pp{i}{j}", [128, BLK], BF16) for j in range(4)] for i in range(2)]
    init = [k.sbuf(f"init{i}", [128, 2], F32) for i in range(2)]
    tini = k.sbuf("tini", [128, 2], F32)
    ytmp = tmp[0]
    d_cv = k.dsem("cv")
    CR = 512
    for c in range(16384 // CR):
        k.dma("pool", puv_bf[c * CR:(c + 1) * CR, 0:D], pu_d[c * CR:(c + 1) * CR, :], d_cv, writes=[tblU])
        k.dma("pool", puv_bf[c * CR:(c + 1) * CR, D:2 * D], pv_d[c * CR:(c + 1) * CR, :], d_cv, writes=[tblU])
    k.group_end(d_cv, [tblU])
    it = 0
    def gen_tables_front(P):
        sl = P % 2
        k.ts(g_ang[:], jvec[:], parT[:, 0, P:P + 1], None, ALU.mult, None, [jvec, parT], [g_ang])
        k.copy(angi[:], g_ang[:], [g_ang], [angi])
        k.tt(g_fr[:], g_ang[:], angi[:], ALU.subtract, [g_ang, angi], [g_fr])
        k.act(tabS[sl][:], g_fr[:], AF.Sin, [g_fr], [tabS[sl]], scale=TWO_PI_S)
        k.act(g_fab[:], g_fr[:], AF.Abs, [g_fr], [g_fab])
        k.act(tabC[sl][:], g_fab[:], AF.Sin, [g_fab], [tabC[sl]], scale=-TWO_PI_S, bias=math.pi / 2)
        k.act(g_wt[:], jrev[:], AF.Exp, [jrev, parT], [g_wt], scale=parT[:, 5, P:P + 1])

    def gen_tables_back(P):
        sl = P % 2
        k.tt(tabWC[sl][:], tabC[sl][:], g_wt[:], ALU.mult, [tabC[sl], g_wt], [tabWC[sl]])
        k.tt(tabWS[sl][:], tabS[sl][:], g_wt[:], ALU.mult, [tabS[sl], g_wt], [tabWS[sl]])

    gen_tables_front(0)
    gen_tables_back(0)
    for P in range(32):
        ct = P // 4
        tC, tS, tWC, tWS = tabC[P % 2], tabS[P % 2], tabWC[P % 2], tabWS[P % 2]
        if P + 1 < 32:
            gen_tables_front(P + 1)
        rho_b = parT[:, 1, P:P + 1].to_broadcast([128, BLK])
        NPB = NBLK - 2
        for b in range(NBLK):
            s = it % 2
            it += 1
            pre, pim = ps[4 + 2 * s], ps[5 + 2 * s]
            rhs = uT[:, ct, b * BLK:(b + 1) * BLK]
            k.mm(pre[:], s5w["Bre_pad"][:, P, :], rhs, True, True, [s5w["Bre_pad"], uT], [pre])
            k.mm(pim[:], s5w["Bim_pad"][:, P, :], rhs, True, True, [s5w["Bim_pad"], uT], [pim])
            if b < NPB:
                for q_, (tw, pq) in enumerate(((tWC, pre), (tWS, pim), (tWC, pim), (tWS, pre))):
                    k.op("dve", lambda e, tw=tw, pq=pq, q_=q_: e.scalar_tensor_tensor(
                        out=tmp[0][:], in0=tw[:], scalar=1.0, in1=pq[:], op0=ALU.mult, op1=ALU.mult,
                        accum_out=pacc[:, b, q_:q_ + 1]), reads=[tw, pq], writes=[tmp[0], pacc])
                if b == NPB - 1:
                    k.tt(psr[:, 0, :], pacc[:, :, 0], pacc[:, :, 1], ALU.add, [pacc], [psr])
                    k.tt(psr[:, 1, :], pacc[:, :, 2], pacc[:, :, 3], ALU.subtract, [pacc], [psr])
                    mre = parT.ap(6 * 32 + P, [[32, 6]])
                    mim = parT.ap(12 * 32 + P, [[32, 6]])
                    for q_, (sv, mw) in enumerate(((0, mre), (1, mim), (1, mre), (0, mim))):
                        k.op("dve", lambda e, sv=sv, mw=mw, q_=q_: e.scalar_tensor_tensor(
                            out=j6[:], in0=psr[:, sv, :], scalar=1.0, in1=mw, op0=ALU.mult, op1=ALU.mult,
                            accum_out=pcc[:, q_:q_ + 1]), reads=[psr, parT], writes=[j6, pcc])
                    k.tt(pcc[:, 4:5], pcc[:, 0:1], pcc[:, 1:2], ALU.subtract, [pcc], [pcc])
                    k.tt(pcc[:, 5:6], pcc[:, 2:3], pcc[:, 3:4], ALU.add, [pcc], [pcc])
                    nin = init[(b + 1) % 2]
                    c5, s5c, ns5 = parT[:, 2, P:P + 1], parT[:, 3, P:P + 1], parT[:, 4, P:P + 1]
                    k.tt(tini[:, 0:1], pcc[:, 4:5], c5, ALU.mult, [pcc, parT], [tini])
                    k.tt(tini[:, 1:2], pcc[:, 5:6], c5, ALU.mult, [pcc, parT], [tini])
                    k.stt(nin[:, 0:1], pcc[:, 5:6], ns5, tini[:, 0:1], ALU.mult, ALU.add, [pcc, parT, tini], [nin])
                    k.stt(nin[:, 1:2], pcc[:, 4:5], s5c, tini[:, 1:2], ALU.mult, ALU.add, [pcc, parT, tini], [nin])
                continue
            k.tt(tmp[0][:], tC[:], pre[:], ALU.mult, [tC, pre], [tmp[0]])
            k.tt(tmp[1][:], tS[:], pim[:], ALU.mult, [tS, pim], [tmp[1]])
            k.tt(bh[s][0][:], tmp[0][:], tmp[1][:], ALU.add, [tmp[0], tmp[1]], [bh[s][0]])
            k.tt(tmp[2][:], tC[:], pim[:], ALU.mult, [tC, pim], [tmp[2]])
            k.tt(tmp[3][:], tS[:], pre[:], ALU.mult, [tS, pre], [tmp[3]])
            k.tt(bh[s][1][:], tmp[2][:], tmp[3][:], ALU.subtract, [tmp[2], tmp[3]], [bh[s][1]])
            ini = init[b % 2]
            for c in range(2):
                iv = ini[:, c:c + 1]
                rd = [parT, bh[s][c], ini]
                k.op("dve", lambda e, c=c, iv=iv: e.tensor_tensor_scan(
                    out=zz[s][c][:], data0=rho_b, data1=bh[s][c][:], initial=iv, op0=ALU.mult, op1=ALU.add),
                    reads=rd, writes=[zz[s][c]])
            if b < NBLK - 1:
                nin = init[(b + 1) % 2]
                e_re, e_im = zz[s][0][:, BLK - 1:BLK], zz[s][1][:, BLK - 1:BLK]
                c5, s5c, ns5 = parT[:, 2, P:P + 1], parT[:, 3, P:P + 1], parT[:, 4, P:P + 1]
                k.tt(tini[:, 0:1], e_re, c5, ALU.mult, [zz[s][0], parT], [tini])
                k.tt(tini[:, 1:2], e_im, c5, ALU.mult, [zz[s][1], parT], [tini])
                k.stt(nin[:, 0:1], e_im, ns5, tini[:, 0:1], ALU.mult, ALU.add, [zz[s][1], parT, tini], [nin])
                k.stt(nin[:, 1:2], e_re, s5c, tini[:, 1:2], ALU.mult, ALU.add, [zz[s][0], parT, tini], [nin])
            if b >= NBLK - 2:
                ob = b - (NBLK - 2)
                yb = ps[2 + ob]
                k.tt(pp[s][0][:], tC[:], zz[s][0][:], ALU.mult, [tC, zz[s][0]], [pp[s][0]])
                k.tt(pp[s][1][:], tS[:], zz[s][1][:], ALU.mult, [tS, zz[s][1]], [pp[s][1]], q="pool")
                k.tt(pp[s][2][:], tS[:], zz[s][0][:], ALU.mult, [tS, zz[s][0]], [pp[s][2]])
                k.tt(pp[s][3][:], tC[:], zz[s][1][:], ALU.mult, [tC, zz[s][1]], [pp[s][3]], q="pool")
                wl = ["WA", "NWA", "NWB", "NWB"]
                for j in range(4):
                    k.mm(yb[:], s5w[wl[j]][:, P, :], pp[s][j][:], (P % 4 == 0 and j == 0), (P % 4 == 3 and j == 3),
                         [s5w[wl[j]], pp[s][j]], [yb])
                if P % 4 == 3:
                    k.stt(ytmp[:], uT[:, ct, b * BLK:(b + 1) * BLK], vT[:, V_SD + ct:V_SD + ct + 1], yb[:],
                          ALU.mult, ALU.add, [uT, vT, yb], [ytmp])
                    k.act(zb[:, ct, ob * BLK:(ob + 1) * BLK], ytmp[:], AF.Gelu_apprx_tanh, [ytmp], [zb])

        if P + 1 < 32:
            gen_tables_back(P + 1)

    if stage == "s5":
        obuf = k.sbuf("obuf", [128, D], F32)
        for tt_ in range(8):
            k.memset(obuf[:], 0.0, [obuf])
            for ct in range(8):
                pvb = ps[0][:].bitcast(BF16)
                k.tr(pvb[:, 0:128], zb[:, ct, tt_ * 128:(tt_ + 1) * 128], ident_b[:], [zb, ident_b], [ps[0]])
                k.copy(obuf[:, ct * 128:(ct + 1) * 128], pvb[:, 0:128], [ps[0]], [obuf])
            k.dma("sp", dbg_d[tt_ * 128:(tt_ + 1) * 128, :], obuf[:], d_out, reads=[obuf])
            k.dma("sp", y_d[tt_ * 128:(tt_ + 1) * 128, :], obuf[:], d_out, reads=[obuf])
        k.finish("sp")
        print("instructions", k.n_inst, "waits", k.n_wait)
        return nc

    k.release(mS)
    k.release(0)

    def chan_rstd(dst, src_sq_fn, nct, blk, bank, extra_mean=None):
        pb = ps[bank]
        for ct in range(nct):
            sqb = src_sq_fn(ct)
            k.mm(pb[:], ones_r[:], sqb[:], ct == 0, ct == nct - 1, [ones_r, sqb], [pb])

    glu_w = k.sbuf("glu_w", [128, 8, 1024], BF16, side="right")
    d_w1 = k.dsem("w1")
    for kt in range(8):
        k.dma("pool", glu_w[:, kt, :], glu_d[kt * 128:(kt + 1) * 128, :], d_w1, writes=[glu_w])
    k.group_end(d_w1, [glu_w])
    w_ag = k.sbuf("w_ag", [128, KT, 2048], BF16)
    d_w2 = k.dsem("w2")
    for kt in range(KT):
        k.dma("pool", w_ag[:, kt, :], w_in_d[kt * 128:(kt + 1) * 128, 0:2048], d_w2, writes=[w_ag])
    k.group_end(d_w2, [w_ag])
    mG = k.mark()
    ysf = k.sbuf("ysf", [128, 8, TOWN], F32)
    sg = [k.sbuf(f"sg{i}", [128, BLK], F32) for i in range(2)]
    sq = [k.sbuf(f"sq{i}", [128, BLK], F32R) for i in range(2)]
    rsd = k.sbuf("rsd", [128, BLK], F32)
    tmpn = k.sbuf("tmpn", [128, BLK], F32)
    for blk in range(2):
        tsl = slice(blk * BLK, (blk + 1) * BLK)
        for ct in range(8):
            pb = ps[ct % 2]
            for kt in range(8):
                k.mm(pb[:], glu_w[:, kt, ct * 128:(ct + 1) * 128], zb[:, kt, tsl], kt == 0, kt == 7, [glu_w, zb], [pb])
            k.act(sg[ct % 2][:], pb[:], AF.Sigmoid, [pb], [sg[ct % 2]])
            k.tt(ysf[:, ct, tsl], zb[:, ct, tsl], sg[ct % 2][:], ALU.mult, [zb, sg[ct % 2]], [ysf])
        pb = ps[2]
        for ct in range(8):
            k.act(sq[ct % 2][:], ysf[:, ct, tsl], AF.Square, [ysf], [sq[ct % 2]])
            k.mm(pb[:], ones_r[:], sq[ct % 2][:], ct == 0, ct == 7, [ones_r, sq[ct % 2]], [pb])
        k.act(rsd[:], pb[:], AF.Sqrt, [pb], [rsd], scale=1.0 / 1024, bias=EPS)
        k.op("dve", lambda e: e.reciprocal(rsd[:], rsd[:]), reads=[rsd], writes=[rsd])
        for ct in range(8):
            k.stt(yn[:, 8 + ct, tsl], ysf[:, ct, tsl], vT[:, V_GS + ct:V_GS + ct + 1], rsd[:], ALU.mult, ALU.mult,
                  [ysf, vT, rsd], [yn])
    k.release(mG)
    k.release(mZ, "right")

    mCr = k.mark("right")
    HAL = 30
    cbuf = k.sbuf("cbuf", [128, 8, HAL + TOWN + 2], BF16, side="right")
    cw = k.sbuf("cw", [128, 8, 31], F32, side="right")
    cwraw = k.sbuf("cwraw", [31, 1024], F32)
    k.dma("sp", cwraw[:], convw_d, d_setup, writes=[cwraw])
    k.group_end(d_setup, [cwraw])
    for ct in range(8):
        k.tr(ps[0][:, ct * 32:ct * 32 + 31], cwraw[:, ct * 128:(ct + 1) * 128], ident_f[0:31, 0:31], [cwraw, ident_f], [ps[0]])
    k.copy(cw[:], ps[0][:, 0:256].rearrange("p (c k) -> p c k", c=8)[:, :, 0:31], [ps[0]], [cw])
    alloc_norm_bufs()
    xtile = NB["xtile"]
    fold_g(w_ag, G_MIX, 2048)
    xnT = k.sbuf("xnT2", [128, KT, BLK], BF16)
    sgc = [k.sbuf(f"sgc{i}", [128, BLK], F32) for i in range(2)]
    def cv_post(t):
        if t % 4 != 3:
            return
        b = 5 + t // 4
        for ct in range(8):
            pa, pg = ps[2 + 2 * (ct % 2)], ps[3 + 2 * (ct % 2)]
            for kt in range(KT):
                k.mm(pg[:], w_ag[:, kt, 1024 + ct * 128:1024 + (ct + 1) * 128], xnT[:, kt, :], kt == 0, kt == KT - 1,
                     [w_ag, xnT], [pg])
            for kt in range(KT):
                k.mm(pa[:], w_ag[:, kt, ct * 128:(ct + 1) * 128], xnT[:, kt, :], kt == 0, kt == KT - 1,
                     [w_ag, xnT], [pa])
            k.act(sgc[ct % 2][:], pg[:], AF.Sigmoid, [pg], [sgc[ct % 2]])
            if b == 5:
                k.tt(cbuf[:, ct, 0:HAL], pa[:, BLK - HAL:BLK], sgc[ct % 2][:, BLK - HAL:BLK], ALU.mult,
                     [pa, sgc[ct % 2]], [cbuf])
            else:
                o = HAL + (b - 6) * BLK
                k.tt(cbuf[:, ct, o:o + BLK], pa[:], sgc[ct % 2][:], ALU.mult, [pa, sgc[ct % 2]], [cbuf])

    items = []
    for b in range(5, 8):
        for tl in range(4):
            r0 = b * BLK + tl * 128
            items.append((xs_d[r0:r0 + 128, :], xnT,
                          lambda half, tl=tl: xnT.ap(half * 8 * BLK + tl * 128, [[BLK, 8], [1, 128]])))
    norm_pipeline(items, cv_post)
    k.release(0)
    cv = k.sbuf("cv", [128, 8, TOWN], F32)
    dg = k.sbuf("dg", [128, 8, 31, 128], BF16, side="right")
    for ct in range(8):
        for kk in range(31):
            k.ts(dg[:, ct, kk, :], ident_f[:], cw[:, ct, kk:kk + 1], None, ALU.mult, None, [ident_f, cw], [dg],
                 q="dve")
    for ct in range(8):
        for blk in range(2):
            pb = ps[2 + (ct * 2 + blk) % 4]
            for kk in range(31):
                k.mm(pb[:], dg[:, ct, kk, :], cbuf[:, ct, blk * BLK + kk:blk * BLK + kk + BLK], kk == 0, kk == 30,
                     [dg, cbuf], [pb])
            k.act(cv[:, ct, blk * BLK:(blk + 1) * BLK], pb[:], AF.Identity, [pb, vT], [cv],
                  bias=vT[:, V_CB + ct:V_CB + ct + 1])
    k.release(mCr, "right")
    wbuf = k.sbuf("wbuf", [128, KT, D], BF16, side="right")
    d_w3 = k.dsem("w3")
    for kt in range(KT):
        k.dma("pool", wbuf[:, kt, :], w_out_d[kt * 128:(kt + 1) * 128, :], d_w3, writes=[wbuf])
    k.group_end(d_w3, [wbuf])
    sqc = [k.sbuf(f"sqc{i}", [128, BLK], F32R) for i in range(2)]
    cvr = [k.sbuf(f"cvr{i}", [128, BLK], F32R) for i in range(2)]
    mean = k.sbuf("mean", [128, BLK], F32)
    var = k.sbuf("var", [128, BLK], F32)
    m2 = k.sbuf("m2", [128, BLK], F32)
    rsd2 = k.sbuf("rsd2", [128, BLK], F32)
    for blk in range(2):
        tsl = slice(blk * BLK, (blk + 1) * BLK)
        p1, p2 = ps[0], ps[1]
        for ct in range(8):
            k.copy(cvr[ct % 2][:], cv[:, ct, tsl], [cv], [cvr[ct % 2]], q="pool")
            k.act(sqc[ct % 2][:], cv[:, ct, tsl], AF.Square, [cv], [sqc[ct % 2]])
            k.mm(p1[:], ones_r[:], cvr[ct % 2][:], ct == 0, ct == 7, [ones_r, cvr[ct % 2]], [p1])
            k.mm(p2[:], ones_r[:], sqc[ct % 2][:], ct == 0, ct == 7, [ones_r, sqc[ct % 2]], [p2])
        k.act(mean[:], p1[:], AF.Copy, [p1], [mean], scale=1.0 / 1024)
        k.tt(m2[:], mean[:], mean[:], ALU.mult, [mean], [m2])
        k.stt(var[:], p2[:], 1.0 / 1024, m2[:], ALU.mult, ALU.subtract, [p2, m2], [var])
        k.act(var[:], var[:], AF.Sqrt, [var], [var], bias=EPS)
        k.op("dve", lambda e: e.reciprocal(var[:], var[:]), reads=[var], writes=[var])
        for ct in range(8):
            k.tt(cv[:, ct, tsl], cv[:, ct, tsl], mean[:], ALU.subtract, [cv, mean], [cv])
            k.tt(cv[:, ct, tsl], cv[:, ct, tsl], var[:], ALU.mult, [cv, var], [cv])
            k.act(cv[:, ct, tsl], cv[:, ct, tsl], AF.Silu, [cv, vT], [cv],
                  scale=vT[:, V_LNG + ct:V_LNG + ct + 1], bias=vT[:, V_LNB + ct:V_LNB + ct + 1])
        p3 = ps[2]
        for ct in range(8):
            k.act(sqc[ct % 2][:], cv[:, ct, tsl], AF.Square, [cv], [sqc[ct % 2]])
            k.mm(p3[:], ones_r[:], sqc[ct % 2][:], ct == 0, ct == 7, [ones_r, sqc[ct % 2]], [p3])
        k.act(rsd2[:], p3[:], AF.Sqrt, [p3], [rsd2], scale=1.0 / 1024, bias=EPS)
        k.op("dve", lambda e: e.reciprocal(rsd2[:], rsd2[:]), reads=[rsd2], writes=[rsd2])
        for ct in range(8):
            k.stt(yn[:, ct, tsl], cv[:, ct, tsl], vT[:, V_GC + ct:V_GC + ct + 1], rsd2[:], ALU.mult, ALU.mult,
                  [cv, vT, rsd2], [yn])
    k.release(0)

    h = k.sbuf("h", [128, 8, D], F32)
    mW = k.mark()
    xres = [k.sbuf(f"xres{i}", [128, D], F32) for i in range(2)]
    d_xr = [k.dsem(f"xr{i}") for i in range(2)]
    for t in range(8):
        xr = xres[t % 2]
        k.dma("sp", xr[:], xs_d[TALL - TOWN + t * 128:TALL - TOWN + (t + 1) * 128, :], d_xr[t % 2], writes=[xr])
        for n in range(4):
            pb = ps[n % 2]
            for kt in range(KT):
                k.mm(pb[:], yn[:, kt, t * 128:(t + 1) * 128], wbuf[:, kt, n * 512:(n + 1) * 512], kt == 0, kt == KT - 1,
                     [yn, wbuf], [pb])
            k.tt(h[:, t, n * 512:(n + 1) * 512], pb[:], xr[:, n * 512:(n + 1) * 512], ALU.add, [pb, xr], [h])
    k.release(mW)
    k.release(mark_yn, "right")

    def dump_h(dst_list):
        for t in range(8):
            for dd in dst_list:
                k.dma("sp", dd[t * 128:(t + 1) * 128, :], h[:, t, :], d_out, reads=[h])

    if stage == "h1":
        dump_h([dbg_d, y_d])
        k.finish("sp")
        print("instructions", k.n_inst, "waits", k.n_wait)
        return nc

    mE = k.mark()
    KTb = k.sbuf("KTb", [128, 16, 256], BF16)
    Vb = k.sbuf("Vb", [128, 2, D], BF16)
    wch = [k.sbuf(f"wch{i}", [128, KT, 512], BF16) for i in range(2)]
    d_wc = [k.dsem(f"wc{i}") for i in range(2)]
    wcnt = {"n": 0}

    def load_wchunk(w_d, c0, g_off=None):
        i = wcnt["n"] % 2
        wcnt["n"] += 1
        for q4 in range(4):
            src = w_d[q4 * 512:(q4 + 1) * 512, c0:c0 + 512].rearrange("(kt p) c -> p kt c", p=128)
            k.dma("pool", wch[i][:, q4 * 4:(q4 + 1) * 4, :], src, d_wc[i], writes=[wch[i]])
        k.group_end(d_wc[i], [wch[i]])
        if g_off is not None:
            fold_g(wch[i], g_off, 512)
        return wch[i]

    mE1 = k.mark()
    alloc_norm_bufs()
    xtile = NB["xtile"]
    memT = k.sbuf("memT", [128, KT, 256], BF16)
    def mem_src(mt):
        def f():
            i = load_tok_tile(mem_d[mt * 128:(mt + 1) * 128, :])
            return xtile[i], xtile[i][:]
        return f
    norm_pipeline([(mem_d[mt * 128:(mt + 1) * 128, :], memT,
                    lambda half, mt=mt: memT.ap(half * 8 * 256 + mt * 128, [[256, 8], [1, 128]])) for mt in range(2)])
    for c in range(8):
        wc = load_wchunk(w_kv_d, c * 512, G_MEMN)
        if c < 4:
            for j in range(4):
                pb = ps[2 + j % 2]
                for kt in range(KT):
                    k.mm(pb[:, 0:256], wc[:, kt, j * 128:(j + 1) * 128], memT[:, kt, :], kt == 0, kt == KT - 1,
                         [wc, memT], [pb])
                k.act(KTb[:, c * 4 + j, :], pb[:, 0:256], AF.Copy, [pb], [KTb])
        else:
            for mt in range(2):
                pb = ps[2 + mt]
                for kt in range(KT):
                    k.mm(pb[:], memT[:, kt, mt * 128:(mt + 1) * 128], wc[:, kt, :], kt == 0, kt == KT - 1,
                         [wc, memT], [pb])
                k.act(Vb[:, mt, (c - 4) * 512:(c - 3) * 512], pb[:], AF.Copy, [pb], [Vb])
    k.release(mE1)
    qT = k.sbuf("qT", [128, 16, TOWN], BF16)
    hnT = k.sbuf("hnT", [128, 16, TOWN], BF16)
    mE2 = k.mark()
    alloc_norm_bufs(False)
    norm_pipeline([((h, h[:, t, :]), hnT,
                    lambda half, t=t: hnT.ap(half * 8 * TOWN + t * 128, [[TOWN, 8], [1, 128]])) for t in range(8)])
    k.release(mE2)
    qscale = 1.0 / math.sqrt(512.0)
    for c in range(4):
        wc = load_wchunk(w_q_d, c * 512, G_MEM)
        for j in range(4):
            for blk in range(2):
                pb = ps[2 + (2 * j + blk) % 2]
                for kt in range(KT):
                    k.mm(pb[:], wc[:, kt, j * 128:(j + 1) * 128], hnT[:, kt, blk * BLK:(blk + 1) * BLK], kt == 0,
                         kt == KT - 1, [wc, hnT], [pb])
                k.act(qT[:, c * 4 + j, blk * BLK:(blk + 1) * BLK], pb[:], AF.Copy, [pb], [qT], scale=qscale)
    PT = k.sbuf("PT", [128, 8, TOWN], BF16)
    pex = [k.sbuf(f"pex{i}", [128, 256], F32) for i in range(2)]
    pnb = [k.sbuf(f"pnb{i}", [128, 256], BF16) for i in range(2)]
    sst = [k.sbuf(f"sst{i}", [128, 4], F32) for i in range(2)]
    n_it = 0
    for hd in range(4):
        for t in range(8):
            i = n_it % 2
            n_it += 1
            pb = ps[2 + i]
            for j in range(4):
                k.mm(pb[:, 0:256], qT[:, hd * 4 + j, t * 128:(t + 1) * 128], KTb[:, hd * 4 + j, :], j == 0, j == 3,
                     [qT, KTb], [pb])
            st = sst[i]
            k.op("dve", lambda e, st=st, pb=pb: e.tensor_reduce(out=st[:, 0:1], in_=pb[:, 0:256], axis=AX.X, op=ALU.max),
                 reads=[pb], writes=[st])
            k.ts(st[:, 1:2], st[:, 0:1], -1.0, None, ALU.mult, None, [st], [st])
            k.act(pex[i][:], pb[:, 0:256], AF.Exp, [pb, st], [pex[i], st], bias=st[:, 1:2], accum_out=st[:, 2:3])
            k.op("dve", lambda e, st=st: e.reciprocal(st[:, 3:4], st[:, 2:3]), reads=[st], writes=[st])
            k.ts(pnb[i][:], pex[i][:], st[:, 3:4], None, ALU.mult, None, [pex[i], st], [pnb[i]])
            pt = ps[4 + i]
            ptv = pt[:].bitcast(BF16)
            for mt in range(2):
                k.tr(ptv[:, mt * 128:(mt + 1) * 128], pnb[i][:, mt * 128:(mt + 1) * 128], ident_b[:], [pnb[i], ident_b], [pt])
            k.copy(PT.ap(hd * 2 * TOWN + t * 128, [[TOWN, 2], [1, 128]]),
                   ptv[:, 0:256].rearrange("p (m t) -> p m t", m=2), [pt], [PT])
    OT = hnT
    for hd in range(4):
        for dvt in range(4):
            for blk in range(2):
                pb = ps[2 + (dvt * 2 + blk) % 2]
                for mt in range(2):
                    k.mm(pb[:], Vb[:, mt, (hd * 4 + dvt) * 128:(hd * 4 + dvt + 1) * 128],
                         PT[:, hd * 2 + mt, blk * BLK:(blk + 1) * BLK], mt == 0, mt == 1, [Vb, PT], [pb])
                k.act(OT[:, hd * 4 + dvt, blk * BLK:(blk + 1) * BLK], pb[:], AF.Copy, [pb], [OT])
    for n in range(4):
        wc = load_wchunk(w_o_d, n * 512)
        for t in range(8):
            pb = ps[2 + t % 2]
            for ct in range(KT):
                k.mm(pb[:], OT[:, ct, t * 128:(t + 1) * 128], wc[:, ct, :], ct == 0, ct == KT - 1, [OT, wc], [pb])
            k.tt(h[:, t, n * 512:(n + 1) * 512], h[:, t, n * 512:(n + 1) * 512], pb[:], ALU.add, [h, pb], [h])
    k.release(mE)

    if stage == "h2":
        dump_h([dbg_d, y_d])
        k.finish("sp")
        print("instructions", k.n_inst, "waits", k.n_wait)
        return nc

    idxT = k.sbuf("idxT", [128, TOWN], U32)
    gT = k.sbuf("gT", [128, TOWN], F32)
    hn3b = k.sbuf("hn3b", [128, 8, D], BF16)
    rstd3 = k.sbuf("rstd3", [128, 16], F32)
    mF0 = k.mark()
    gffn_b = k.sbuf("gffn_b", [128, D], F32)
    k.dma("sp", gffn_b[:], bass.AP(tensor=vecs_d.tensor, offset=G_FFN * 128, ap=[[0, 128], [1, D]]), d_setup,
          writes=[gffn_b])
    k.group_end(d_setup, [gffn_b])
    xs0 = k.sbuf("xs0", [128, D], F32)
    for t in range(8):
        k.act(xs0[:], h[:, t, :], AF.Square, [h], [xs0, rstd3], accum_out=rstd3[:, 8 + t:9 + t])
        pool_rsqrt(rstd3, rstd3[:, t:t + 1], rstd3[:, 8 + t:9 + t], 1.0 / D, [128, 1])
        k.stt(hn3b[:, t, :], h[:, t, :], rstd3[:, t:t + 1], gffn_b[:], ALU.mult, ALU.mult, [h, rstd3, gffn_b], [hn3b])
    k.release(mF0)
    mF = k.mark()
    import os
    RL = float(os.environ.get("RL", "99")) if stage == "route" else 99

    def route_body():
        idxf = k.sbuf("idxf", [128, 8, 128], F32)
        gtf = k.sbuf("gtf", [128, 8, 128], F32)
        keysT = k.sbuf("keysT", [128, 8, 128], F32)
        kraw = k.sbuf("kraw", [128, 128], F32)
        d_kr = k.dsem("kr")
        for hq in range(8):
            k.dma("sp", kraw[:].rearrange("n (s c) -> n s c", s=2), pkeys_d[hq].rearrange("s n c -> n s c"), d_kr,
                  writes=[kraw])
            k.tr(ps[0][:, 0:128], kraw[:], ident_f[:], [kraw, ident_f], [ps[0]])
            k.copy(keysT[:, hq, :], ps[0][:, 0:128], [ps[0]], [keysT])
        if RL <= 1:
            return
        iota16 = k.sbuf("iota16", [128, 16], F32)
        k.op("pool", lambda e: e.iota(iota16[:], pattern=[[1, 16]], base=0, channel_multiplier=0,
                                      allow_small_or_imprecise_dtypes=True), writes=[iota16])
        pwq = k.sbuf("pwq", [128, KT, 512], F32R)
        wsts = [k.sbuf(f"wst{i}", [128, 512], F32) for i in range(2)]
        d_wss = [k.dsem(f"ws{i}") for i in range(2)]
        xs32 = k.sbuf("xs32", [128, D], F32)
        hnT3s = [k.sbuf(f"hnT3{i}", [128, KT, 128], F32R) for i in range(2)]
        qTts = [k.sbuf(f"qTt{i}", [128, 4, 128], F32) for i in range(2)]
        scss = [k.sbuf(f"scs{i}", [128, 8, 128], F32) for i in range(2)]
        NCH = 4
        SC = []
        for c_ in range(NCH):
            SC.append(dict(
                scw=k.sbuf(f"scw{c_}", [128, 256], F32), cand=k.sbuf(f"cand{c_}", [128, 256], F32),
                s1v=k.sbuf(f"s1v{c_}", [128, 2, 16], F32), s1i=k.sbuf(f"s1i{c_}", [128, 2, 16], F32),
                iu=k.sbuf(f"iu{c_}", [128, 16], U32), tv=k.sbuf(f"tv{c_}", [128, 16], F32),
                posf=k.sbuf(f"posf{c_}", [128, 16], F32), pai=k.sbuf(f"pai{c_}", [128, 16], I32),
                paf=k.sbuf(f"paf{c_}", [128, 16], F32), pbf=k.sbuf(f"pbf{c_}", [128, 16], F32),
                eq=k.sbuf(f"eq{c_}", [128, 16, 16], F32), sel=k.sbuf(f"sel{c_}", [128, 2, 16], F32),
                rst=k.sbuf(f"rst{c_}", [128, 8], F32), ex=k.sbuf(f"ex{c_}", [128, 16], F32)))

        def top16(B, vals_buf, vals_ap, src_buf, src_ap, idx_buf, idx_ap, n):
            scw, iu = B["scw"], B["iu"]
            k.op("dve", lambda e: e.max(out=vals_ap[:, 0:8], in_=src_ap), reads=[src_buf], writes=[vals_buf])
            yield
            k.op("dve", lambda e: e.match_replace(out=scw[:, 0:n], in_to_replace=vals_ap[:, 0:8], in_values=src_ap,
                                                  imm_value=-1e30), reads=[src_buf, vals_buf], writes=[scw])
            yield
            k.op("dve", lambda e: e.max(out=vals_ap[:, 8:16], in_=scw[:, 0:n]), reads=[scw], writes=[vals_buf])
            yield
            k.op("dve", lambda e: e.max_index(out=iu[:, 0:8], in_max=vals_ap[:, 0:8], in_values=src_ap),
                 reads=[src_buf, vals_buf], writes=[iu])
            yield
            k.op("dve", lambda e: e.max_index(out=iu[:, 8:16], in_max=vals_ap[:, 8:16], in_values=src_ap),
                 reads=[src_buf, vals_buf], writes=[iu])
            yield
            k.copy(idx_ap, iu[:], [iu], [idx_buf])
            yield

        def head_chain(B, scs, hh, hq, t):
            s1v, s1i, cand, tv, posf = B["s1v"], B["s1i"], B["cand"], B["tv"], B["posf"]
            rst, ex, paf, pai, pbf, eq, sel = B["rst"], B["ex"], B["paf"], B["pai"], B["pbf"], B["eq"], B["sel"]
            for sd in range(2):
                yield from top16(B, s1v, s1v[:, sd, :], scs, scs[:, hh * 2 + sd, :], s1i, s1i[:, sd, :], 128)
            k.tt(cand[:].rearrange("p (a b) -> p a b", a=16), s1v.ap(0, [[1, 16], [0, 16]]),
                 s1v.ap(16, [[0, 16], [1, 16]]), ALU.add, [s1v], [cand])
            yield
            yield from top16(B, tv, tv[:], cand, cand[:], posf, posf[:], 256)
            k.ts(rst[:, 2:3], tv[:, 0:1], -1.0, None, ALU.mult, None, [tv], [rst])
            yield
            k.act(ex[:], tv[:], AF.Exp, [tv, rst], [ex, rst], bias=rst[:, 2:3], accum_out=rst[:, 3:4])
            yield
            k.op("dve", lambda e: e.reciprocal(rst[:, 4:5], rst[:, 3:4]), reads=[rst], writes=[rst])
            yield
            k.ts(gtf[:, t, hq * 16:(hq + 1) * 16], ex[:], rst[:, 4:5], None, ALU.mult, None, [ex, rst], [gtfB[hq]])
            yield
            k.ts(paf[:], posf[:], 0.0625, -0.46875, ALU.mult, ALU.add, [posf], [paf])
            yield
            k.copy(pai[:], paf[:], [paf], [pai])
            yield
            k.copy(paf[:], pai[:], [pai], [paf])
            yield
            k.stt(pbf[:], paf[:], -16.0, posf[:], ALU.mult, ALU.add, [paf, posf], [pbf])
            yield
            for sd, pos_b in ((0, paf), (1, pbf)):
                k.tt(eq[:], pos_b.ap(0, [[1, 16], [0, 16]]), iota16.ap(0, [[0, 16], [1, 16]]), ALU.is_equal,
                     [pos_b, iota16], [eq])
                yield
                k.tt(eq[:], eq[:], s1i.ap(sd * 16, [[0, 16], [1, 16]]), ALU.mult, [eq, s1i], [eq])
                yield
                k.op("dve", lambda e, sd=sd: e.tensor_reduce(out=sel[:, sd, :], in_=eq[:], axis=AX.X, op=ALU.add),
                     reads=[eq], writes=[sel])
                yield
            k.stt(idxf[:, t, hq * 16:(hq + 1) * 16], sel[:, 0, :], 128.0, sel[:, 1, :], ALU.mult, ALU.add,
                  [sel], [idxfB[hq]])
            yield

        idxfB = [Buf(f"idxf_h{i}") for i in range(8)]
        gtfB = [Buf(f"gtf_h{i}") for i in range(8)]

        for half in range(2):
            for kt in range(KT):
                wst, d_ws = wsts[kt % 2], d_wss[kt % 2]
                k.dma("sp", wst[:], pwq_d[kt * 128:(kt + 1) * 128, half * 512:(half + 1) * 512], d_ws, writes=[wst])
                k.act(pwq[:, kt, :], wst[:], AF.Copy, [wst, vT], [pwq], scale=vT[:, G_FFN + kt:G_FFN + kt + 1])
            if RL <= 1.4:
                return
            def front(t):
                hnT3, qTt, scs = hnT3s[t % 2], qTts[t % 2], scss[t % 2]
                k.act(xs32[:], h[:, t, :], AF.Copy, [h, rstd3], [xs32], scale=rstd3[:, t:t + 1])
                for q4 in range(4):
                    pb = ps[q4]
                    for j in range(4):
                        kt = q4 * 4 + j
                        k.tr(pb[:, j * 128:(j + 1) * 128], xs32[:, kt * 128:(kt + 1) * 128], ident_f[:], [xs32, ident_f], [pb])
                    k.act(hnT3[:, q4 * 4:(q4 + 1) * 4, :], pb[:].rearrange("p (j t) -> p j t", j=4), AF.Copy, [pb], [hnT3])
                for hh in range(4):
                    pb = ps[4 + hh % 2]
                    for kt in range(KT):
                        k.mm(pb[:, 0:128], pwq[:, kt, hh * 128:(hh + 1) * 128], hnT3[:, kt, :], kt == 0, kt == KT - 1,
                             [pwq, hnT3], [pb])
                    k.act(qTt[:, hh, :], pb[:, 0:128], AF.Copy, [pb], [qTt])
                for hh in range(4):
                    hq = half * 4 + hh
                    for sd in range(2):
                        pb = ps[6 + sd]
                        k.mm(pb[:, hh * 128:(hh + 1) * 128], qTt[sd * 64:(sd + 1) * 64, hh, :],
                             keysT[sd * 64:(sd + 1) * 64, hq, :], True, True, [qTt, keysT], [pb])
                for sd in range(2):
                    k.act(scs.ap(sd * 128, [[256, 4], [1, 128]]), ps[6 + sd][:].rearrange("p (h n) -> p h n", h=4),
                          AF.Copy, [ps[6 + sd]], [scs])

            front(0)
            for t in range(8):
                if t + 1 < 8:
                    front(t + 1)
                scs = scss[t % 2]
                chains = [head_chain(SC[hh], scs, hh, half * 4 + hh, t) for hh in range(4)]
                live = list(chains)
                while live:
                    for g_ in list(live):
                        try:
                            next(g_)
                        except StopIteration:
                            live.remove(g_)
        if RL <= 6:
            return
        for t in range(8):
            k.tr(ps[0][:, 0:128], idxf[:, t, :], ident_f[:], [idxf, ident_f] + idxfB, [ps[0]])
            k.copy(idxT[:, t * 128:(t + 1) * 128], ps[0][:, 0:128], [ps[0]], [idxT])
            k.tr(ps[1][:, 0:128], gtf[:, t, :], ident_f[:], [gtf, ident_f] + gtfB, [ps[1]])
            k.copy(gT[:, t * 128:(t + 1) * 128], ps[1][:, 0:128], [ps[1]], [gT])

    route_body()
    k.release(mF)

    if stage == "route":
        obuf = k.sbuf("obuf", [128, D], F32)
        for t in range(8):
            k.memset(obuf[:], 0.0, [obuf])
            k.tr(ps[0][:, 0:128], gT[:, t * 128:(t + 1) * 128], ident_f[:], [gT, ident_f], [ps[0]])
            k.copy(obuf[:, 0:128], ps[0][:, 0:128], [ps[0]], [obuf])
            k.copy(gT[:, t * 128:(t + 1) * 128], idxT[:, t * 128:(t + 1) * 128], [idxT, ps[0]], [gT])
            k.tr(ps[1][:, 0:128], gT[:, t * 128:(t + 1) * 128], ident_f[:], [gT, ident_f], [ps[1]])
            k.copy(obuf[:, 128:256], ps[1][:, 0:128], [ps[1]], [obuf])
            for dd in (dbg_d, y_d):
                k.dma("sp", dd[t * 128:(t + 1) * 128, :], obuf[:], d_out, reads=[obuf])
        k.finish("sp")
        print("instructions", k.n_inst, "waits", k.n_wait)
        return nc

    mF3 = k.mark()
    Ebig = k.sbuf("Ebig", [128, 256], F32)
    k.memset(Ebig[:], 0.0, [Ebig])
    k.memset(Ebig[:, 127:128], 1.0, [Ebig])
    fng_b = k.sbuf("fng_b", [128, D], F32)
    k.dma("sp", fng_b[:], bass.AP(tensor=fng_d.tensor, offset=0, ap=[[0, 128], [1, D]]), d_setup, writes=[fng_b])
    k.group_end(d_setup, [fng_b])
    NG = 9
    UV = [k.sbuf(f"UV{i}", [128, 2 * D], BF16) for i in range(NG)]
    d_u = [k.dsem(f"u{i}") for i in range(NG)]
    lw = [k.sbuf(f"lw{i}", [128, 128], BF16) for i in range(4)]
    accs = [k.sbuf(f"accs{i}", [128, 8], F32) for i in range(2)]
    gact = [k.sbuf(f"gact{i}", [128, 1], F32) for i in range(2)]
    junkd = k.sbuf("junkd", [128, 1024], BF16)
    obuf = k.sbuf("obuf", [128, D], F32)
    fst = k.sbuf("fst", [128, 4], F32)
    d_o = k.dsem("o")
    NTOK = TOWN

    def issue_gather(tg):
        i = tg % NG
        k.dma_custom("pool", lambda e: e.indirect_dma_start(
            out=UV[i][:], out_offset=None, in_=puv_bf,
            in_offset=bass.IndirectOffsetOnAxis(ap=idxT[:, tg:tg + 1], axis=0)), d_u[i], reads=[idxT, tblU], writes=[UV[i]])

    def issue_bcast(tg):
        t, tl = tg // 128, tg % 128
        for n in range(4):
            k.mm(ps[n][:], ident_b[:, tl:tl + 1].to_broadcast([128, 128]), hn3b[:, t, n * 512:(n + 1) * 512], True, True,
                 [ident_b, hn3b], [ps[n]])

    def issue_dots(tg):
        i, ac = tg % NG, accs[tg % 2]
        for n in range(2):
            k.op("dve", lambda e, n=n: e.scalar_tensor_tensor(
                out=junkd[:], in0=UV[i][:, n * 1024:(n + 1) * 1024], scalar=1.0, in1=psd[n][:],
                op0=ALU.mult, op1=ALU.mult, accum_out=ac[:, n:n + 1]), reads=[UV[i], ps[2 * n], ps[2 * n + 1]],
                writes=[junkd, ac])
        k.act(gact[tg % 2][:], ac[:, 0:1], AF.Gelu_apprx_tanh, [ac], [gact[tg % 2]], bias=ac[:, 1:2])

    def issue_combine(tg):
        t, tl = tg // 128, tg % 128
        i, j = tg % NG, tg % 2
        lwj = lw[tg % 4]
        k.ts(lwj[:], Ebig[:, 127 - tl:255 - tl], gact[j][:, 0:1], gT[:, tg:tg + 1], ALU.mult, ALU.mult,
             [Ebig, gact[j], gT], [lwj], q="pool")
        for n in range(4):
            k.mm(ps[4 + n][:], lwj[:], UV[i][:, D + n * 512:D + (n + 1) * 512], tl == 0, tl == 127, [lwj, UV[i]],
                 [ps[4 + n]])
        if tl == 127:
            for n in range(4):
                k.tt(h[:, t, n * 512:(n + 1) * 512], h[:, t, n * 512:(n + 1) * 512], ps[4 + n][:], ALU.add,
                     [h, ps[4 + n]], [h])
            k.act(obuf[:], h[:, t, :], AF.Square, [h], [obuf, fst], accum_out=fst[:, 0:1])
            pool_rsqrt(fst, fst[:, 1:2], fst[:, 0:1], 1.0 / D, [128, 1])
            k.stt(obuf[:], h[:, t, :], fst[:, 1:2], fng_b[:], ALU.mult, ALU.mult, [h, fst, fng_b], [obuf])
            k.dma("sp", y_d[t * 128:(t + 1) * 128, :], obuf[:], d_o, reads=[obuf])
            if dbg_d is not None:
                k.dma("sp", dbg_d[t * 128:(t + 1) * 128, :], obuf[:], d_o, reads=[obuf])

    for tg in range(min(NG - 1, NTOK)):
        issue_gather(tg)
    issue_bcast(0)
    issue_dots(0)
    issue_bcast(1)
    for tg in range(NTOK):
        if tg + NG - 1 < NTOK:
            issue_gather(tg + NG - 1)
        if tg + 1 < NTOK:
            issue_dots(tg + 1)
        if tg + 2 < NTOK:
            issue_bcast(tg + 2)
        issue_combine(tg)
    k.finish("sp")
    print("instructions", k.n_inst, "waits", k.n_wait)
    return nc


def make_in_maps(inp):
    f = lambda a: np.ascontiguousarray(np.asarray(a, dtype=np.float32))
    x = f(inp["x"])
    mem = f(inp["mem"])
    vec_names = ["norm_mix_g", "norm_mem_g", "mem_norm_g", "norm_ffn_g"]
    rows = [f(inp[n]).reshape(16, 128) for n in vec_names]
    for n in ["conv_b", "conv_ln_g", "conv_ln_b", "grp_norm_conv_g", "grp_norm_ssm_g", "ssm_d"]:
        rows.append(f(inp[n]).reshape(8, 128))
    vecs = np.ascontiguousarray(np.concatenate(rows, axis=0))
    shared = {
        "vecs": vecs,
        "conv_w": f(inp["conv_w"]).reshape(31, 1024),
        "lam_re": f(inp["ssm_lam_re"]).reshape(32, 128),
        "lam_im": f(inp["ssm_lam_im"]).reshape(32, 128),
        "log_dt": f(inp["ssm_log_dt"]).reshape(32, 2),
        "b_re": f(inp["ssm_b_re"]).reshape(32, 2048),
        "b_im": f(inp["ssm_b_im"]).reshape(32, 2048),
        "c_re": f(inp["ssm_c_re"]).reshape(32, 2048),
        "c_im": f(inp["ssm_c_im"]).reshape(32, 2048),
        "w_in": f(inp["w_in"]).reshape(D, 3072),
        "glu_w": f(inp["ssm_glu_w"]).reshape(1024, 1024),
        "w_out": f(inp["w_out"]).reshape(D, D),
        "w_q": f(inp["w_q_mem"]).reshape(D, D),
        "w_kv": f(inp["w_kv_mem"]).reshape(D, 2 * D),
        "w_o": f(inp["w_o_mem"]).reshape(D, D),
        "peer_w_q": f(inp["peer_w_q"]).reshape(D, 1024),
        "peer_keys": f(inp["peer_keys"]).reshape(8, 2, 128, 64),
        "peer_u": f(inp["peer_u"]).reshape(16384, D),
        "peer_v": f(inp["peer_v"]).reshape(16384, D),
        "final_g": f(inp["final_norm_g"]).reshape(1, D),
    }
    maps = []
    for c in range(NCORES):
        b, seg = c // 4, c % 4
        xs = np.zeros((TALL, D), np.float32)
        n_have = (seg + 1) * TOWN
        xs[TALL - n_have:] = x[b, :n_have]
        m = dict(shared)
        m["xs"] = xs
        m["mem"] = np.ascontiguousarray(mem[b])
        maps.append(m)
    return maps


def kernel(**inputs):
    nc = build_program("full")
    in_maps = make_in_maps(inputs)
    res = run_bass_kernel_spmd(nc, in_maps, core_ids=list(range(NCORES)))
    out = np.zeros((2, 4096, D), np.float32)
    for c in range(NCORES):
        b, seg = c // 4, c % 4
        out[b, seg * TOWN:(seg + 1) * TOWN] = res.results[c]["y"]
    return out
```

```python
import math
import numpy as np
import concourse.bass as bass
import concourse.mybir as mybir
from concourse.bass_utils import run_bass_kernel_spmd

F32 = mybir.dt.float32
F32R = mybir.dt.float32r
BF16 = mybir.dt.bfloat16
U32 = mybir.dt.uint32
I32 = mybir.dt.int32
ALU = mybir.AluOpType
AF = mybir.ActivationFunctionType
AX = mybir.AxisListType

NCORES = 8
D = 2048
KT = 16
TOWN = 1024
TALL = 4096
BLK = 512
NBLK = TALL // BLK
EPS = 1e-6
TWO_PI_S = 6.28318


class Sem:
    def __init__(self, nc, name, step):
        self.h = nc.semaphore(name).__enter__()
        self.name = name
        self.cnt = 0
        self.step = step


class Buf:
    def __init__(self, name, t=None):
        self.name = name
        self.t = t
        self.last_w = None
        self.readers = {}

    def __getitem__(self, k):
        return self.t[k]

    def ap(self, offset, dims):
        full = self.t[:]
        pstep = full.ap[0][0]
        return bass.AP(tensor=self.t, offset=offset, ap=[[pstep, full.ap[0][1]]] + [list(d) for d in dims])

    def app(self, p0, pn, offset, dims):
        full = self.t[:]
        pstep = full.ap[0][0]
        return bass.AP(tensor=self.t, offset=p0 * pstep + offset, ap=[[pstep, pn]] + [list(d) for d in dims])


class K:
    def __init__(self, nc):
        self.nc = nc
        self.eng = {"pe": nc.tensor, "act": nc.scalar, "dve": nc.vector,
                    "pool": nc.gpsimd, "sp": nc.sync}
        self.qsem = {q: Sem(nc, "q_" + q, 1) for q in self.eng}
        self.seen = {q: {} for q in self.eng}
        self.n_inst = 0
        self.n_wait = 0
        self.dsems = []
        self._stack = []
        self._rstack = []
        self.fence = {}

    def sbuf(self, name, shape, dt=F32, side="left"):
        self._uid = getattr(self, "_uid", 0) + 1
        cm = self.nc.sbuf_tensor(f"sb{self._uid}_{name}", list(shape), dt, side=side)
        t = cm.__enter__()
        b = Buf(name, t)
        b.readers = dict(self.fence)
        (self._stack if side == "left" else self._rstack).append((cm, b))
        return b

    def psum(self, name, shape, dt=F32):
        cm = self.nc.psum_tensor(name, list(shape), dt)
        t = cm.__enter__()
        b = Buf(name, t)
        self._stack.append((cm, b))
        return b

    def mark(self, side="left"):
        return len(self._stack if side == "left" else self._rstack)

    def release(self, mark, side="left"):
        st = self._stack if side == "left" else self._rstack
        while len(st) > mark:
            cm, b = st.pop()
            deps = list(b.readers.items()) + ([b.last_w] if b.last_w else [])
            for sm, tick in deps:
                if self.fence.get(sm, 0) < tick:
                    self.fence[sm] = tick
            cm.__exit__(None, None, None)

    def dsem(self, name):
        s = Sem(self.nc, "d_" + name, 16)
        self.dsems.append(s)
        return s

    def _waits(self, q, reads, writes, skip=None):
        need = {}
        myq = self.qsem[q]

        def add(dep, same_ok):
            if dep is None:
                return
            s, tick = dep
            if s is skip:
                return
            if s is myq and not same_ok:
                return
            if need.get(s, 0) < tick:
                need[s] = tick

        same = q != "pe"
        for b in reads:
            add(b.last_w, same)
        for b in writes:
            add(b.last_w, same)
            for s, tick in b.readers.items():
                add((s, tick), same)
        eng = self.eng[q]
        seen = self.seen[q]
        for s, tick in need.items():
            if seen.get(s, 0) >= tick:
                continue
            eng.wait_ge(s.h, tick)
            self.n_wait += 1
            seen[s] = tick

    def _record(self, sem, reads, writes):
        tick = sem.cnt
        for b in reads:
            if b.readers.get(sem, 0) < tick:
                b.readers[sem] = tick
        for b in writes:
            b.last_w = (sem, tick)
            b.readers = {}

    def op(self, q, fn, reads=(), writes=()):
        self._waits(q, reads, writes)
        ins = fn(self.eng[q])
        s = self.qsem[q]
        s.cnt += 1
        ins.then_inc(s.h, 1)
        self.n_inst += 1
        self._record(s, reads, writes)
        return ins

    def dma(self, q, out, in_, dsem, reads=(), writes=(), **kw):
        self._waits(q, reads, writes, skip=dsem)
        ins = self.eng[q].dma_start(out=out, in_=in_, **kw)
        dsem.cnt += 16
        ins.then_inc(dsem.h, 16)
        self.n_inst += 1
        self._record(dsem, reads, writes)
        return ins

    def dma_custom(self, q, fn, dsem, reads=(), writes=()):
        self._waits(q, reads, writes)
        ins = fn(self.eng[q])
        dsem.cnt += 16
        ins.then_inc(dsem.h, 16)
        self.n_inst += 1
        self._record(dsem, reads, writes)
        return ins

    def group_end(self, dsem, bufs):
        for b in bufs:
            b.last_w = (dsem, dsem.cnt)

    def finish(self, q="sp"):
        eng = self.eng[q]
        for s in list(self.qsem.values()) + self.dsems:
            if s.cnt > 0 and s is not self.qsem[q]:
                eng.wait_ge(s.h, s.cnt)

    def mm(self, out, lhsT, rhs, start, stop, reads, writes):
        return self.op("pe", lambda e: e.matmul(out, lhsT=lhsT, rhs=rhs, start=start, stop=stop),
                       reads=reads, writes=writes)

    def tr(self, out, in_, ident, reads, writes):
        return self.op("pe", lambda e: e.transpose(out, in_, ident), reads=reads, writes=writes)

    def act(self, out, in_, func, reads, writes, **kw):
        return self.op("act", lambda e: e.activation(out=out, in_=in_, func=func, **kw), reads=reads, writes=writes)

    def tt(self, out, in0, in1, op, reads, writes, q="dve"):
        return self.op(q, lambda e: e.tensor_tensor(out=out, in0=in0, in1=in1, op=op), reads=reads, writes=writes)

    def ts(self, out, in0, s1, s2, op0, op1, reads, writes, q="dve"):
        if s2 is None:
            return self.op(q, lambda e: e.tensor_scalar(out, in0, s1, None, op0=op0), reads=reads, writes=writes)
        return self.op(q, lambda e: e.tensor_scalar(out, in0, s1, s2, op0=op0, op1=op1), reads=reads, writes=writes)

    def stt(self, out, in0, scalar, in1, op0, op1, reads, writes):
        return self.op("dve", lambda e: e.scalar_tensor_tensor(out=out, in0=in0, scalar=scalar, in1=in1,
                                                               op0=op0, op1=op1), reads=reads, writes=writes)

    def copy(self, out, in_, reads, writes, q="dve"):
        return self.op(q, lambda e: e.tensor_copy(out, in_), reads=reads, writes=writes)

    def memset(self, ap, val, writes, q="dve"):
        return self.op(q, lambda e: e.memset(ap, val), writes=writes)


def build_program(stage="full"):
    nc = bass.Bass("TRN2", target_bir_lowering=False)

    def din(name, shape, dt=F32):
        return nc.dram_tensor(name, list(shape), dt, kind="ExternalInput").ap()

    xs_d = din("xs", [TALL, D])
    mem_d = din("mem", [256, D])
    vecs_d = din("vecs", [112, 128])
    convw_d = din("conv_w", [31, 1024])
    lre_d = din("lam_re", [32, 128])
    lim_d = din("lam_im", [32, 128])
    ldt_d = din("log_dt", [32, 2])
    bre_d = din("b_re", [32, 2048])
    bim_d = din("b_im", [32, 2048])
    cre_d = din("c_re", [32, 2048])
    cim_d = din("c_im", [32, 2048])
    w_in_d = din("w_in", [D, 3072])
    glu_d = din("glu_w", [1024, 1024])
    w_out_d = din("w_out", [D, D])
    w_q_d = din("w_q", [D, D])
    w_kv_d = din("w_kv", [D, 2 * D])
    w_o_d = din("w_o", [D, D])
    pwq_d = din("peer_w_q", [D, 1024])
    pkeys_d = din("peer_keys", [8, 2, 128, 64])
    pu_d = din("peer_u", [16384, D])
    pv_d = din("peer_v", [16384, D])
    fng_d = din("final_g", [1, D])
    y_d = nc.dram_tensor("y", [TOWN, D], F32, kind="ExternalOutput").ap()
    dbg_d = None
    if stage != "full":
        dbg_d = nc.dram_tensor("dbg", [TOWN, D], F32, kind="ExternalOutput").ap()

    puv_bf = nc.dram_tensor("puv_bf", [16384, 2 * D], BF16, kind="Internal").ap()
    tblU = Buf("tblUV")

    k = K(nc)

    ident_f = k.sbuf("ident_f", [128, 128], F32, side="right")
    ident_b = k.sbuf("ident_b", [128, 128], BF16, side="right")
    ones_r = k.sbuf("ones_r", [128, 128], F32R, side="right")
    iot = k.sbuf("iot", [128, 128], F32, side="right")
    vT = k.sbuf("vT", [128, 112], F32, side="right")
    pidx = k.sbuf("pidx", [128, 1], F32, side="right")
    negh = k.sbuf("negh", [128, 1], F32, side="right")
    psd = [k.psum(f"psd{i}", [128, 1024], F32) for i in range(4)]
    ps = [Buf(f"ps{i}", psd[i // 2][:, (i % 2) * 512:(i % 2 + 1) * 512]) for i in range(8)]
    d_setup = k.dsem("setup")
    d_out = k.dsem("out")

    k.op("pool", lambda e: e.iota(iot[:], pattern=[[1, 128]], base=0, channel_multiplier=-1,
                                  allow_small_or_imprecise_dtypes=True), writes=[iot])
    k.ts(ident_f[:], iot[:], 0.0, None, ALU.is_equal, None, [iot], [ident_f])
    k.copy(ident_b[:], ident_f[:], [ident_f], [ident_b])
    k.memset(iot[:], 1.0, [iot])
    k.copy(ones_r[:], iot[:], [iot], [ones_r])
    k.op("pool", lambda e: e.iota(pidx[:], pattern=[[0, 1]], base=0, channel_multiplier=1,
                                  allow_small_or_imprecise_dtypes=True), writes=[pidx])

    k.memset(negh[:], -0.5, [negh])
    m0 = k.mark()
    vraw = k.sbuf("vraw", [112, 128], F32)
    k.dma("sp", vraw[:], vecs_d, d_setup, writes=[vraw])
    k.tr(ps[0][:, 0:112], vraw[:], ident_f[0:112, 0:112], [vraw, ident_f], [ps[0]])
    k.copy(vT[:], ps[0][:, 0:112], [ps[0]], [vT])
    k.release(m0)
    G_MIX, G_MEM, G_MEMN, G_FFN = 0, 16, 32, 48
    V_CB, V_LNG, V_LNB, V_GC, V_GS, V_SD = 64, 72, 80, 88, 96, 104

    NB = {}
    NXB = 4
    d_x = [k.dsem(f"x{i}") for i in range(NXB)]
    cnt = {"x": 0}

    def alloc_norm_bufs(with_x=True):
        if with_x:
            NB["xtile"] = [k.sbuf(f"xtile{i}", [128, D], F32) for i in range(NXB)]
        NB["xsb"] = [k.sbuf(f"xsb{i}", [128, D], BF16) for i in range(2)]
        NB["junk"] = k.sbuf("junk", [128, D], BF16)
        NB["stat"] = [k.sbuf(f"stat{i}", [128, 4], F32) for i in range(2)]

    def load_tok_tile(src_ap):
        i = cnt["x"] % 2
        cnt["x"] += 1
        k.dma("sp", NB["xtile"][i][:], src_ap, d_x[i], writes=[NB["xtile"][i]])
        return i

    def pool_rsqrt(buf, out_ap, in_ap, scale, shape):
        k.ts(out_ap, in_ap, scale, EPS, ALU.mult, ALU.add, [buf], [buf], q="pool")
        nh = negh[:, 0:1] if shape[1] == 1 else negh[:, 0:1].to_broadcast(list(shape))
        k.tt(out_ap, out_ap, nh, ALU.pow, [buf, negh], [buf], q="pool")

    def rstd_of(src_buf, src_ap, st, n):
        junk = NB["junk"]
        k.act(junk[:, 0:n], src_ap, AF.Square, [src_buf], [junk, st], accum_out=st[:, 0:1])
        pool_rsqrt(st, st[:, 1:2], st[:, 0:1], 1.0 / n, [128, 1])

    def fold_g(wb, g_off, ncols):
        for kt in range(KT):
            k.ts(wb[:, kt, 0:ncols], wb[:, kt, 0:ncols], vT[:, g_off + kt:g_off + kt + 1], None, ALU.mult, None,
                 [wb, vT], [wb])

    def norm_A(src_buf, src_ap, i):
        st = NB["stat"][i]
        rstd_of(src_buf, src_ap, st, D)
        k.ts(NB["xsb"][i][:], src_ap, st[:, 1:2], None, ALU.mult, None, [src_buf, st], [NB["xsb"][i]])

    def norm_B(dstT_buf, dst_ap_fn, pbank, i):
        xsb = NB["xsb"]
        for half in range(2):
            pb = ps[pbank + half]
            pv = pb[:].bitcast(BF16)
            for j in range(8):
                kt = half * 8 + j
                k.tr(pv[:, j * 128:(j + 1) * 128], xsb[i][:, kt * 128:(kt + 1) * 128], ident_b[:],
                     [xsb[i], ident_b], [pb])
            pin = pv.rearrange("p (j t) -> p j t", j=8)
            if half == 0:
                k.act(dst_ap_fn(half), pin, AF.Copy, [pb], [dstT_buf])
            else:
                k.copy(dst_ap_fn(half), pin, [pb], [dstT_buf])

    def norm_pipeline(items, post=None):
        n = len(items)

        def is_dram(t):
            return not isinstance(items[t][0], tuple)

        def issue_load(t):
            if is_dram(t):
                i = t % NXB
                k.dma("sp", NB["xtile"][i][:], items[t][0], d_x[i], writes=[NB["xtile"][i]])

        def doA(t):
            if is_dram(t):
                xb = NB["xtile"][t % NXB]
                norm_A(xb, xb[:], t % 2)
            else:
                norm_A(items[t][0][0], items[t][0][1], t % 2)
        for t in range(min(NXB - 1, n)):
            issue_load(t)
        doA(0)
        for t in range(n):
            if t + NXB - 1 < n:
                issue_load(t + NXB - 1)
            if t + 1 < n:
                doA(t + 1)
            norm_B(items[t][1], items[t][2], 0, t % 2)
            if post is not None:
                post(t)

    uT = k.sbuf("uT", [128, 8, TALL], BF16)
    mA = k.mark()
    alloc_norm_bufs()
    xtile = NB["xtile"]
    w_ssm = k.sbuf("w_ssm", [128, KT, 1024], BF16)
    d_w = k.dsem("w0")
    for kt in range(KT):
        k.dma("pool", w_ssm[:, kt, :], w_in_d[kt * 128:(kt + 1) * 128, 2048:3072], d_w, writes=[w_ssm])
    k.group_end(d_w, [w_ssm])
    fold_g(w_ssm, G_MIX, 1024)
    xnTs = [k.sbuf(f"xnT{i}", [128, KT, BLK], BF16) for i in range(2)]
    def a1_src(r0):
        def f():
            i = load_tok_tile(xs_d[r0:r0 + 128, :])
            return xtile[i], xtile[i][:]
        return f

    def a1_post(t):
        if t % 4 != 3:
            return
        b = t // 4
        xnT = xnTs[b % 2]
        for ct in range(8):
            pb = ps[2 + ct % 2]
            for kt in range(KT):
                k.mm(pb[:], w_ssm[:, kt, ct * 128:(ct + 1) * 128], xnT[:, kt, :], kt == 0, kt == KT - 1,
                     [w_ssm, xnT], [pb])
            k.act(uT[:, ct, b * BLK:(b + 1) * BLK], pb[:], AF.Copy, [pb], [uT])

    items = []
    for b in range(NBLK):
        for tl in range(4):
            xnT = xnTs[b % 2]
            r0 = b * BLK + tl * 128
            items.append((xs_d[r0:r0 + 128, :], xnT,
                          lambda half, tl=tl, xnT=xnT: xnT.ap(half * 8 * BLK + tl * 128, [[BLK, 8], [1, 128]])))
    norm_pipeline(items, a1_post)
    k.release(mA)

    mS = k.mark()
    s5w = {nm: k.sbuf(nm, [128, 32, 128], BF16) for nm in ("Bre_pad", "Bim_pad", "WA", "NWA", "NWB")}
    parT = k.sbuf("parT", [128, 18, 32], F32)
    jvec = k.sbuf("jvec", [128, BLK], F32)
    k.op("pool", lambda e: e.iota(jvec[:], pattern=[[1, BLK]], base=0, channel_multiplier=0,
                                  allow_small_or_imprecise_dtypes=True), writes=[jvec])
    jrev = k.sbuf("jrev", [128, BLK], F32)
    k.op("pool", lambda e: e.iota(jrev[:], pattern=[[-1, BLK]], base=BLK - 1, channel_multiplier=0,
                                  allow_small_or_imprecise_dtypes=True), writes=[jrev])
    mP = k.mark()
    la = {nm: k.sbuf("la_" + nm, [32, 128], F32) for nm in
          ("lre", "lim", "dtb", "ar", "th", "rho", "phi", "f", "fa", "sinv", "cosv", "lbr", "lbi", "nr", "den",
           "t1", "t2", "cr", "ci", "f512", "g512", "ga512")}
    lai = k.sbuf("la_i", [32, 128], I32)
    par = k.sbuf("par", [32, 18, 128], F32)
    ldt = k.sbuf("ldt", [32, 2], F32)
    dt2 = k.sbuf("dt2", [32, 2], F32)
    k.dma("sp", la["lre"][:], lre_d, d_setup, writes=[la["lre"]])
    k.dma("sp", la["lim"][:], lim_d, d_setup, writes=[la["lim"]])
    k.dma("sp", ldt[:], ldt_d, d_setup, writes=[ldt])
    k.group_end(d_setup, [la["lre"], la["lim"], ldt, vraw])
    k.act(dt2[:], ldt[:], AF.Exp, [ldt], [dt2])
    k.copy(la["dtb"].ap(0, [[64, 2], [1, 64]]), dt2.ap(0, [[1, 2], [0, 64]]), [dt2], [la["dtb"]])

    def L(n):
        return la[n][:]
    k.tt(L("ar"), L("lre"), L("dtb"), ALU.mult, [la["lre"], la["dtb"]], [la["ar"]])
    k.tt(L("th"), L("lim"), L("dtb"), ALU.mult, [la["lim"], la["dtb"]], [la["th"]])
    k.act(L("rho"), L("ar"), AF.Exp, [la["ar"]], [la["rho"]])
    k.ts(L("phi"), L("th"), 1.0 / (2 * math.pi), None, ALU.mult, None, [la["th"]], [la["phi"]])

    def frac(dst, src):
        k.copy(lai[:], la[src][:], [la[src]], [lai])
        k.tt(la[dst][:], la[src][:], lai[:], ALU.subtract, [la[src], lai], [la[dst]])

    def sincos(sin_dst, cos_dst, fsrc, tmp):
        k.act(la[sin_dst][:], la[fsrc][:], AF.Sin, [la[fsrc]], [la[sin_dst]], scale=TWO_PI_S)
        k.act(la[tmp][:], la[fsrc][:], AF.Abs, [la[fsrc]], [la[tmp]])
        k.act(la[cos_dst][:], la[tmp][:], AF.Sin, [la[tmp]], [la[cos_dst]], scale=-TWO_PI_S, bias=math.pi / 2)

    frac("f", "phi")
    sincos("sinv", "cosv", "f", "fa")
    k.tt(L("lbr"), L("rho"), L("cosv"), ALU.mult, [la["rho"], la["cosv"]], [la["lbr"]])
    k.tt(L("lbi"), L("rho"), L("sinv"), ALU.mult, [la["rho"], la["sinv"]], [la["lbi"]])
    k.ts(L("nr"), L("lbr"), -1.0, None, ALU.add, None, [la["lbr"]], [la["nr"]])
    k.tt(L("t1"), L("lre"), L("lre"), ALU.mult, [la["lre"]], [la["t1"]])
    k.tt(L("t2"), L("lim"), L("lim"), ALU.mult, [la["lim"]], [la["t2"]])
    k.tt(L("den"), L("t1"), L("t2"), ALU.add, [la["t1"], la["t2"]], [la["den"]])
    k.op("dve", lambda e: e.reciprocal(L("den"), L("den")), reads=[la["den"]], writes=[la["den"]])
    k.tt(L("t1"), L("nr"), L("lre"), ALU.mult, [la["nr"], la["lre"]], [la["t1"]])
    k.tt(L("t2"), L("lbi"), L("lim"), ALU.mult, [la["lbi"], la["lim"]], [la["t2"]])
    k.tt(L("cr"), L("t1"), L("t2"), ALU.add, [la["t1"], la["t2"]], [la["cr"]])
    k.tt(L("cr"), L("cr"), L("den"), ALU.mult, [la["cr"], la["den"]], [la["cr"]])
    k.tt(L("t1"), L("lbi"), L("lre"), ALU.mult, [la["lbi"], la["lre"]], [la["t1"]])
    k.tt(L("t2"), L("nr"), L("lim"), ALU.mult, [la["nr"], la["lim"]], [la["t2"]])
    k.tt(L("ci"), L("t1"), L("t2"), ALU.subtract, [la["t1"], la["t2"]], [la["ci"]])
    k.tt(L("ci"), L("ci"), L("den"), ALU.mult, [la["ci"], la["den"]], [la["ci"]])
    k.ts(L("f512"), L("f"), float(BLK), None, ALU.mult, None, [la["f"]], [la["f512"]])
    frac("g512", "f512")
    k.copy(par[:, 0, :], L("f"), [la["f"]], [par])
    k.copy(par[:, 1, :], L("rho"), [la["rho"]], [par])
    sincos("t1", "t2", "g512", "ga512")
    k.copy(par[:, 2, :], L("t2"), [la["t2"]], [par])
    k.copy(par[:, 3, :], L("t1"), [la["t1"]], [par])
    k.ts(par[:, 4, :], L("t1"), -1.0, None, ALU.mult, None, [la["t1"]], [par])
    k.copy(par[:, 5, :], L("ar"), [la["ar"]], [par])
    for b_ in range(6):
        kk = 5 - b_
        k.ts(L("f512"), L("g512"), float(kk), None, ALU.mult, None, [la["g512"]], [la["f512"]])
        frac("ga512", "f512")
        sincos("t1", "t2", "ga512", "fa")
        k.act(L("lbr"), L("ar"), AF.Exp, [la["ar"]], [la["lbr"]], scale=float(BLK * kk))
        k.tt(par[:, 6 + b_, :], L("lbr"), L("t2"), ALU.mult, [la["lbr"], la["t2"]], [par])
        k.tt(par[:, 12 + b_, :], L("lbr"), L("t1"), ALU.mult, [la["lbr"], la["t1"]], [par])
    for i in range(18):
        pb_ = ps[i // 16]
        k.tr(pb_[:, (i % 16) * 32:(i % 16 + 1) * 32], par[:, i, :], ident_f[0:32, 0:32], [par, ident_f], [pb_])
    k.copy(parT[:, 0:16, :].rearrange("p a b -> p (a b)"), ps[0][:, 0:512], [ps[0]], [parT])
    k.copy(parT[:, 16:18, :].rearrange("p a b -> p (a b)"), ps[1][:, 0:64], [ps[1]], [parT])

    XT = {nm: k.sbuf("XT_" + nm, [128, 32, 16], F32) for nm in ("bre", "bim", "cre", "cim")}
    mP1 = k.mark()
    raw_re = k.sbuf("raw_re", [32, 2048], F32)
    raw_im = k.sbuf("raw_im", [32, 2048], F32)
    Bre = k.sbuf("Bre", [32, 2048], F32)
    Bim = k.sbuf("Bim", [32, 2048], F32)
    tb = k.sbuf("tb", [32, 2048], F32)
    d_raw = k.dsem("raw")
    k.dma("sp", raw_re[:], bre_d, d_raw, writes=[raw_re])
    k.dma("sp", raw_im[:], bim_d, d_raw, writes=[raw_im])
    k.group_end(d_raw, [raw_re, raw_im])

    def v3(buf):
        return buf.ap(0, [[16, 128], [1, 16]])

    def bc(buf):
        return buf.ap(0, [[1, 128], [0, 16]])
    k.tt(v3(Bre), v3(raw_re), bc(la["cr"]), ALU.mult, [raw_re, la["cr"]], [Bre])
    k.tt(v3(tb), v3(raw_im), bc(la["ci"]), ALU.mult, [raw_im, la["ci"]], [tb])
    k.tt(Bre[:], Bre[:], tb[:], ALU.subtract, [Bre, tb], [Bre])
    k.tt(v3(Bim), v3(raw_im), bc(la["cr"]), ALU.mult, [raw_im, la["cr"]], [Bim])
    k.tt(v3(tb), v3(raw_re), bc(la["ci"]), ALU.mult, [raw_re, la["ci"]], [tb])
    k.tt(Bim[:], Bim[:], tb[:], ALU.add, [Bim, tb], [Bim])

    def to_XT(nm, in_fn, rd):
        pb = ps[1]
        for h in range(16):
            k.tr(pb[:, h * 32:(h + 1) * 32], in_fn(h), ident_f[0:32, 0:32], rd + [ident_f], [pb])
        k.copy(XT[nm].ap(0, [[1, 16], [16, 32]]), pb[:].rearrange("p (h a) -> p h a", h=16), [pb], [XT[nm]])

    to_XT("bre", lambda h: Bre.ap(h, [[16, 128]]), [Bre])
    to_XT("bim", lambda h: Bim.ap(h, [[16, 128]]), [Bim])
    k.dma("sp", raw_re[:], cre_d, d_raw, writes=[raw_re])
    k.dma("sp", raw_im[:], cim_d, d_raw, writes=[raw_im])
    k.group_end(d_raw, [raw_re, raw_im])
    for nm, src in (("cre", raw_re), ("cim", raw_im)):
        k.copy(tb.ap(0, [[128, 16], [64, 2], [1, 64]]), src.ap(0, [[64, 16], [1024, 2], [1, 64]]), [src], [tb])
        to_XT(nm, lambda h: tb[:, h * 128:(h + 1) * 128], [tb])
    k.release(mP1)

    mask2 = k.sbuf("mask2", [128, 4], F32)
    k.ts(mask2[:, 0:1], pidx[:], 64.0, None, ALU.is_lt, None, [pidx], [mask2])
    k.ts(mask2[:, 1:2], pidx[:], 64.0, None, ALU.is_ge, None, [pidx], [mask2])
    k.ts(mask2[:, 2:4], mask2[:, 0:2], -1.0, None, ALU.mult, None, [mask2], [mask2])

    padf = k.sbuf("padf", [128, 32, 128], F32)

    def build_padded(dst_buf, src, neg, dt_direct):
        tgt = dst_buf if dt_direct else padf
        k.memset(tgt[:], 0.0, [tgt])
        for g2 in range(2):
            for b4 in range(4):
                o_ap = tgt.ap(b4 * 128 + b4 * 32 + 16 * g2, [[512, 8], [1, 16]])
                i_ap = src.ap(b4 * 16, [[64, 8], [1, 16]])
                mcol = mask2[:, (2 if neg else 0) + g2:(2 if neg else 0) + g2 + 1]
                k.ts(o_ap, i_ap, mcol, None, ALU.mult, None, [src, mask2], [tgt])

    build_padded(s5w["WA"], XT["cre"], False, True)
    build_padded(s5w["NWA"], XT["cre"], True, True)
    build_padded(s5w["NWB"], XT["cim"], True, True)
    for nm, dst in (("bre", "Bre_pad"), ("bim", "Bim_pad")):
        build_padded(None, XT[nm], False, False)
        for a in range(8):
            pb = ps[2 + a % 2]
            for b4 in range(4):
                k.tr(pb[:, b4 * 128:(b4 + 1) * 128], padf[:, a * 4 + b4, :], ident_f[:], [padf, ident_f], [pb])
            k.copy(s5w[dst][:, a * 4:(a + 1) * 4, :].rearrange("p a b -> p (a b)"), pb[:], [pb], [s5w[dst]])
    k.release(mP)


    mark_yn = k.mark("right")
    yn = k.sbuf("yn", [128, 16, TOWN], BF16, side="right")
    mZ = k.mark("right")
    zb = k.sbuf("zb", [128, 8, TOWN], BF16, side="right")
    tabC = [k.sbuf(f"tabC{i}", [128, BLK], F32) for i in range(2)]
    tabS = [k.sbuf(f"tabS{i}", [128, BLK], F32) for i in range(2)]
    tabWC = [k.sbuf(f"tWC{i}", [128, BLK], F32) for i in range(2)]
    tabWS = [k.sbuf(f"tWS{i}", [128, BLK], F32) for i in range(2)]
    g_ang = k.sbuf("g_ang", [128, BLK], F32)
    g_fr = k.sbuf("g_fr", [128, BLK], F32)
    g_fab = g_ang
    g_wt = g_fr
    pacc = k.sbuf("pacc", [128, 6, 4], F32)
    psr = k.sbuf("psr", [128, 2, 6], F32)
    pcc = k.sbuf("pcc", [128, 8], F32)
    j6 = k.sbuf("j6", [128, 6], F32)
    print("S5 loop: sbuf bytes remaining", nc.sbuf_bytes_remaining)
    angi = k.sbuf("angi", [128, BLK], I32)
    tmp = [k.sbuf(f"s5t{i}", [128, BLK], F32) for i in range(4)]
    ang, fr, fab = tmp[0], tmp[1], tmp[2]
    bh0 = [k.sbuf(f"bh{j}", [128, BLK], F32) for j in range(2)]
    zz0 = [k.sbuf(f"zz{j}", [128, BLK], F32) for j in range(2)]
    bh = [bh0, bh0]
    zz = [zz0, zz0]
    pp = [[k.sbuf(f"pp{i}{j}", [128, BLK], BF16) for j in range(4)] for i in range(2)]
    init = [k.sbuf(f"init{i}", [128, 2], F32) for i in range(2)]
    tini = k.sbuf("tini", [128, 2], F32)
    ytmp = tmp[0]
    d_cv = k.dsem("cv")
    CR = 512
    def issue_conversion(c):
        k.dma("pool", puv_bf[c * CR:(c + 1) * CR, 0:D], pu_d[c * CR:(c + 1) * CR, :], d_cv, writes=[tblU])
        k.dma("pool", puv_bf[c * CR:(c + 1) * CR, D:2 * D], pv_d[c * CR:(c + 1) * CR, :], d_cv, writes=[tblU])
    assert 16384 // CR == 32
    it = 0
    def gen_tables_front(P):
        sl = P % 2
        k.ts(g_ang[:], jvec[:], parT[:, 0, P:P + 1], None, ALU.mult, None, [jvec, parT], [g_ang])
        k.copy(angi[:], g_ang[:], [g_ang], [angi])
        k.tt(g_fr[:], g_ang[:], angi[:], ALU.subtract, [g_ang, angi], [g_fr])
        k.act(tabS[sl][:], g_fr[:], AF.Sin, [g_fr], [tabS[sl]], scale=TWO_PI_S)
        k.act(g_fab[:], g_fr[:], AF.Abs, [g_fr], [g_fab])
        k.act(tabC[sl][:], g_fab[:], AF.Sin, [g_fab], [tabC[sl]], scale=-TWO_PI_S, bias=math.pi / 2)
        k.act(g_wt[:], jrev[:], AF.Exp, [jrev, parT], [g_wt], scale=parT[:, 5, P:P + 1])

    def gen_tables_back(P):
        sl = P % 2
        k.tt(tabWC[sl][:], tabC[sl][:], g_wt[:], ALU.mult, [tabC[sl], g_wt], [tabWC[sl]])
        k.tt(tabWS[sl][:], tabS[sl][:], g_wt[:], ALU.mult, [tabS[sl], g_wt], [tabWS[sl]])

    gen_tables_front(0)
    gen_tables_back(0)
    for P in range(32):
        ct = P // 4
        tC, tS, tWC, tWS = tabC[P % 2], tabS[P % 2], tabWC[P % 2], tabWS[P % 2]
        if P + 1 < 32:
            gen_tables_front(P + 1)
        issue_conversion(P)
        rho_b = parT[:, 1, P:P + 1].to_broadcast([128, BLK])
        NPB = NBLK - 2
        for b in range(NBLK):
            s = it % 2
            it += 1
            pre, pim = ps[4 + 2 * s], ps[5 + 2 * s]
            rhs = uT[:, ct, b * BLK:(b + 1) * BLK]
            k.mm(pre[:], s5w["Bre_pad"][:, P, :], rhs, True, True, [s5w["Bre_pad"], uT], [pre])
            k.mm(pim[:], s5w["Bim_pad"][:, P, :], rhs, True, True, [s5w["Bim_pad"], uT], [pim])
            if b < NPB:
                for q_, (tw, pq) in enumerate(((tWC, pre), (tWS, pim), (tWC, pim), (tWS, pre))):
                    k.op("dve", lambda e, tw=tw, pq=pq, q_=q_: e.scalar_tensor_tensor(
                        out=tmp[0][:], in0=tw[:], scalar=1.0, in1=pq[:], op0=ALU.mult, op1=ALU.mult,
                        accum_out=pacc[:, b, q_:q_ + 1]), reads=[tw, pq], writes=[tmp[0], pacc])
                if b == NPB - 1:
                    k.tt(psr[:, 0, :], pacc[:, :, 0], pacc[:, :, 1], ALU.add, [pacc], [psr])
                    k.tt(psr[:, 1, :], pacc[:, :, 2], pacc[:, :, 3], ALU.subtract, [pacc], [psr])
                    mre = parT.ap(6 * 32 + P, [[32, 6]])
                    mim = parT.ap(12 * 32 + P, [[32, 6]])
                    for q_, (sv, mw) in enumerate(((0, mre), (1, mim), (1, mre), (0, mim))):
                        k.op("dve", lambda e, sv=sv, mw=mw, q_=q_: e.scalar_tensor_tensor(
                            out=j6[:], in0=psr[:, sv, :], scalar=1.0, in1=mw, op0=ALU.mult, op1=ALU.mult,
                            accum_out=pcc[:, q_:q_ + 1]), reads=[psr, parT], writes=[j6, pcc])
                    k.tt(pcc[:, 4:5], pcc[:, 0:1], pcc[:, 1:2], ALU.subtract, [pcc], [pcc])
                    k.tt(pcc[:, 5:6], pcc[:, 2:3], pcc[:, 3:4], ALU.add, [pcc], [pcc])
                    nin = init[(b + 1) % 2]
                    c5, s5c, ns5 = parT[:, 2, P:P + 1], parT[:, 3, P:P + 1], parT[:, 4, P:P + 1]
                    k.tt(tini[:, 0:1], pcc[:, 4:5], c5, ALU.mult, [pcc, parT], [tini])
                    k.tt(tini[:, 1:2], pcc[:, 5:6], c5, ALU.mult, [pcc, parT], [tini])
                    k.stt(nin[:, 0:1], pcc[:, 5:6], ns5, tini[:, 0:1], ALU.mult, ALU.add, [pcc, parT, tini], [nin])
                    k.stt(nin[:, 1:2], pcc[:, 4:5], s5c, tini[:, 1:2], ALU.mult, ALU.add, [pcc, parT, tini], [nin])
                continue
            k.tt(tmp[0][:], tC[:], pre[:], ALU.mult, [tC, pre], [tmp[0]])
            k.tt(tmp[1][:], tS[:], pim[:], ALU.mult, [tS, pim], [tmp[1]])
            k.tt(bh[s][0][:], tmp[0][:], tmp[1][:], ALU.add, [tmp[0], tmp[1]], [bh[s][0]])
            k.tt(tmp[2][:], tC[:], pim[:], ALU.mult, [tC, pim], [tmp[2]])
            k.tt(tmp[3][:], tS[:], pre[:], ALU.mult, [tS, pre], [tmp[3]])
            k.tt(bh[s][1][:], tmp[2][:], tmp[3][:], ALU.subtract, [tmp[2], tmp[3]], [bh[s][1]])
            ini = init[b % 2]
            for c in range(2):
                iv = ini[:, c:c + 1]
                rd = [parT, bh[s][c], ini]
                k.op("dve", lambda e, c=c, iv=iv: e.tensor_tensor_scan(
                    out=zz[s][c][:], data0=rho_b, data1=bh[s][c][:], initial=iv, op0=ALU.mult, op1=ALU.add),
                    reads=rd, writes=[zz[s][c]])
            if b < NBLK - 1:
                nin = init[(b + 1) % 2]
                e_re, e_im = zz[s][0][:, BLK - 1:BLK], zz[s][1][:, BLK - 1:BLK]
                c5, s5c, ns5 = parT[:, 2, P:P + 1], parT[:, 3, P:P + 1], parT[:, 4, P:P + 1]
                k.tt(tini[:, 0:1], e_re, c5, ALU.mult, [zz[s][0], parT], [tini])
                k.tt(tini[:, 1:2], e_im, c5, ALU.mult, [zz[s][1], parT], [tini])
                k.stt(nin[:, 0:1], e_im, ns5, tini[:, 0:1], ALU.mult, ALU.add, [zz[s][1], parT, tini], [nin])
                k.stt(nin[:, 1:2], e_re, s5c, tini[:, 1:2], ALU.mult, ALU.add, [zz[s][0], parT, tini], [nin])
            if b >= NBLK - 2:
                ob = b - (NBLK - 2)
                yb = ps[2 + ob]
                k.tt(pp[s][0][:], tC[:], zz[s][0][:], ALU.mult, [tC, zz[s][0]], [pp[s][0]])
                k.tt(pp[s][1][:], tS[:], zz[s][1][:], ALU.mult, [tS, zz[s][1]], [pp[s][1]], q="pool")
                k.tt(pp[s][2][:], tS[:], zz[s][0][:], ALU.mult, [tS, zz[s][0]], [pp[s][2]])
                k.tt(pp[s][3][:], tC[:], zz[s][1][:], ALU.mult, [tC, zz[s][1]], [pp[s][3]], q="pool")
                wl = ["WA", "NWA", "NWB", "NWB"]
                for j in range(4):
                    k.mm(yb[:], s5w[wl[j]][:, P, :], pp[s][j][:], (P % 4 == 0 and j == 0), (P % 4 == 3 and j == 3),
                         [s5w[wl[j]], pp[s][j]], [yb])
                if P % 4 == 3:
                    k.stt(ytmp[:], uT[:, ct, b * BLK:(b + 1) * BLK], vT[:, V_SD + ct:V_SD + ct + 1], yb[:],
                          ALU.mult, ALU.add, [uT, vT, yb], [ytmp])
                    k.act(zb[:, ct, ob * BLK:(ob + 1) * BLK], ytmp[:], AF.Gelu_apprx_tanh, [ytmp], [zb])

        if P + 1 < 32:
            gen_tables_back(P + 1)

    k.group_end(d_cv, [tblU])
    if stage == "s5":
        obuf = k.sbuf("obuf", [128, D], F32)
        for tt_ in range(8):
            k.memset(obuf[:], 0.0, [obuf])
            for ct in range(8):
                pvb = ps[0][:].bitcast(BF16)
                k.tr(pvb[:, 0:128], zb[:, ct, tt_ * 128:(tt_ + 1) * 128], ident_b[:], [zb, ident_b], [ps[0]])
                k.copy(obuf[:, ct * 128:(ct + 1) * 128], pvb[:, 0:128], [ps[0]], [obuf])
            k.dma("sp", dbg_d[tt_ * 128:(tt_ + 1) * 128, :], obuf[:], d_out, reads=[obuf])
            k.dma("sp", y_d[tt_ * 128:(tt_ + 1) * 128, :], obuf[:], d_out, reads=[obuf])
        k.finish("sp")
        print("instructions", k.n_inst, "waits", k.n_wait)
        return nc

    k.release(mS)
    k.release(0)

    def chan_rstd(dst, src_sq_fn, nct, blk, bank, extra_mean=None):
        pb = ps[bank]
        for ct in range(nct):
            sqb = src_sq_fn(ct)
            k.mm(pb[:], ones_r[:], sqb[:], ct == 0, ct == nct - 1, [ones_r, sqb], [pb])

    glu_w = k.sbuf("glu_w", [128, 8, 1024], BF16, side="right")
    d_w1 = k.dsem("w1")
    for kt in range(8):
        k.dma("pool", glu_w[:, kt, :], glu_d[kt * 128:(kt + 1) * 128, :], d_w1, writes=[glu_w])
    k.group_end(d_w1, [glu_w])
    w_ag = k.sbuf("w_ag", [128, KT, 2048], BF16)
    d_w2 = k.dsem("w2")
    for kt in range(KT):
        k.dma("pool", w_ag[:, kt, :], w_in_d[kt * 128:(kt + 1) * 128, 0:2048], d_w2, writes=[w_ag])
    k.group_end(d_w2, [w_ag])
    mG = k.mark()
    ysf = k.sbuf("ysf", [128, 8, TOWN], F32)
    sg = [k.sbuf(f"sg{i}", [128, BLK], F32) for i in range(2)]
    sq = [k.sbuf(f"sq{i}", [128, BLK], F32R) for i in range(2)]
    rsd = k.sbuf("rsd", [128, BLK], F32)
    tmpn = k.sbuf("tmpn", [128, BLK], F32)
    for blk in range(2):
        tsl = slice(blk * BLK, (blk + 1) * BLK)
        for ct in range(8):
            pb = ps[ct % 2]
            for kt in range(8):
                k.mm(pb[:], glu_w[:, kt, ct * 128:(ct + 1) * 128], zb[:, kt, tsl], kt == 0, kt == 7, [glu_w, zb], [pb])
            k.act(sg[ct % 2][:], pb[:], AF.Sigmoid, [pb], [sg[ct % 2]])
            k.tt(ysf[:, ct, tsl], zb[:, ct, tsl], sg[ct % 2][:], ALU.mult, [zb, sg[ct % 2]], [ysf])
        pb = ps[2]
        for ct in range(8):
            k.act(sq[ct % 2][:], ysf[:, ct, tsl], AF.Square, [ysf], [sq[ct % 2]])
            k.mm(pb[:], ones_r[:], sq[ct % 2][:], ct == 0, ct == 7, [ones_r, sq[ct % 2]], [pb])
        k.act(rsd[:], pb[:], AF.Sqrt, [pb], [rsd], scale=1.0 / 1024, bias=EPS)
        k.op("dve", lambda e: e.reciprocal(rsd[:], rsd[:]), reads=[rsd], writes=[rsd])
        for ct in range(8):
            k.stt(yn[:, 8 + ct, tsl], ysf[:, ct, tsl], vT[:, V_GS + ct:V_GS + ct + 1], rsd[:], ALU.mult, ALU.mult,
                  [ysf, vT, rsd], [yn])
    k.release(mG)
    k.release(mZ, "right")

    mCr = k.mark("right")
    HAL = 30
    cbuf = k.sbuf("cbuf", [128, 8, HAL + TOWN + 2], BF16, side="right")
    cw = k.sbuf("cw", [128, 8, 31], F32, side="right")
    cwraw = k.sbuf("cwraw", [31, 1024], F32)
    k.dma("sp", cwraw[:], convw_d, d_setup, writes=[cwraw])
    k.group_end(d_setup, [cwraw])
    for ct in range(8):
        k.tr(ps[0][:, ct * 32:ct * 32 + 31], cwraw[:, ct * 128:(ct + 1) * 128], ident_f[0:31, 0:31], [cwraw, ident_f], [ps[0]])
    k.copy(cw[:], ps[0][:, 0:256].rearrange("p (c k) -> p c k", c=8)[:, :, 0:31], [ps[0]], [cw])
    alloc_norm_bufs()
    xtile = NB["xtile"]
    fold_g(w_ag, G_MIX, 2048)
    xnT = k.sbuf("xnT2", [128, KT, BLK], BF16)
    sgc = [k.sbuf(f"sgc{i}", [128, BLK], F32) for i in range(2)]
    def cv_post(t):
        if t % 4 != 3:
            return
        b = 5 + t // 4
        for ct in range(8):
            pa, pg = ps[2 + 2 * (ct % 2)], ps[3 + 2 * (ct % 2)]
            for kt in range(KT):
                k.mm(pg[:], w_ag[:, kt, 1024 + ct * 128:1024 + (ct + 1) * 128], xnT[:, kt, :], kt == 0, kt == KT - 1,
                     [w_ag, xnT], [pg])
            for kt in range(KT):
                k.mm(pa[:], w_ag[:, kt, ct * 128:(ct + 1) * 128], xnT[:, kt, :], kt == 0, kt == KT - 1,
                     [w_ag, xnT], [pa])
            k.act(sgc[ct % 2][:], pg[:], AF.Sigmoid, [pg], [sgc[ct % 2]])
            if b == 5:
                k.tt(cbuf[:, ct, 0:HAL], pa[:, BLK - HAL:BLK], sgc[ct % 2][:, BLK - HAL:BLK], ALU.mult,
                     [pa, sgc[ct % 2]], [cbuf])
            else:
                o = HAL + (b - 6) * BLK
                k.tt(cbuf[:, ct, o:o + BLK], pa[:], sgc[ct % 2][:], ALU.mult, [pa, sgc[ct % 2]], [cbuf])

    items = []
    for b in range(5, 8):
        for tl in range(4):
            r0 = b * BLK + tl * 128
            items.append((xs_d[r0:r0 + 128, :], xnT,
                          lambda half, tl=tl: xnT.ap(half * 8 * BLK + tl * 128, [[BLK, 8], [1, 128]])))
    norm_pipeline(items, cv_post)
    k.release(0)
    cv = k.sbuf("cv", [128, 8, TOWN], F32)
    dg = k.sbuf("dg", [128, 8, 31, 128], BF16, side="right")
    for ct in range(8):
        for kk in range(31):
            k.ts(dg[:, ct, kk, :], ident_f[:], cw[:, ct, kk:kk + 1], None, ALU.mult, None, [ident_f, cw], [dg],
                 q="dve")
    for ct in range(8):
        for blk in range(2):
            pb = ps[2 + (ct * 2 + blk) % 4]
            for kk in range(31):
                k.mm(pb[:], dg[:, ct, kk, :], cbuf[:, ct, blk * BLK + kk:blk * BLK + kk + BLK], kk == 0, kk == 30,
                     [dg, cbuf], [pb])
            k.act(cv[:, ct, blk * BLK:(blk + 1) * BLK], pb[:], AF.Identity, [pb, vT], [cv],
                  bias=vT[:, V_CB + ct:V_CB + ct + 1])
    k.release(mCr, "right")
    wbuf = k.sbuf("wbuf", [128, KT, D], BF16, side="right")
    d_w3 = k.dsem("w3")
    for kt in range(KT):
        k.dma("pool", wbuf[:, kt, :], w_out_d[kt * 128:(kt + 1) * 128, :], d_w3, writes=[wbuf])
    k.group_end(d_w3, [wbuf])
    sqc = [k.sbuf(f"sqc{i}", [128, BLK], F32R) for i in range(2)]
    cvr = [k.sbuf(f"cvr{i}", [128, BLK], F32R) for i in range(2)]
    mean = k.sbuf("mean", [128, BLK], F32)
    var = k.sbuf("var", [128, BLK], F32)
    m2 = k.sbuf("m2", [128, BLK], F32)
    rsd2 = k.sbuf("rsd2", [128, BLK], F32)
    for blk in range(2):
        tsl = slice(blk * BLK, (blk + 1) * BLK)
        p1, p2 = ps[0], ps[1]
        for ct in range(8):
            k.copy(cvr[ct % 2][:], cv[:, ct, tsl], [cv], [cvr[ct % 2]], q="pool")
            k.act(sqc[ct % 2][:], cv[:, ct, tsl], AF.Square, [cv], [sqc[ct % 2]])
            k.mm(p1[:], ones_r[:], cvr[ct % 2][:], ct == 0, ct == 7, [ones_r, cvr[ct % 2]], [p1])
            k.mm(p2[:], ones_r[:], sqc[ct % 2][:], ct == 0, ct == 7, [ones_r, sqc[ct % 2]], [p2])
        k.act(mean[:], p1[:], AF.Copy, [p1], [mean], scale=1.0 / 1024)
        k.tt(m2[:], mean[:], mean[:], ALU.mult, [mean], [m2])
        k.stt(var[:], p2[:], 1.0 / 1024, m2[:], ALU.mult, ALU.subtract, [p2, m2], [var])
        k.act(var[:], var[:], AF.Sqrt, [var], [var], bias=EPS)
        k.op("dve", lambda e: e.reciprocal(var[:], var[:]), reads=[var], writes=[var])
        for ct in range(8):
            k.tt(cv[:, ct, tsl], cv[:, ct, tsl], mean[:], ALU.subtract, [cv, mean], [cv])
            k.tt(cv[:, ct, tsl], cv[:, ct, tsl], var[:], ALU.mult, [cv, var], [cv])
            k.act(cv[:, ct, tsl], cv[:, ct, tsl], AF.Silu, [cv, vT], [cv],
                  scale=vT[:, V_LNG + ct:V_LNG + ct + 1], bias=vT[:, V_LNB + ct:V_LNB + ct + 1])
        p3 = ps[2]
        for ct in range(8):
            k.act(sqc[ct % 2][:], cv[:, ct, tsl], AF.Square, [cv], [sqc[ct % 2]])
            k.mm(p3[:], ones_r[:], sqc[ct % 2][:], ct == 0, ct == 7, [ones_r, sqc[ct % 2]], [p3])
        k.act(rsd2[:], p3[:], AF.Sqrt, [p3], [rsd2], scale=1.0 / 1024, bias=EPS)
        k.op("dve", lambda e: e.reciprocal(rsd2[:], rsd2[:]), reads=[rsd2], writes=[rsd2])
        for ct in range(8):
            k.stt(yn[:, ct, tsl], cv[:, ct, tsl], vT[:, V_GC + ct:V_GC + ct + 1], rsd2[:], ALU.mult, ALU.mult,
                  [cv, vT, rsd2], [yn])
    k.release(0)

    h = k.sbuf("h", [128, 8, D], F32)
    mW = k.mark()
    xres = [k.sbuf(f"xres{i}", [128, D], F32) for i in range(2)]
    d_xr = [k.dsem(f"xr{i}") for i in range(2)]
    for t in range(8):
        xr = xres[t % 2]
        k.dma("sp", xr[:], xs_d[TALL - TOWN + t * 128:TALL - TOWN + (t + 1) * 128, :], d_xr[t % 2], writes=[xr])
        for n in range(4):
            pb = ps[n % 2]
            for kt in range(KT):
                k.mm(pb[:], yn[:, kt, t * 128:(t + 1) * 128], wbuf[:, kt, n * 512:(n + 1) * 512], kt == 0, kt == KT - 1,
                     [yn, wbuf], [pb])
            k.tt(h[:, t, n * 512:(n + 1) * 512], pb[:], xr[:, n * 512:(n + 1) * 512], ALU.add, [pb, xr], [h])
    k.release(mW)
    k.release(mark_yn, "right")

    def dump_h(dst_list):
        for t in range(8):
            for dd in dst_list:
                k.dma("sp", dd[t * 128:(t + 1) * 128, :], h[:, t, :], d_out, reads=[h])

    if stage == "h1":
        dump_h([dbg_d, y_d])
        k.finish("sp")
        print("instructions", k.n_inst, "waits", k.n_wait)
        return nc

    mE = k.mark()
    KTb = k.sbuf("KTb", [128, 16, 256], BF16)
    Vb = k.sbuf("Vb", [128, 2, D], BF16)
    wch = [k.sbuf(f"wch{i}", [128, KT, 512], BF16) for i in range(2)]
    d_wc = [k.dsem(f"wc{i}") for i in range(2)]
    wcnt = {"n": 0}

    def load_wchunk(w_d, c0, g_off=None):
        i = wcnt["n"] % 2
        wcnt["n"] += 1
        for q4 in range(4):
            src = w_d[q4 * 512:(q4 + 1) * 512, c0:c0 + 512].rearrange("(kt p) c -> p kt c", p=128)
            k.dma("pool", wch[i][:, q4 * 4:(q4 + 1) * 4, :], src, d_wc[i], writes=[wch[i]])
        k.group_end(d_wc[i], [wch[i]])
        if g_off is not None:
            fold_g(wch[i], g_off, 512)
        return wch[i]

    mE1 = k.mark()
    alloc_norm_bufs()
    xtile = NB["xtile"]
    memT = k.sbuf("memT", [128, KT, 256], BF16)
    def mem_src(mt):
        def f():
            i = load_tok_tile(mem_d[mt * 128:(mt + 1) * 128, :])
            return xtile[i], xtile[i][:]
        return f
    norm_pipeline([(mem_d[mt * 128:(mt + 1) * 128, :], memT,
                    lambda half, mt=mt: memT.ap(half * 8 * 256 + mt * 128, [[256, 8], [1, 128]])) for mt in range(2)])
    for c in range(8):
        wc = load_wchunk(w_kv_d, c * 512, G_MEMN)
        if c < 4:
            for j in range(4):
                pb = ps[2 + j % 2]
                for kt in range(KT):
                    k.mm(pb[:, 0:256], wc[:, kt, j * 128:(j + 1) * 128], memT[:, kt, :], kt == 0, kt == KT - 1,
                         [wc, memT], [pb])
                k.act(KTb[:, c * 4 + j, :], pb[:, 0:256], AF.Copy, [pb], [KTb])
        else:
            for mt in range(2):
                pb = ps[2 + mt]
                for kt in range(KT):
                    k.mm(pb[:], memT[:, kt, mt * 128:(mt + 1) * 128], wc[:, kt, :], kt == 0, kt == KT - 1,
                         [wc, memT], [pb])
                k.act(Vb[:, mt, (c - 4) * 512:(c - 3) * 512], pb[:], AF.Copy, [pb], [Vb])
    k.release(mE1)
    qT = k.sbuf("qT", [128, 16, TOWN], BF16)
    hnT = k.sbuf("hnT", [128, 16, TOWN], BF16)
    mE2 = k.mark()
    alloc_norm_bufs(False)
    norm_pipeline([((h, h[:, t, :]), hnT,
                    lambda half, t=t: hnT.ap(half * 8 * TOWN + t * 128, [[TOWN, 8], [1, 128]])) for t in range(8)])
    k.release(mE2)
    qscale = 1.0 / math.sqrt(512.0)
    for c in range(4):
        wc = load_wchunk(w_q_d, c * 512, G_MEM)
        for j in range(4):
            for blk in range(2):
                pb = ps[2 + (2 * j + blk) % 2]
                for kt in range(KT):
                    k.mm(pb[:], wc[:, kt, j * 128:(j + 1) * 128], hnT[:, kt, blk * BLK:(blk + 1) * BLK], kt == 0,
                         kt == KT - 1, [wc, hnT], [pb])
                k.act(qT[:, c * 4 + j, blk * BLK:(blk + 1) * BLK], pb[:], AF.Copy, [pb], [qT], scale=qscale)
    PT = k.sbuf("PT", [128, 8, TOWN], BF16)
    pex = [k.sbuf(f"pex{i}", [128, 256], F32) for i in range(2)]
    pnb = [k.sbuf(f"pnb{i}", [128, 256], BF16) for i in range(2)]
    sst = [k.sbuf(f"sst{i}", [128, 4], F32) for i in range(2)]
    n_it = 0
    for hd in range(4):
        for t in range(8):
            i = n_it % 2
            n_it += 1
            pb = ps[2 + i]
            for j in range(4):
                k.mm(pb[:, 0:256], qT[:, hd * 4 + j, t * 128:(t + 1) * 128], KTb[:, hd * 4 + j, :], j == 0, j == 3,
                     [qT, KTb], [pb])
            st = sst[i]
            k.op("dve", lambda e, st=st, pb=pb: e.tensor_reduce(out=st[:, 0:1], in_=pb[:, 0:256], axis=AX.X, op=ALU.max),
                 reads=[pb], writes=[st])
            k.ts(st[:, 1:2], st[:, 0:1], -1.0, None, ALU.mult, None, [st], [st])
            k.act(pex[i][:], pb[:, 0:256], AF.Exp, [pb, st], [pex[i], st], bias=st[:, 1:2], accum_out=st[:, 2:3])
            k.op("dve", lambda e, st=st: e.reciprocal(st[:, 3:4], st[:, 2:3]), reads=[st], writes=[st])
            k.ts(pnb[i][:], pex[i][:], st[:, 3:4], None, ALU.mult, None, [pex[i], st], [pnb[i]])
            pt = ps[4 + i]
            ptv = pt[:].bitcast(BF16)
            for mt in range(2):
                k.tr(ptv[:, mt * 128:(mt + 1) * 128], pnb[i][:, mt * 128:(mt + 1) * 128], ident_b[:], [pnb[i], ident_b], [pt])
            k.copy(PT.ap(hd * 2 * TOWN + t * 128, [[TOWN, 2], [1, 128]]),
                   ptv[:, 0:256].rearrange("p (m t) -> p m t", m=2), [pt], [PT])
    OT = hnT
    for hd in range(4):
        for dvt in range(4):
            for blk in range(2):
                pb = ps[2 + (dvt * 2 + blk) % 2]
                for mt in range(2):
                    k.mm(pb[:], Vb[:, mt, (hd * 4 + dvt) * 128:(hd * 4 + dvt + 1) * 128],
                         PT[:, hd * 2 + mt, blk * BLK:(blk + 1) * BLK], mt == 0, mt == 1, [Vb, PT], [pb])
                k.act(OT[:, hd * 4 + dvt, blk * BLK:(blk + 1) * BLK], pb[:], AF.Copy, [pb], [OT])
    for n in range(4):
        wc = load_wchunk(w_o_d, n * 512)
        for t in range(8):
            pb = ps[2 + t % 2]
            for ct in range(KT):
                k.mm(pb[:], OT[:, ct, t * 128:(t + 1) * 128], wc[:, ct, :], ct == 0, ct == KT - 1, [OT, wc], [pb])
            k.tt(h[:, t, n * 512:(n + 1) * 512], h[:, t, n * 512:(n + 1) * 512], pb[:], ALU.add, [h, pb], [h])
    k.release(mE)

    if stage == "h2":
        dump_h([dbg_d, y_d])
        k.finish("sp")
        print("instructions", k.n_inst, "waits", k.n_wait)
        return nc

    idxT = k.sbuf("idxT", [128, TOWN], U32)
    gT = k.sbuf("gT", [128, TOWN], F32)
    hn3b = k.sbuf("hn3b", [128, 8, D], BF16)
    rstd3 = k.sbuf("rstd3", [128, 16], F32)
    mF0 = k.mark()
    gffn_b = k.sbuf("gffn_b", [128, D], F32)
    k.dma("sp", gffn_b[:], bass.AP(tensor=vecs_d.tensor, offset=G_FFN * 128, ap=[[0, 128], [1, D]]), d_setup,
          writes=[gffn_b])
    k.group_end(d_setup, [gffn_b])
    xs0 = k.sbuf("xs0", [128, D], F32)
    for t in range(8):
        k.act(xs0[:], h[:, t, :], AF.Square, [h], [xs0, rstd3], accum_out=rstd3[:, 8 + t:9 + t])
        pool_rsqrt(rstd3, rstd3[:, t:t + 1], rstd3[:, 8 + t:9 + t], 1.0 / D, [128, 1])
        k.stt(hn3b[:, t, :], h[:, t, :], rstd3[:, t:t + 1], gffn_b[:], ALU.mult, ALU.mult, [h, rstd3, gffn_b], [hn3b])
    k.release(mF0)
    mF = k.mark()
    import os
    RL = float(os.environ.get("RL", "99")) if stage == "route" else 99

    def route_body():
        idxf = k.sbuf("idxf", [128, 8, 128], F32)
        gtf = k.sbuf("gtf", [128, 8, 128], F32)
        keysT = k.sbuf("keysT", [128, 8, 128], F32)
        kraw = k.sbuf("kraw", [128, 128], F32)
        d_kr = k.dsem("kr")
        for hq in range(8):
            k.dma("sp", kraw[:].rearrange("n (s c) -> n s c", s=2), pkeys_d[hq].rearrange("s n c -> n s c"), d_kr,
                  writes=[kraw])
            k.tr(ps[0][:, 0:128], kraw[:], ident_f[:], [kraw, ident_f], [ps[0]])
            k.copy(keysT[:, hq, :], ps[0][:, 0:128], [ps[0]], [keysT])
        if RL <= 1:
            return
        iota16 = k.sbuf("iota16", [128, 16], F32)
        k.op("pool", lambda e: e.iota(iota16[:], pattern=[[1, 16]], base=0, channel_multiplier=0,
                                      allow_small_or_imprecise_dtypes=True), writes=[iota16])
        pwq = k.sbuf("pwq", [128, KT, 512], F32R)
        wsts = [k.sbuf(f"wst{i}", [128, 512], F32) for i in range(2)]
        d_wss = [k.dsem(f"ws{i}") for i in range(2)]
        xs32 = k.sbuf("xs32", [128, D], F32)
        hnT3s = [k.sbuf(f"hnT3{i}", [128, KT, 128], F32R) for i in range(2)]
        qTts = [k.sbuf(f"qTt{i}", [128, 4, 128], F32) for i in range(2)]
        scss = [k.sbuf(f"scs{i}", [128, 8, 128], F32) for i in range(2)]
        NCH = 4
        SC = []
        for c_ in range(NCH):
            SC.append(dict(
                scw=k.sbuf(f"scw{c_}", [128, 256], F32), cand=k.sbuf(f"cand{c_}", [128, 256], F32),
                s1v=k.sbuf(f"s1v{c_}", [128, 2, 16], F32), s1i=k.sbuf(f"s1i{c_}", [128, 2, 16], F32),
                iu=k.sbuf(f"iu{c_}", [128, 16], U32), tv=k.sbuf(f"tv{c_}", [128, 16], F32),
                posf=k.sbuf(f"posf{c_}", [128, 16], F32), pai=k.sbuf(f"pai{c_}", [128, 16], I32),
                paf=k.sbuf(f"paf{c_}", [128, 16], F32), pbf=k.sbuf(f"pbf{c_}", [128, 16], F32),
                eq=k.sbuf(f"eq{c_}", [128, 16, 16], F32), sel=k.sbuf(f"sel{c_}", [128, 2, 16], F32),
                rst=k.sbuf(f"rst{c_}", [128, 8], F32), ex=k.sbuf(f"ex{c_}", [128, 16], F32)))

        def top16(B, vals_buf, vals_ap, src_buf, src_ap, idx_buf, idx_ap, n):
            scw, iu = B["scw"], B["iu"]
            k.op("dve", lambda e: e.max(out=vals_ap[:, 0:8], in_=src_ap), reads=[src_buf], writes=[vals_buf])
            yield
            k.op("dve", lambda e: e.match_replace(out=scw[:, 0:n], in_to_replace=vals_ap[:, 0:8], in_values=src_ap,
                                                  imm_value=-1e30), reads=[src_buf, vals_buf], writes=[scw])
            yield
            k.op("dve", lambda e: e.max(out=vals_ap[:, 8:16], in_=scw[:, 0:n]), reads=[scw], writes=[vals_buf])
            yield
            k.op("dve", lambda e: e.max_index(out=iu[:, 0:8], in_max=vals_ap[:, 0:8], in_values=src_ap),
                 reads=[src_buf, vals_buf], writes=[iu])
            yield
            k.op("dve", lambda e: e.max_index(out=iu[:, 8:16], in_max=vals_ap[:, 8:16], in_values=src_ap),
                 reads=[src_buf, vals_buf], writes=[iu])
            yield
            k.copy(idx_ap, iu[:], [iu], [idx_buf])
            yield

        def head_chain(B, scs, hh, hq, t):
            s1v, s1i, cand, tv, posf = B["s1v"], B["s1i"], B["cand"], B["tv"], B["posf"]
            rst, ex, paf, pai, pbf, eq, sel = B["rst"], B["ex"], B["paf"], B["pai"], B["pbf"], B["eq"], B["sel"]
            for sd in range(2):
                yield from top16(B, s1v, s1v[:, sd, :], scs, scs[:, hh * 2 + sd, :], s1i, s1i[:, sd, :], 128)
            k.tt(cand[:].rearrange("p (a b) -> p a b", a=16), s1v.ap(0, [[1, 16], [0, 16]]),
                 s1v.ap(16, [[0, 16], [1, 16]]), ALU.add, [s1v], [cand])
            yield
            yield from top16(B, tv, tv[:], cand, cand[:], posf, posf[:], 256)
            k.ts(rst[:, 2:3], tv[:, 0:1], -1.0, None, ALU.mult, None, [tv], [rst])
            yield
            k.act(ex[:], tv[:], AF.Exp, [tv, rst], [ex, rst], bias=rst[:, 2:3], accum_out=rst[:, 3:4])
            yield
            k.op("dve", lambda e: e.reciprocal(rst[:, 4:5], rst[:, 3:4]), reads=[rst], writes=[rst])
            yield
            k.ts(gtf[:, t, hq * 16:(hq + 1) * 16], ex[:], rst[:, 4:5], None, ALU.mult, None, [ex, rst], [gtfB[hq]])
            yield
            k.ts(paf[:], posf[:], 0.0625, -0.46875, ALU.mult, ALU.add, [posf], [paf])
            yield
            k.copy(pai[:], paf[:], [paf], [pai])
            yield
            k.copy(paf[:], pai[:], [pai], [paf])
            yield
            k.stt(pbf[:], paf[:], -16.0, posf[:], ALU.mult, ALU.add, [paf, posf], [pbf])
            yield
            for sd, pos_b in ((0, paf), (1, pbf)):
                k.tt(eq[:], pos_b.ap(0, [[1, 16], [0, 16]]), iota16.ap(0, [[0, 16], [1, 16]]), ALU.is_equal,
                     [pos_b, iota16], [eq])
                yield
                k.tt(eq[:], eq[:], s1i.ap(sd * 16, [[0, 16], [1, 16]]), ALU.mult, [eq, s1i], [eq])
                yield
                k.op("dve", lambda e, sd=sd: e.tensor_reduce(out=sel[:, sd, :], in_=eq[:], axis=AX.X, op=ALU.add),
                     reads=[eq], writes=[sel])
                yield
            k.stt(idxf[:, t, hq * 16:(hq + 1) * 16], sel[:, 0, :], 128.0, sel[:, 1, :], ALU.mult, ALU.add,
                  [sel], [idxfB[hq]])
            yield

        idxfB = [Buf(f"idxf_h{i}") for i in range(8)]
        gtfB = [Buf(f"gtf_h{i}") for i in range(8)]

        for half in range(2):
            for kt in range(KT):
                wst, d_ws = wsts[kt % 2], d_wss[kt % 2]
                k.dma("sp", wst[:], pwq_d[kt * 128:(kt + 1) * 128, half * 512:(half + 1) * 512], d_ws, writes=[wst])
                k.act(pwq[:, kt, :], wst[:], AF.Copy, [wst, vT], [pwq], scale=vT[:, G_FFN + kt:G_FFN + kt + 1])
            if RL <= 1.4:
                return
            def front(t):
                hnT3, qTt, scs = hnT3s[t % 2], qTts[t % 2], scss[t % 2]
                k.act(xs32[:], h[:, t, :], AF.Copy, [h, rstd3], [xs32], scale=rstd3[:, t:t + 1])
                for q4 in range(4):
                    pb = ps[q4]
                    for j in range(4):
                        kt = q4 * 4 + j
                        k.tr(pb[:, j * 128:(j + 1) * 128], xs32[:, kt * 128:(kt + 1) * 128], ident_f[:], [xs32, ident_f], [pb])
                    k.act(hnT3[:, q4 * 4:(q4 + 1) * 4, :], pb[:].rearrange("p (j t) -> p j t", j=4), AF.Copy, [pb], [hnT3])
                for hh in range(4):
                    pb = ps[4 + hh % 2]
                    for kt in range(KT):
                        k.mm(pb[:, 0:128], pwq[:, kt, hh * 128:(hh + 1) * 128], hnT3[:, kt, :], kt == 0, kt == KT - 1,
                             [pwq, hnT3], [pb])
                    k.act(qTt[:, hh, :], pb[:, 0:128], AF.Copy, [pb], [qTt])
                for hh in range(4):
                    hq = half * 4 + hh
                    for sd in range(2):
                        pb = ps[6 + sd]
                        k.mm(pb[:, hh * 128:(hh + 1) * 128], qTt[sd * 64:(sd + 1) * 64, hh, :],
                             keysT[sd * 64:(sd + 1) * 64, hq, :], True, True, [qTt, keysT], [pb])
                for sd in range(2):
                    k.act(scs.ap(sd * 128, [[256, 4], [1, 128]]), ps[6 + sd][:].rearrange("p (h n) -> p h n", h=4),
                          AF.Copy, [ps[6 + sd]], [scs])

            front(0)
            for t in range(8):
                if t + 1 < 8:
                    front(t + 1)
                scs = scss[t % 2]
                chains = [head_chain(SC[hh], scs, hh, half * 4 + hh, t) for hh in range(4)]
                live = list(chains)
                while live:
                    for g_ in list(live):
                        try:
                            next(g_)
                        except StopIteration:
                            live.remove(g_)
        if RL <= 6:
            return
        for t in range(8):
            k.tr(ps[0][:, 0:128], idxf[:, t, :], ident_f[:], [idxf, ident_f] + idxfB, [ps[0]])
            k.copy(idxT[:, t * 128:(t + 1) * 128], ps[0][:, 0:128], [ps[0]], [idxT])
            k.tr(ps[1][:, 0:128], gtf[:, t, :], ident_f[:], [gtf, ident_f] + gtfB, [ps[1]])
            k.copy(gT[:, t * 128:(t + 1) * 128], ps[1][:, 0:128], [ps[1]], [gT])

    route_body()
    k.release(mF)

    if stage == "route":
        obuf = k.sbuf("obuf", [128, D], F32)
        for t in range(8):
            k.memset(obuf[:], 0.0, [obuf])
            k.tr(ps[0][:, 0:128], gT[:, t * 128:(t + 1) * 128], ident_f[:], [gT, ident_f], [ps[0]])
            k.copy(obuf[:, 0:128], ps[0][:, 0:128], [ps[0]], [obuf])
            k.copy(gT[:, t * 128:(t + 1) * 128], idxT[:, t * 128:(t + 1) * 128], [idxT, ps[0]], [gT])
            k.tr(ps[1][:, 0:128], gT[:, t * 128:(t + 1) * 128], ident_f[:], [gT, ident_f], [ps[1]])
            k.copy(obuf[:, 128:256], ps[1][:, 0:128], [ps[1]], [obuf])
            for dd in (dbg_d, y_d):
                k.dma("sp", dd[t * 128:(t + 1) * 128, :], obuf[:], d_out, reads=[obuf])
        k.finish("sp")
        print("instructions", k.n_inst, "waits", k.n_wait)
        return nc

    mF3 = k.mark()
    Ebig = k.sbuf("Ebig", [128, 256], F32)
    k.memset(Ebig[:], 0.0, [Ebig])
    k.memset(Ebig[:, 127:128], 1.0, [Ebig])
    fng_b = k.sbuf("fng_b", [128, D], F32)
    k.dma("sp", fng_b[:], bass.AP(tensor=fng_d.tensor, offset=0, ap=[[0, 128], [1, D]]), d_setup, writes=[fng_b])
    k.group_end(d_setup, [fng_b])
    NG = 9
    UV = [k.sbuf(f"UV{i}", [128, 2 * D], BF16) for i in range(NG)]
    d_u = [k.dsem(f"u{i}") for i in range(NG)]
    lw = [k.sbuf(f"lw{i}", [128, 128], BF16) for i in range(4)]
    accs = [k.sbuf(f"accs{i}", [128, 8], F32) for i in range(2)]
    gact = [k.sbuf(f"gact{i}", [128, 1], F32) for i in range(2)]
    junkd = k.sbuf("junkd", [128, 1024], BF16)
    obuf = k.sbuf("obuf", [128, D], F32)
    fst = k.sbuf("fst", [128, 4], F32)
    d_o = k.dsem("o")
    NTOK = TOWN

    def issue_gather(tg):
        i = tg % NG
        k.dma_custom("pool", lambda e: e.indirect_dma_start(
            out=UV[i][:], out_offset=None, in_=puv_bf,
            in_offset=bass.IndirectOffsetOnAxis(ap=idxT[:, tg:tg + 1], axis=0)), d_u[i], reads=[idxT, tblU], writes=[UV[i]])

    def issue_bcast(tg):
        t, tl = tg // 128, tg % 128
        for n in range(4):
            k.mm(ps[n][:], ident_b[:, tl:tl + 1].to_broadcast([128, 128]), hn3b[:, t, n * 512:(n + 1) * 512], True, True,
                 [ident_b, hn3b], [ps[n]])

    def issue_dots(tg):
        i, ac = tg % NG, accs[tg % 2]
        for n in range(2):
            k.op("dve", lambda e, n=n: e.scalar_tensor_tensor(
                out=junkd[:], in0=UV[i][:, n * 1024:(n + 1) * 1024], scalar=1.0, in1=psd[n][:],
                op0=ALU.mult, op1=ALU.mult, accum_out=ac[:, n:n + 1]), reads=[UV[i], ps[2 * n], ps[2 * n + 1]],
                writes=[junkd, ac])
        k.act(gact[tg % 2][:], ac[:, 0:1], AF.Gelu_apprx_tanh, [ac], [gact[tg % 2]], bias=ac[:, 1:2])

    def issue_combine(tg):
        t, tl = tg // 128, tg % 128
        i, j = tg % NG, tg % 2
        lwj = lw[tg % 4]
        k.ts(lwj[:], Ebig[:, 127 - tl:255 - tl], gact[j][:, 0:1], gT[:, tg:tg + 1], ALU.mult, ALU.mult,
             [Ebig, gact[j], gT], [lwj], q="pool")
        for n in range(4):
            k.mm(ps[4 + n][:], lwj[:], UV[i][:, D + n * 512:D + (n + 1) * 512], tl == 0, tl == 127, [lwj, UV[i]],
                 [ps[4 + n]])
        if tl == 127:
            for n in range(4):
                k.tt(h[:, t, n * 512:(n + 1) * 512], h[:, t, n * 512:(n + 1) * 512], ps[4 + n][:], ALU.add,
                     [h, ps[4 + n]], [h])
            k.act(obuf[:], h[:, t, :], AF.Square, [h], [obuf, fst], accum_out=fst[:, 0:1])
            pool_rsqrt(fst, fst[:, 1:2], fst[:, 0:1], 1.0 / D, [128, 1])
            k.stt(obuf[:], h[:, t, :], fst[:, 1:2], fng_b[:], ALU.mult, ALU.mult, [h, fst, fng_b], [obuf])
            k.dma("sp", y_d[t * 128:(t + 1) * 128, :], obuf[:], d_o, reads=[obuf])
            if dbg_d is not None:
                k.dma("sp", dbg_d[t * 128:(t + 1) * 128, :], obuf[:], d_o, reads=[obuf])

    for tg in range(min(NG - 1, NTOK)):
        issue_gather(tg)
    issue_bcast(0)
    issue_dots(0)
    issue_bcast(1)
    for tg in range(NTOK):
        if tg + NG - 1 < NTOK:
            issue_gather(tg + NG - 1)
        if tg + 1 < NTOK:
            issue_dots(tg + 1)
        if tg + 2 < NTOK:
            issue_bcast(tg + 2)
        issue_combine(tg)
    k.finish("sp")
    print("instructions", k.n_inst, "waits", k.n_wait)
    return nc


def make_in_maps(inp):
    f = lambda a: np.ascontiguousarray(np.asarray(a, dtype=np.float32))
    x = f(inp["x"])
    mem = f(inp["mem"])
    vec_names = ["norm_mix_g", "norm_mem_g", "mem_norm_g", "norm_ffn_g"]
    rows = [f(inp[n]).reshape(16, 128) for n in vec_names]
    for n in ["conv_b", "conv_ln_g", "conv_ln_b", "grp_norm_conv_g", "grp_norm_ssm_g", "ssm_d"]:
        rows.append(f(inp[n]).reshape(8, 128))
    vecs = np.ascontiguousarray(np.concatenate(rows, axis=0))
    shared = {
        "vecs": vecs,
        "conv_w": f(inp["conv_w"]).reshape(31, 1024),
        "lam_re": f(inp["ssm_lam_re"]).reshape(32, 128),
        "lam_im": f(inp["ssm_lam_im"]).reshape(32, 128),
        "log_dt": f(inp["ssm_log_dt"]).reshape(32, 2),
        "b_re": f(inp["ssm_b_re"]).reshape(32, 2048),
        "b_im": f(inp["ssm_b_im"]).reshape(32, 2048),
        "c_re": f(inp["ssm_c_re"]).reshape(32, 2048),
        "c_im": f(inp["ssm_c_im"]).reshape(32, 2048),
        "w_in": f(inp["w_in"]).reshape(D, 3072),
        "glu_w": f(inp["ssm_glu_w"]).reshape(1024, 1024),
        "w_out": f(inp["w_out"]).reshape(D, D),
        "w_q": f(inp["w_q_mem"]).reshape(D, D),
        "w_kv": f(inp["w_kv_mem"]).reshape(D, 2 * D),
        "w_o": f(inp["w_o_mem"]).reshape(D, D),
        "peer_w_q": f(inp["peer_w_q"]).reshape(D, 1024),
        "peer_keys": f(inp["peer_keys"]).reshape(8, 2, 128, 64),
        "peer_u": f(inp["peer_u"]).reshape(16384, D),
        "peer_v": f(inp["peer_v"]).reshape(16384, D),
        "final_g": f(inp["final_norm_g"]).reshape(1, D),
    }
    maps = []
    for c in range(NCORES):
        b, seg = c // 4, c % 4
        xs = np.zeros((TALL, D), np.float32)
        n_have = (seg + 1) * TOWN
        xs[TALL - n_have:] = x[b, :n_have]
        m = dict(shared)
        m["xs"] = xs
        m["mem"] = np.ascontiguousarray(mem[b])
        maps.append(m)
    return maps


def kernel(**inputs):
    nc = build_program("full")
    in_maps = make_in_maps(inputs)
    res = run_bass_kernel_spmd(nc, in_maps, core_ids=list(range(NCORES)))
    out = np.zeros((2, 4096, D), np.float32)
    for c in range(NCORES):
        b, seg = c // 4, c % 4
        out[b, seg * TOWN:(seg + 1) * TOWN] = res.results[c]["y"]
    return out
```

```python
import math
import numpy as np
import concourse.bass as bass
import concourse.mybir as mybir
from concourse.bass_utils import run_bass_kernel_spmd

F32 = mybir.dt.float32
F32R = mybir.dt.float32r
BF16 = mybir.dt.bfloat16
U32 = mybir.dt.uint32
I32 = mybir.dt.int32
ALU = mybir.AluOpType
AF = mybir.ActivationFunctionType
AX = mybir.AxisListType

NCORES = 8
D = 2048
KT = 16
TOWN = 1024
TALL = 4096
BLK = 512
NBLK = TALL // BLK
EPS = 1e-6
TWO_PI_S = 6.28318


class Sem:
    def __init__(self, nc, name, step):
        self.h = nc.semaphore(name).__enter__()
        self.name = name
        self.cnt = 0
        self.step = step


class Buf:
    def __init__(self, name, t=None):
        self.name = name
        self.t = t
        self.last_w = None
        self.readers = {}

    def __getitem__(self, k):
        return self.t[k]

    def ap(self, offset, dims):
        full = self.t[:]
        pstep = full.ap[0][0]
        return bass.AP(tensor=self.t, offset=offset, ap=[[pstep, full.ap[0][1]]] + [list(d) for d in dims])

    def app(self, p0, pn, offset, dims):
        full = self.t[:]
        pstep = full.ap[0][0]
        return bass.AP(tensor=self.t, offset=p0 * pstep + offset, ap=[[pstep, pn]] + [list(d) for d in dims])


class K:
    def __init__(self, nc):
        self.nc = nc
        self.eng = {"pe": nc.tensor, "act": nc.scalar, "dve": nc.vector,
                    "pool": nc.gpsimd, "sp": nc.sync}
        self.qsem = {q: Sem(nc, "q_" + q, 1) for q in self.eng}
        self.seen = {q: {} for q in self.eng}
        self.n_inst = 0
        self.n_wait = 0
        self.dsems = []
        self._stack = []
        self._rstack = []
        self.fence = {}

    def sbuf(self, name, shape, dt=F32, side="left"):
        self._uid = getattr(self, "_uid", 0) + 1
        cm = self.nc.sbuf_tensor(f"sb{self._uid}_{name}", list(shape), dt, side=side)
        t = cm.__enter__()
        b = Buf(name, t)
        b.readers = dict(self.fence)
        (self._stack if side == "left" else self._rstack).append((cm, b))
        return b

    def psum(self, name, shape, dt=F32):
        cm = self.nc.psum_tensor(name, list(shape), dt)
        t = cm.__enter__()
        b = Buf(name, t)
        self._stack.append((cm, b))
        return b

    def mark(self, side="left"):
        return len(self._stack if side == "left" else self._rstack)

    def release(self, mark, side="left"):
        st = self._stack if side == "left" else self._rstack
        while len(st) > mark:
            cm, b = st.pop()
            deps = list(b.readers.items()) + ([b.last_w] if b.last_w else [])
            for sm, tick in deps:
                if self.fence.get(sm, 0) < tick:
                    self.fence[sm] = tick
            cm.__exit__(None, None, None)

    def dsem(self, name):
        s = Sem(self.nc, "d_" + name, 16)
        self.dsems.append(s)
        return s

    def _waits(self, q, reads, writes, skip=None):
        need = {}
        myq = self.qsem[q]

        def add(dep, same_ok):
            if dep is None:
                return
            s, tick = dep
            if s is skip:
                return
            if s is myq and not same_ok:
                return
            if need.get(s, 0) < tick:
                need[s] = tick

        same = q != "pe"
        for b in reads:
            add(b.last_w, same)
        for b in writes:
            add(b.last_w, same)
            for s, tick in b.readers.items():
                add((s, tick), same)
        eng = self.eng[q]
        seen = self.seen[q]
        for s, tick in need.items():
            if seen.get(s, 0) >= tick:
                continue
            eng.wait_ge(s.h, tick)
            self.n_wait += 1
            seen[s] = tick

    def _record(self, sem, reads, writes):
        tick = sem.cnt
        for b in reads:
            if b.readers.get(sem, 0) < tick:
                b.readers[sem] = tick
        for b in writes:
            b.last_w = (sem, tick)
            b.readers = {}

    def op(self, q, fn, reads=(), writes=()):
        self._waits(q, reads, writes)
        ins = fn(self.eng[q])
        s = self.qsem[q]
        s.cnt += 1
        ins.then_inc(s.h, 1)
        self.n_inst += 1
        self._record(s, reads, writes)
        return ins

    def dma(self, q, out, in_, dsem, reads=(), writes=(), **kw):
        self._waits(q, reads, writes, skip=dsem)
        ins = self.eng[q].dma_start(out=out, in_=in_, **kw)
        dsem.cnt += 16
        ins.then_inc(dsem.h, 16)
        self.n_inst += 1
        self._record(dsem, reads, writes)
        return ins

    def dma_custom(self, q, fn, dsem, reads=(), writes=()):
        self._waits(q, reads, writes)
        ins = fn(self.eng[q])
        dsem.cnt += 16
        ins.then_inc(dsem.h, 16)
        self.n_inst += 1
        self._record(dsem, reads, writes)
        return ins

    def group_end(self, dsem, bufs):
        for b in bufs:
            b.last_w = (dsem, dsem.cnt)

    def finish(self, q="sp"):
        eng = self.eng[q]
        for s in list(self.qsem.values()) + self.dsems:
            if s.cnt > 0 and s is not self.qsem[q]:
                eng.wait_ge(s.h, s.cnt)

    def mm(self, out, lhsT, rhs, start, stop, reads, writes):
        return self.op("pe", lambda e: e.matmul(out, lhsT=lhsT, rhs=rhs, start=start, stop=stop),
                       reads=reads, writes=writes)

    def tr(self, out, in_, ident, reads, writes):
        return self.op("pe", lambda e: e.transpose(out, in_, ident), reads=reads, writes=writes)

    def act(self, out, in_, func, reads, writes, **kw):
        return self.op("act", lambda e: e.activation(out=out, in_=in_, func=func, **kw), reads=reads, writes=writes)

    def tt(self, out, in0, in1, op, reads, writes, q="dve"):
        return self.op(q, lambda e: e.tensor_tensor(out=out, in0=in0, in1=in1, op=op), reads=reads, writes=writes)

    def ts(self, out, in0, s1, s2, op0, op1, reads, writes, q="dve"):
        if s2 is None:
            return self.op(q, lambda e: e.tensor_scalar(out, in0, s1, None, op0=op0), reads=reads, writes=writes)
        return self.op(q, lambda e: e.tensor_scalar(out, in0, s1, s2, op0=op0, op1=op1), reads=reads, writes=writes)

    def stt(self, out, in0, scalar, in1, op0, op1, reads, writes):
        return self.op("dve", lambda e: e.scalar_tensor_tensor(out=out, in0=in0, scalar=scalar, in1=in1,
                                                               op0=op0, op1=op1), reads=reads, writes=writes)

    def copy(self, out, in_, reads, writes, q="dve"):
        return self.op(q, lambda e: e.tensor_copy(out, in_), reads=reads, writes=writes)

    def memset(self, ap, val, writes, q="dve"):
        return self.op(q, lambda e: e.memset(ap, val), writes=writes)


def build_program(stage="full"):
    nc = bass.Bass("TRN2", target_bir_lowering=False)

    def din(name, shape, dt=F32):
        return nc.dram_tensor(name, list(shape), dt, kind="ExternalInput").ap()

    xs_d = din("xs", [TALL, D])
    mem_d = din("mem", [256, D])
    vecs_d = din("vecs", [112, 128])
    convw_d = din("conv_w", [31, 1024])
    lre_d = din("lam_re", [32, 128])
    lim_d = din("lam_im", [32, 128])
    ldt_d = din("log_dt", [32, 2])
    bre_d = din("b_re", [32, 2048])
    bim_d = din("b_im", [32, 2048])
    cre_d = din("c_re", [32, 2048])
    cim_d = din("c_im", [32, 2048])
    w_in_d = din("w_in", [D, 3072])
    glu_d = din("glu_w", [1024, 1024])
    w_out_d = din("w_out", [D, D])
    w_q_d = din("w_q", [D, D])
    w_kv_d = din("w_kv", [D, 2 * D])
    w_o_d = din("w_o", [D, D])
    pwq_d = din("peer_w_q", [D, 1024])
    pkeys_d = din("peer_keys", [8, 2, 128, 64])
    pu_d = din("peer_u", [16384, D])
    pv_d = din("peer_v", [16384, D])
    fng_d = din("final_g", [1, D])
    y_d = nc.dram_tensor("y", [TOWN, D], F32, kind="ExternalOutput").ap()
    dbg_d = None
    if stage != "full":
        dbg_d = nc.dram_tensor("dbg", [TOWN, D], F32, kind="ExternalOutput").ap()

    puv_bf = nc.dram_tensor("puv_bf", [16384, 2 * D], BF16, kind="Internal").ap()
    tblU = Buf("tblUV")

    k = K(nc)

    ident_f = k.sbuf("ident_f", [128, 128], F32, side="right")
    ident_b = k.sbuf("ident_b", [128, 128], BF16, side="right")
    ones_r = k.sbuf("ones_r", [128, 128], F32R, side="right")
    iot = k.sbuf("iot", [128, 128], F32, side="right")
    vT = k.sbuf("vT", [128, 112], F32, side="right")
    pidx = k.sbuf("pidx", [128, 1], F32, side="right")
    negh = k.sbuf("negh", [128, 1], F32, side="right")
    psd = [k.psum(f"psd{i}", [128, 1024], F32) for i in range(4)]
    ps = [Buf(f"ps{i}", psd[i // 2][:, (i % 2) * 512:(i % 2 + 1) * 512]) for i in range(8)]
    d_setup = k.dsem("setup")
    d_out = k.dsem("out")

    k.op("pool", lambda e: e.iota(iot[:], pattern=[[1, 128]], base=0, channel_multiplier=-1,
                                  allow_small_or_imprecise_dtypes=True), writes=[iot])
    k.ts(ident_f[:], iot[:], 0.0, None, ALU.is_equal, None, [iot], [ident_f])
    k.copy(ident_b[:], ident_f[:], [ident_f], [ident_b])
    k.memset(iot[:], 1.0, [iot])
    k.copy(ones_r[:], iot[:], [iot], [ones_r])
    k.op("pool", lambda e: e.iota(pidx[:], pattern=[[0, 1]], base=0, channel_multiplier=1,
                                  allow_small_or_imprecise_dtypes=True), writes=[pidx])

    k.memset(negh[:], -0.5, [negh])
    m0 = k.mark()
    vraw = k.sbuf("vraw", [112, 128], F32)
    k.dma("sp", vraw[:], vecs_d, d_setup, writes=[vraw])
    k.tr(ps[0][:, 0:112], vraw[:], ident_f[0:112, 0:112], [vraw, ident_f], [ps[0]])
    k.copy(vT[:], ps[0][:, 0:112], [ps[0]], [vT])
    k.release(m0)
    G_MIX, G_MEM, G_MEMN, G_FFN = 0, 16, 32, 48
    V_CB, V_LNG, V_LNB, V_GC, V_GS, V_SD = 64, 72, 80, 88, 96, 104

    NB = {}
    NXB = 4
    d_x = [k.dsem(f"x{i}") for i in range(NXB)]
    cnt = {"x": 0}

    def alloc_norm_bufs(with_x=True):
        if with_x:
            NB["xtile"] = [k.sbuf(f"xtile{i}", [128, D], F32) for i in range(NXB)]
        NB["xsb"] = [k.sbuf(f"xsb{i}", [128, D], BF16) for i in range(2)]
        NB["junk"] = k.sbuf("junk", [128, D], BF16)
        NB["stat"] = [k.sbuf(f"stat{i}", [128, 4], F32) for i in range(2)]

    def load_tok_tile(src_ap):
        i = cnt["x"] % 2
        cnt["x"] += 1
        k.dma("sp", NB["xtile"][i][:], src_ap, d_x[i], writes=[NB["xtile"][i]])
        return i

    def pool_rsqrt(buf, out_ap, in_ap, scale, shape):
        k.ts(out_ap, in_ap, scale, EPS, ALU.mult, ALU.add, [buf], [buf], q="pool")
        nh = negh[:, 0:1] if shape[1] == 1 else negh[:, 0:1].to_broadcast(list(shape))
        k.tt(out_ap, out_ap, nh, ALU.pow, [buf, negh], [buf], q="pool")

    def rstd_of(src_buf, src_ap, st, n):
        junk = NB["junk"]
        k.act(junk[:, 0:n], src_ap, AF.Square, [src_buf], [junk, st], accum_out=st[:, 0:1])
        pool_rsqrt(st, st[:, 1:2], st[:, 0:1], 1.0 / n, [128, 1])

    def fold_g(wb, g_off, ncols):
        for kt in range(KT):
            k.ts(wb[:, kt, 0:ncols], wb[:, kt, 0:ncols], vT[:, g_off + kt:g_off + kt + 1], None, ALU.mult, None,
                 [wb, vT], [wb])

    def norm_A(src_buf, src_ap, i):
        st = NB["stat"][i]
        rstd_of(src_buf, src_ap, st, D)
        k.ts(NB["xsb"][i][:], src_ap, st[:, 1:2], None, ALU.mult, None, [src_buf, st], [NB["xsb"][i]])

    def norm_B(dstT_buf, dst_ap_fn, pbank, i):
        xsb = NB["xsb"]
        for half in range(2):
            pb = ps[pbank + half]
            pv = pb[:].bitcast(BF16)
            for j in range(8):
                kt = half * 8 + j
                k.tr(pv[:, j * 128:(j + 1) * 128], xsb[i][:, kt * 128:(kt + 1) * 128], ident_b[:],
                     [xsb[i], ident_b], [pb])
            pin = pv.rearrange("p (j t) -> p j t", j=8)
            if half == 0:
                k.act(dst_ap_fn(half), pin, AF.Copy, [pb], [dstT_buf])
            else:
                k.copy(dst_ap_fn(half), pin, [pb], [dstT_buf])

    def norm_pipeline(items, post=None):
        n = len(items)

        def is_dram(t):
            return not isinstance(items[t][0], tuple)

        def issue_load(t):
            if is_dram(t):
                i = t % NXB
                k.dma("sp", NB["xtile"][i][:], items[t][0], d_x[i], writes=[NB["xtile"][i]])

        def doA(t):
            if is_dram(t):
                xb = NB["xtile"][t % NXB]
                norm_A(xb, xb[:], t % 2)
            else:
                norm_A(items[t][0][0], items[t][0][1], t % 2)
        for t in range(min(NXB - 1, n)):
            issue_load(t)
        doA(0)
        for t in range(n):
            if t + NXB - 1 < n:
                issue_load(t + NXB - 1)
            if t + 1 < n:
                doA(t + 1)
            norm_B(items[t][1], items[t][2], 0, t % 2)
            if post is not None:
                post(t)

    uT = k.sbuf("uT", [128, 8, TALL], BF16)
    mA = k.mark()
    alloc_norm_bufs()
    xtile = NB["xtile"]
    w_ssm = k.sbuf("w_ssm", [128, KT, 1024], BF16)
    d_w = k.dsem("w0")
    for kt in range(KT):
        k.dma("pool", w_ssm[:, kt, :], w_in_d[kt * 128:(kt + 1) * 128, 2048:3072], d_w, writes=[w_ssm])
    k.group_end(d_w, [w_ssm])
    fold_g(w_ssm, G_MIX, 1024)
    xnTs = [k.sbuf(f"xnT{i}", [128, KT, BLK], BF16) for i in range(2)]
    def a1_src(r0):
        def f():
            i = load_tok_tile(xs_d[r0:r0 + 128, :])
            return xtile[i], xtile[i][:]
        return f

    def a1_post(t):
        if t % 4 != 3:
            return
        b = t // 4
        xnT = xnTs[b % 2]
        for ct in range(8):
            pb = ps[2 + ct % 2]
            for kt in range(KT):
                k.mm(pb[:], w_ssm[:, kt, ct * 128:(ct + 1) * 128], xnT[:, kt, :], kt == 0, kt == KT - 1,
                     [w_ssm, xnT], [pb])
            k.act(uT[:, ct, b * BLK:(b + 1) * BLK], pb[:], AF.Copy, [pb], [uT])

    items = []
    for b in range(NBLK):
        for tl in range(4):
            xnT = xnTs[b % 2]
            r0 = b * BLK + tl * 128
            items.append((xs_d[r0:r0 + 128, :], xnT,
                          lambda half, tl=tl, xnT=xnT: xnT.ap(half * 8 * BLK + tl * 128, [[BLK, 8], [1, 128]])))
    norm_pipeline(items, a1_post)
    k.release(mA)

    mS = k.mark()
    s5w = {nm: k.sbuf(nm, [128, 32, 128], BF16) for nm in ("Bre_pad", "Bim_pad", "WA", "NWA", "NWB")}
    parT = k.sbuf("parT", [128, 18, 32], F32)
    jvec = k.sbuf("jvec", [128, BLK], F32)
    k.op("pool", lambda e: e.iota(jvec[:], pattern=[[1, BLK]], base=0, channel_multiplier=0,
                                  allow_small_or_imprecise_dtypes=True), writes=[jvec])
    jrev = k.sbuf("jrev", [128, BLK], F32)
    k.op("pool", lambda e: e.iota(jrev[:], pattern=[[-1, BLK]], base=BLK - 1, channel_multiplier=0,
                                  allow_small_or_imprecise_dtypes=True), writes=[jrev])
    mP = k.mark()
    la = {nm: k.sbuf("la_" + nm, [32, 128], F32) for nm in
          ("lre", "lim", "dtb", "ar", "th", "rho", "phi", "f", "fa", "sinv", "cosv", "lbr", "lbi", "nr", "den",
           "t1", "t2", "cr", "ci", "f512", "g512", "ga512")}
    lai = k.sbuf("la_i", [32, 128], I32)
    par = k.sbuf("par", [32, 18, 128], F32)
    ldt = k.sbuf("ldt", [32, 2], F32)
    dt2 = k.sbuf("dt2", [32, 2], F32)
    k.dma("sp", la["lre"][:], lre_d, d_setup, writes=[la["lre"]])
    k.dma("sp", la["lim"][:], lim_d, d_setup, writes=[la["lim"]])
    k.dma("sp", ldt[:], ldt_d, d_setup, writes=[ldt])
    k.group_end(d_setup, [la["lre"], la["lim"], ldt, vraw])
    k.act(dt2[:], ldt[:], AF.Exp, [ldt], [dt2])
    k.copy(la["dtb"].ap(0, [[64, 2], [1, 64]]), dt2.ap(0, [[1, 2], [0, 64]]), [dt2], [la["dtb"]])

    def L(n):
        return la[n][:]
    k.tt(L("ar"), L("lre"), L("dtb"), ALU.mult, [la["lre"], la["dtb"]], [la["ar"]])
    k.tt(L("th"), L("lim"), L("dtb"), ALU.mult, [la["lim"], la["dtb"]], [la["th"]])
    k.act(L("rho"), L("ar"), AF.Exp, [la["ar"]], [la["rho"]])
    k.ts(L("phi"), L("th"), 1.0 / (2 * math.pi), None, ALU.mult, None, [la["th"]], [la["phi"]])

    def frac(dst, src):
        k.copy(lai[:], la[src][:], [la[src]], [lai])
        k.tt(la[dst][:], la[src][:], lai[:], ALU.subtract, [la[src], lai], [la[dst]])

    def sincos(sin_dst, cos_dst, fsrc, tmp):
        k.act(la[sin_dst][:], la[fsrc][:], AF.Sin, [la[fsrc]], [la[sin_dst]], scale=TWO_PI_S)
        k.act(la[tmp][:], la[fsrc][:], AF.Abs, [la[fsrc]], [la[tmp]])
        k.act(la[cos_dst][:], la[tmp][:], AF.Sin, [la[tmp]], [la[cos_dst]], scale=-TWO_PI_S, bias=math.pi / 2)

    frac("f", "phi")
    sincos("sinv", "cosv", "f", "fa")
    k.tt(L("lbr"), L("rho"), L("cosv"), ALU.mult, [la["rho"], la["cosv"]], [la["lbr"]])
    k.tt(L("lbi"), L("rho"), L("sinv"), ALU.mult, [la["rho"], la["sinv"]], [la["lbi"]])
    k.ts(L("nr"), L("lbr"), -1.0, None, ALU.add, None, [la["lbr"]], [la["nr"]])
    k.tt(L("t1"), L("lre"), L("lre"), ALU.mult, [la["lre"]], [la["t1"]])
    k.tt(L("t2"), L("lim"), L("lim"), ALU.mult, [la["lim"]], [la["t2"]])
    k.tt(L("den"), L("t1"), L("t2"), ALU.add, [la["t1"], la["t2"]], [la["den"]])
    k.op("dve", lambda e: e.reciprocal(L("den"), L("den")), reads=[la["den"]], writes=[la["den"]])
    k.tt(L("t1"), L("nr"), L("lre"), ALU.mult, [la["nr"], la["lre"]], [la["t1"]])
    k.tt(L("t2"), L("lbi"), L("lim"), ALU.mult, [la["lbi"], la["lim"]], [la["t2"]])
    k.tt(L("cr"), L("t1"), L("t2"), ALU.add, [la["t1"], la["t2"]], [la["cr"]])
    k.tt(L("cr"), L("cr"), L("den"), ALU.mult, [la["cr"], la["den"]], [la["cr"]])
    k.tt(L("t1"), L("lbi"), L("lre"), ALU.mult, [la["lbi"], la["lre"]], [la["t1"]])
    k.tt(L("t2"), L("nr"), L("lim"), ALU.mult, [la["nr"], la["lim"]], [la["t2"]])
    k.tt(L("ci"), L("t1"), L("t2"), ALU.subtract, [la["t1"], la["t2"]], [la["ci"]])
    k.tt(L("ci"), L("ci"), L("den"), ALU.mult, [la["ci"], la["den"]], [la["ci"]])
    k.ts(L("f512"), L("f"), float(BLK), None, ALU.mult, None, [la["f"]], [la["f512"]])
    frac("g512", "f512")
    k.copy(par[:, 0, :], L("f"), [la["f"]], [par])
    k.copy(par[:, 1, :], L("rho"), [la["rho"]], [par])
    sincos("t1", "t2", "g512", "ga512")
    k.copy(par[:, 2, :], L("t2"), [la["t2"]], [par])
    k.copy(par[:, 3, :], L("t1"), [la["t1"]], [par])
    k.ts(par[:, 4, :], L("t1"), -1.0, None, ALU.mult, None, [la["t1"]], [par])
    k.copy(par[:, 5, :], L("ar"), [la["ar"]], [par])
    for b_ in range(6):
        kk = 5 - b_
        k.ts(L("f512"), L("g512"), float(kk), None, ALU.mult, None, [la["g512"]], [la["f512"]])
        frac("ga512", "f512")
        sincos("t1", "t2", "ga512", "fa")
        k.act(L("lbr"), L("ar"), AF.Exp, [la["ar"]], [la["lbr"]], scale=float(BLK * kk))
        k.tt(par[:, 6 + b_, :], L("lbr"), L("t2"), ALU.mult, [la["lbr"], la["t2"]], [par])
        k.tt(par[:, 12 + b_, :], L("lbr"), L("t1"), ALU.mult, [la["lbr"], la["t1"]], [par])
    for i in range(18):
        pb_ = ps[i // 16]
        k.tr(pb_[:, (i % 16) * 32:(i % 16 + 1) * 32], par[:, i, :], ident_f[0:32, 0:32], [par, ident_f], [pb_])
    k.copy(parT[:, 0:16, :].rearrange("p a b -> p (a b)"), ps[0][:, 0:512], [ps[0]], [parT])
    k.copy(parT[:, 16:18, :].rearrange("p a b -> p (a b)"), ps[1][:, 0:64], [ps[1]], [parT])

    XT = {nm: k.sbuf("XT_" + nm, [128, 32, 16], F32) for nm in ("bre", "bim", "cre", "cim")}
    mP1 = k.mark()
    raw_re = k.sbuf("raw_re", [32, 2048], F32)
    raw_im = k.sbuf("raw_im", [32, 2048], F32)
    Bre = k.sbuf("Bre", [32, 2048], F32)
    Bim = k.sbuf("Bim", [32, 2048], F32)
    tb = k.sbuf("tb", [32, 2048], F32)
    d_raw = k.dsem("raw")
    k.dma("sp", raw_re[:], bre_d, d_raw, writes=[raw_re])
    k.dma("sp", raw_im[:], bim_d, d_raw, writes=[raw_im])
    k.group_end(d_raw, [raw_re, raw_im])

    def v3(buf):
        return buf.ap(0, [[16, 128], [1, 16]])

    def bc(buf):
        return buf.ap(0, [[1, 128], [0, 16]])
    k.tt(v3(Bre), v3(raw_re), bc(la["cr"]), ALU.mult, [raw_re, la["cr"]], [Bre])
    k.tt(v3(tb), v3(raw_im), bc(la["ci"]), ALU.mult, [raw_im, la["ci"]], [tb])
    k.tt(Bre[:], Bre[:], tb[:], ALU.subtract, [Bre, tb], [Bre])
    k.tt(v3(Bim), v3(raw_im), bc(la["cr"]), ALU.mult, [raw_im, la["cr"]], [Bim])
    k.tt(v3(tb), v3(raw_re), bc(la["ci"]), ALU.mult, [raw_re, la["ci"]], [tb])
    k.tt(Bim[:], Bim[:], tb[:], ALU.add, [Bim, tb], [Bim])

    def to_XT(nm, in_fn, rd):
        pb = ps[1]
        for h in range(16):
            k.tr(pb[:, h * 32:(h + 1) * 32], in_fn(h), ident_f[0:32, 0:32], rd + [ident_f], [pb])
        k.copy(XT[nm].ap(0, [[1, 16], [16, 32]]), pb[:].rearrange("p (h a) -> p h a", h=16), [pb], [XT[nm]])

    to_XT("bre", lambda h: Bre.ap(h, [[16, 128]]), [Bre])
    to_XT("bim", lambda h: Bim.ap(h, [[16, 128]]), [Bim])
    k.dma("sp", raw_re[:], cre_d, d_raw, writes=[raw_re])
    k.dma("sp", raw_im[:], cim_d, d_raw, writes=[raw_im])
    k.group_end(d_raw, [raw_re, raw_im])
    for nm, src in (("cre", raw_re), ("cim", raw_im)):
        k.copy(tb.ap(0, [[128, 16], [64, 2], [1, 64]]), src.ap(0, [[64, 16], [1024, 2], [1, 64]]), [src], [tb])
        to_XT(nm, lambda h: tb[:, h * 128:(h + 1) * 128], [tb])
    k.release(mP1)

    mask2 = k.sbuf("mask2", [128, 4], F32)
    k.ts(mask2[:, 0:1], pidx[:], 64.0, None, ALU.is_lt, None, [pidx], [mask2])
    k.ts(mask2[:, 1:2], pidx[:], 64.0, None, ALU.is_ge, None, [pidx], [mask2])
    k.ts(mask2[:, 2:4], mask2[:, 0:2], -1.0, None, ALU.mult, None, [mask2], [mask2])

    padf = k.sbuf("padf", [128, 32, 128], F32)

    def build_padded(dst_buf, src, neg, dt_direct):
        tgt = dst_buf if dt_direct else padf
        k.memset(tgt[:], 0.0, [tgt])
        for g2 in range(2):
            for b4 in range(4):
                o_ap = tgt.ap(b4 * 128 + b4 * 32 + 16 * g2, [[512, 8], [1, 16]])
                i_ap = src.ap(b4 * 16, [[64, 8], [1, 16]])
                mcol = mask2[:, (2 if neg else 0) + g2:(2 if neg else 0) + g2 + 1]
                k.ts(o_ap, i_ap, mcol, None, ALU.mult, None, [src, mask2], [tgt])

    build_padded(s5w["WA"], XT["cre"], False, True)
    build_padded(s5w["NWA"], XT["cre"], True, True)
    build_padded(s5w["NWB"], XT["cim"], True, True)
    for nm, dst in (("bre", "Bre_pad"), ("bim", "Bim_pad")):
        build_padded(None, XT[nm], False, False)
        for a in range(8):
            pb = ps[2 + a % 2]
            for b4 in range(4):
                k.tr(pb[:, b4 * 128:(b4 + 1) * 128], padf[:, a * 4 + b4, :], ident_f[:], [padf, ident_f], [pb])
            k.copy(s5w[dst][:, a * 4:(a + 1) * 4, :].rearrange("p a b -> p (a b)"), pb[:], [pb], [s5w[dst]])
    k.release(mP)


    mark_yn = k.mark("right")
    yn = k.sbuf("yn", [128, 16, TOWN], BF16, side="right")
    mZ = k.mark("right")
    zb = k.sbuf("zb", [128, 8, TOWN], BF16, side="right")
    tabC = [k.sbuf(f"tabC{i}", [128, BLK], F32) for i in range(2)]
    tabS = [k.sbuf(f"tabS{i}", [128, BLK], F32) for i in range(2)]
    tabWC = [k.sbuf(f"tWC{i}", [128, BLK], F32) for i in range(2)]
    tabWS = [k.sbuf(f"tWS{i}", [128, BLK], F32) for i in range(2)]
    g_ang = k.sbuf("g_ang", [128, BLK], F32)
    g_fr = k.sbuf("g_fr", [128, BLK], F32)
    g_fab = g_ang
    g_wt = g_fr
    pacc = k.sbuf("pacc", [128, 6, 4], F32)
    psr = k.sbuf("psr", [128, 2, 6], F32)
    pcc = k.sbuf("pcc", [128, 8], F32)
    j6 = k.sbuf("j6", [128, 6], F32)
    print("S5 loop: sbuf bytes remaining", nc.sbuf_bytes_remaining)
    angi = k.sbuf("angi", [128, BLK], I32)
    tmp = [k.sbuf(f"s5t{i}", [128, BLK], F32) for i in range(4)]
    ang, fr, fab = tmp[0], tmp[1], tmp[2]
    bh0 = [k.sbuf(f"bh{j}", [128, BLK], F32) for j in range(2)]
    zz0 = [k.sbuf(f"zz{j}", [128, BLK], F32) for j in range(2)]
    bh = [bh0, bh0]
    zz = [zz0, zz0]
    pp = [[k.sbuf(f"pp{i}{j}", [128, BLK], BF16) for j in range(4)] for i in range(2)]
    init = [k.sbuf(f"init{i}", [128, 2], F32) for i in range(2)]
    tini = k.sbuf("tini", [128, 2], F32)
    ytmp = tmp[0]
    d_cv = k.dsem("cv")
    CR = 512
    for c in range(16384 // CR):
        k.dma("pool", puv_bf[c * CR:(c + 1) * CR, 0:D], pu_d[c * CR:(c + 1) * CR, :], d_cv, writes=[tblU])
        k.dma("pool", puv_bf[c * CR:(c + 1) * CR, D:2 * D], pv_d[c * CR:(c + 1) * CR, :], d_cv, writes=[tblU])
    k.group_end(d_cv, [tblU])
    it = 0
    def gen_tables_front(P):
        sl = P % 2
        k.ts(g_ang[:], jvec[:], parT[:, 0, P:P + 1], None, ALU.mult, None, [jvec, parT], [g_ang])
        k.copy(angi[:], g_ang[:], [g_ang], [angi])
        k.tt(g_fr[:], g_ang[:], angi[:], ALU.subtract, [g_ang, angi], [g_fr])
        k.act(tabS[sl][:], g_fr[:], AF.Sin, [g_fr], [tabS[sl]], scale=TWO_PI_S)
        k.act(g_fab[:], g_fr[:], AF.Abs, [g_fr], [g_fab])
        k.act(tabC[sl][:], g_fab[:], AF.Sin, [g_fab], [tabC[sl]], scale=-TWO_PI_S, bias=math.pi / 2)
        k.act(g_wt[:], jrev[:], AF.Exp, [jrev, parT], [g_wt], scale=parT[:, 5, P:P + 1])

    def gen_tables_back(P):
        sl = P % 2
        k.tt(tabWC[sl][:], tabC[sl][:], g_wt[:], ALU.mult, [tabC[sl], g_wt], [tabWC[sl]])
        k.tt(tabWS[sl][:], tabS[sl][:], g_wt[:], ALU.mult, [tabS[sl], g_wt], [tabWS[sl]])

    gen_tables_front(0)
    gen_tables_back(0)
    for P in range(32):
        ct = P // 4
        tC, tS, tWC, tWS = tabC[P % 2], tabS[P % 2], tabWC[P % 2], tabWS[P % 2]
        if P + 1 < 32:
            gen_tables_front(P + 1)
        rho_b = parT[:, 1, P:P + 1].to_broadcast([128, BLK])
        NPB = NBLK - 2
        for b in range(NBLK):
            s = it % 2
            it += 1
            pre, pim = ps[4 + 2 * s], ps[5 + 2 * s]
            rhs = uT[:, ct, b * BLK:(b + 1) * BLK]
            k.mm(pre[:], s5w["Bre_pad"][:, P, :], rhs, True, True, [s5w["Bre_pad"], uT], [pre])
            k.mm(pim[:], s5w["Bim_pad"][:, P, :], rhs, True, True, [s5w["Bim_pad"], uT], [pim])
            if b < NPB:
                for q_, (tw, pq) in enumerate(((tWC, pre), (tWS, pim), (tWC, pim), (tWS, pre))):
                    k.op("dve", lambda e, tw=tw, pq=pq, q_=q_: e.scalar_tensor_tensor(
                        out=tmp[0][:], in0=tw[:], scalar=1.0, in1=pq[:], op0=ALU.mult, op1=ALU.mult,
                        accum_out=pacc[:, b, q_:q_ + 1]), reads=[tw, pq], writes=[tmp[0], pacc])
                if b == NPB - 1:
                    k.tt(psr[:, 0, :], pacc[:, :, 0], pacc[:, :, 1], ALU.add, [pacc], [psr])
                    k.tt(psr[:, 1, :], pacc[:, :, 2], pacc[:, :, 3], ALU.subtract, [pacc], [psr])
                    mre = parT.ap(6 * 32 + P, [[32, 6]])
                    mim = parT.ap(12 * 32 + P, [[32, 6]])
                    for q_, (sv, mw) in enumerate(((0, mre), (1, mim), (1, mre), (0, mim))):
                        k.op("dve", lambda e, sv=sv, mw=mw, q_=q_: e.scalar_tensor_tensor(
                            out=j6[:], in0=psr[:, sv, :], scalar=1.0, in1=mw, op0=ALU.mult, op1=ALU.mult,
                            accum_out=pcc[:, q_:q_ + 1]), reads=[psr, parT], writes=[j6, pcc])
                    k.tt(pcc[:, 4:5], pcc[:, 0:1], pcc[:, 1:2], ALU.subtract, [pcc], [pcc])
                    k.tt(pcc[:, 5:6], pcc[:, 2:3], pcc[:, 3:4], ALU.add, [pcc], [pcc])
                    nin = init[(b + 1) % 2]
                    c5, s5c, ns5 = parT[:, 2, P:P + 1], parT[:, 3, P:P + 1], parT[:, 4, P:P + 1]
                    k.tt(tini[:, 0:1], pcc[:, 4:5], c5, ALU.mult, [pcc, parT], [tini])
                    k.tt(tini[:, 1:2], pcc[:, 5:6], c5, ALU.mult, [pcc, parT], [tini])
                    k.stt(nin[:, 0:1], pcc[:, 5:6], ns5, tini[:, 0:1], ALU.mult, ALU.add, [pcc, parT, tini], [nin])
                    k.stt(nin[:, 1:2], pcc[:, 4:5], s5c, tini[:, 1:2], ALU.mult, ALU.add, [pcc, parT, tini], [nin])
                continue
            k.tt(tmp[0][:], tC[:], pre[:], ALU.mult, [tC, pre], [tmp[0]])
            k.tt(tmp[1][:], tS[:], pim[:], ALU.mult, [tS, pim], [tmp[1]])
            k.tt(bh[s][0][:], tmp[0][:], tmp[1][:], ALU.add, [tmp[0], tmp[1]], [bh[s][0]])
            k.tt(tmp[2][:], tC[:], pim[:], ALU.mult, [tC, pim], [tmp[2]])
            k.tt(tmp[3][:], tS[:], pre[:], ALU.mult, [tS, pre], [tmp[3]])
            k.tt(bh[s][1][:], tmp[2][:], tmp[3][:], ALU.subtract, [tmp[2], tmp[3]], [bh[s][1]])
            ini = init[b % 2]
            for c in range(2):
                iv = ini[:, c:c + 1]
                rd = [parT, bh[s][c], ini]
                k.op("dve", lambda e, c=c, iv=iv: e.tensor_tensor_scan(
                    out=zz[s][c][:], data0=rho_b, data1=bh[s][c][:], initial=iv, op0=ALU.mult, op1=ALU.add),
                    reads=rd, writes=[zz[s][c]])
            if b < NBLK - 1:
                nin = init[(b + 1) % 2]
                e_re, e_im = zz[s][0][:, BLK - 1:BLK], zz[s][1][:, BLK - 1:BLK]
                c5, s5c, ns5 = parT[:, 2, P:P + 1], parT[:, 3, P:P + 1], parT[:, 4, P:P + 1]
                k.tt(tini[:, 0:1], e_re, c5, ALU.mult, [zz[s][0], parT], [tini])
                k.tt(tini[:, 1:2], e_im, c5, ALU.mult, [zz[s][1], parT], [tini])
                k.stt(nin[:, 0:1], e_im, ns5, tini[:, 0:1], ALU.mult, ALU.add, [zz[s][1], parT, tini], [nin])
                k.stt(nin[:, 1:2], e_re, s5c, tini[:, 1:2], ALU.mult, ALU.add, [zz[s][0], parT, tini], [nin])
            if b >= NBLK - 2:
                ob = b - (NBLK - 2)
                yb = ps[2 + ob]
                k.tt(pp[s][0][:], tC[:], zz[s][0][:], ALU.mult, [tC, zz[s][0]], [pp[s][0]])
                k.tt(pp[s][1][:], tS[:], zz[s][1][:], ALU.mult, [tS, zz[s][1]], [pp[s][1]])
                k.tt(pp[s][2][:], tS[:], zz[s][0][:], ALU.mult, [tS, zz[s][0]], [pp[s][2]])
                k.tt(pp[s][3][:], tC[:], zz[s][1][:], ALU.mult, [tC, zz[s][1]], [pp[s][3]])
                wl = ["WA", "NWA", "NWB", "NWB"]
                for j in range(4):
                    k.mm(yb[:], s5w[wl[j]][:, P, :], pp[s][j][:], (P % 4 == 0 and j == 0), (P % 4 == 3 and j == 3),
                         [s5w[wl[j]], pp[s][j]], [yb])
                if P % 4 == 3:
                    k.stt(ytmp[:], uT[:, ct, b * BLK:(b + 1) * BLK], vT[:, V_SD + ct:V_SD + ct + 1], yb[:],
                          ALU.mult, ALU.add, [uT, vT, yb], [ytmp])
                    k.act(zb[:, ct, ob * BLK:(ob + 1) * BLK], ytmp[:], AF.Gelu_apprx_tanh, [ytmp], [zb])

        if P + 1 < 32:
            gen_tables_back(P + 1)

    if stage == "s5":
        obuf = k.sbuf("obuf", [128, D], F32)
        for tt_ in range(8):
            k.memset(obuf[:], 0.0, [obuf])
            for ct in range(8):
                pvb = ps[0][:].bitcast(BF16)
                k.tr(pvb[:, 0:128], zb[:, ct, tt_ * 128:(tt_ + 1) * 128], ident_b[:], [zb, ident_b], [ps[0]])
                k.copy(obuf[:, ct * 128:(ct + 1) * 128], pvb[:, 0:128], [ps[0]], [obuf])
            k.dma("sp", dbg_d[tt_ * 128:(tt_ + 1) * 128, :], obuf[:], d_out, reads=[obuf])
            k.dma("sp", y_d[tt_ * 128:(tt_ + 1) * 128, :], obuf[:], d_out, reads=[obuf])
        k.finish("sp")
        print("instructions", k.n_inst, "waits", k.n_wait)
        return nc

    k.release(mS)
    k.release(0)

    def chan_rstd(dst, src_sq_fn, nct, blk, bank, extra_mean=None):
        pb = ps[bank]
        for ct in range(nct):
            sqb = src_sq_fn(ct)
            k.mm(pb[:], ones_r[:], sqb[:], ct == 0, ct == nct - 1, [ones_r, sqb], [pb])

    glu_w = k.sbuf("glu_w", [128, 8, 1024], BF16, side="right")
    d_w1 = k.dsem("w1")
    for kt in range(8):
        k.dma("pool", glu_w[:, kt, :], glu_d[kt * 128:(kt + 1) * 128, :], d_w1, writes=[glu_w])
    k.group_end(d_w1, [glu_w])
    w_ag = k.sbuf("w_ag", [128, KT, 2048], BF16)
    d_w2 = k.dsem("w2")
    for kt in range(KT):
        k.dma("pool", w_ag[:, kt, :], w_in_d[kt * 128:(kt + 1) * 128, 0:2048], d_w2, writes=[w_ag])
    k.group_end(d_w2, [w_ag])
    mG = k.mark()
    ysf = k.sbuf("ysf", [128, 8, TOWN], F32)
    sg = [k.sbuf(f"sg{i}", [128, BLK], F32) for i in range(2)]
    sq = [k.sbuf(f"sq{i}", [128, BLK], F32R) for i in range(2)]
    rsd = k.sbuf("rsd", [128, BLK], F32)
    tmpn = k.sbuf("tmpn", [128, BLK], F32)
    for blk in range(2):
        tsl = slice(blk * BLK, (blk + 1) * BLK)
        for ct in range(8):
            pb = ps[ct % 2]
            for kt in range(8):
                k.mm(pb[:], glu_w[:, kt, ct * 128:(ct + 1) * 128], zb[:, kt, tsl], kt == 0, kt == 7, [glu_w, zb], [pb])
            k.act(sg[ct % 2][:], pb[:], AF.Sigmoid, [pb], [sg[ct % 2]])
            k.tt(ysf[:, ct, tsl], zb[:, ct, tsl], sg[ct % 2][:], ALU.mult, [zb, sg[ct % 2]], [ysf])
        pb = ps[2]
        for ct in range(8):
            k.act(sq[ct % 2][:], ysf[:, ct, tsl], AF.Square, [ysf], [sq[ct % 2]])
            k.mm(pb[:], ones_r[:], sq[ct % 2][:], ct == 0, ct == 7, [ones_r, sq[ct % 2]], [pb])
        k.act(rsd[:], pb[:], AF.Sqrt, [pb], [rsd], scale=1.0 / 1024, bias=EPS)
        k.op("dve", lambda e: e.reciprocal(rsd[:], rsd[:]), reads=[rsd], writes=[rsd])
        for ct in range(8):
            k.stt(yn[:, 8 + ct, tsl], ysf[:, ct, tsl], vT[:, V_GS + ct:V_GS + ct + 1], rsd[:], ALU.mult, ALU.mult,
                  [ysf, vT, rsd], [yn])
    k.release(mG)
    k.release(mZ, "right")

    mCr = k.mark("right")
    HAL = 30
    cbuf = k.sbuf("cbuf", [128, 8, HAL + TOWN + 2], BF16, side="right")
    cw = k.sbuf("cw", [128, 8, 31], F32, side="right")
    cwraw = k.sbuf("cwraw", [31, 1024], F32)
    k.dma("sp", cwraw[:], convw_d, d_setup, writes=[cwraw])
    k.group_end(d_setup, [cwraw])
    for ct in range(8):
        k.tr(ps[0][:, ct * 32:ct * 32 + 31], cwraw[:, ct * 128:(ct + 1) * 128], ident_f[0:31, 0:31], [cwraw, ident_f], [ps[0]])
    k.copy(cw[:], ps[0][:, 0:256].rearrange("p (c k) -> p c k", c=8)[:, :, 0:31], [ps[0]], [cw])
    alloc_norm_bufs()
    xtile = NB["xtile"]
    fold_g(w_ag, G_MIX, 2048)
    xnT = k.sbuf("xnT2", [128, KT, BLK], BF16)
    sgc = [k.sbuf(f"sgc{i}", [128, BLK], F32) for i in range(2)]
    def cv_post(t):
        if t % 4 != 3:
            return
        b = 5 + t // 4
        for ct in range(8):
            pa, pg = ps[2 + 2 * (ct % 2)], ps[3 + 2 * (ct % 2)]
            for kt in range(KT):
                k.mm(pg[:], w_ag[:, kt, 1024 + ct * 128:1024 + (ct + 1) * 128], xnT[:, kt, :], kt == 0, kt == KT - 1,
                     [w_ag, xnT], [pg])
            for kt in range(KT):
                k.mm(pa[:], w_ag[:, kt, ct * 128:(ct + 1) * 128], xnT[:, kt, :], kt == 0, kt == KT - 1,
                     [w_ag, xnT], [pa])
            k.act(sgc[ct % 2][:], pg[:], AF.Sigmoid, [pg], [sgc[ct % 2]])
            if b == 5:
                k.tt(cbuf[:, ct, 0:HAL], pa[:, BLK - HAL:BLK], sgc[ct % 2][:, BLK - HAL:BLK], ALU.mult,
                     [pa, sgc[ct % 2]], [cbuf])
            else:
                o = HAL + (b - 6) * BLK
                k.tt(cbuf[:, ct, o:o + BLK], pa[:], sgc[ct % 2][:], ALU.mult, [pa, sgc[ct % 2]], [cbuf])

    items = []
    for b in range(5, 8):
        for tl in range(4):
            r0 = b * BLK + tl * 128
            items.append((xs_d[r0:r0 + 128, :], xnT,
                          lambda half, tl=tl: xnT.ap(half * 8 * BLK + tl * 128, [[BLK, 8], [1, 128]])))
    norm_pipeline(items, cv_post)
    k.release(0)
    cv = k.sbuf("cv", [128, 8, TOWN], F32)
    dg = k.sbuf("dg", [128, 8, 31, 128], BF16, side="right")
    for ct in range(8):
        for kk in range(31):
            k.ts(dg[:, ct, kk, :], ident_f[:], cw[:, ct, kk:kk + 1], None, ALU.mult, None, [ident_f, cw], [dg],
                 q="dve")
    for ct in range(8):
        for blk in range(2):
            pb = ps[2 + (ct * 2 + blk) % 4]
            for kk in range(31):
                k.mm(pb[:], dg[:, ct, kk, :], cbuf[:, ct, blk * BLK + kk:blk * BLK + kk + BLK], kk == 0, kk == 30,
                     [dg, cbuf], [pb])
            k.act(cv[:, ct, blk * BLK:(blk + 1) * BLK], pb[:], AF.Identity, [pb, vT], [cv],
                  bias=vT[:, V_CB + ct:V_CB + ct + 1])
    k.release(mCr, "right")
    wbuf = k.sbuf("wbuf", [128, KT, D], BF16, side="right")
    d_w3 = k.dsem("w3")
    for kt in range(KT):
        k.dma("pool", wbuf[:, kt, :], w_out_d[kt * 128:(kt + 1) * 128, :], d_w3, writes=[wbuf])
    k.group_end(d_w3, [wbuf])
    sqc = [k.sbuf(f"sqc{i}", [128, BLK], F32R) for i in range(2)]
    cvr = [k.sbuf(f"cvr{i}", [128, BLK], F32R) for i in range(2)]
    mean = k.sbuf("mean", [128, BLK], F32)
    var = k.sbuf("var", [128, BLK], F32)
    m2 = k.sbuf("m2", [128, BLK], F32)
    rsd2 = k.sbuf("rsd2", [128, BLK], F32)
    for blk in range(2):
        tsl = slice(blk * BLK, (blk + 1) * BLK)
        p1, p2 = ps[0], ps[1]
        for ct in range(8):
            k.copy(cvr[ct % 2][:], cv[:, ct, tsl], [cv], [cvr[ct % 2]], q="pool")
            k.act(sqc[ct % 2][:], cv[:, ct, tsl], AF.Square, [cv], [sqc[ct % 2]])
            k.mm(p1[:], ones_r[:], cvr[ct % 2][:], ct == 0, ct == 7, [ones_r, cvr[ct % 2]], [p1])
            k.mm(p2[:], ones_r[:], sqc[ct % 2][:], ct == 0, ct == 7, [ones_r, sqc[ct % 2]], [p2])
        k.act(mean[:], p1[:], AF.Copy, [p1], [mean], scale=1.0 / 1024)
        k.tt(m2[:], mean[:], mean[:], ALU.mult, [mean], [m2])
        k.stt(var[:], p2[:], 1.0 / 1024, m2[:], ALU.mult, ALU.subtract, [p2, m2], [var])
        k.act(var[:], var[:], AF.Sqrt, [var], [var], bias=EPS)
        k.op("dve", lambda e: e.reciprocal(var[:], var[:]), reads=[var], writes=[var])
        for ct in range(8):
            k.tt(cv[:, ct, tsl], cv[:, ct, tsl], mean[:], ALU.subtract, [cv, mean], [cv])
            k.tt(cv[:, ct, tsl], cv[:, ct, tsl], var[:], ALU.mult, [cv, var], [cv])
            k.act(cv[:, ct, tsl], cv[:, ct, tsl], AF.Silu, [cv, vT], [cv],
                  scale=vT[:, V_LNG + ct:V_LNG + ct + 1], bias=vT[:, V_LNB + ct:V_LNB + ct + 1])
        p3 = ps[2]
        for ct in range(8):
            k.act(sqc[ct % 2][:], cv[:, ct, tsl], AF.Square, [cv], [sqc[ct % 2]])
            k.mm(p3[:], ones_r[:], sqc[ct % 2][:], ct == 0, ct == 7, [ones_r, sqc[ct % 2]], [p3])
        k.act(rsd2[:], p3[:], AF.Sqrt, [p3], [rsd2], scale=1.0 / 1024, bias=EPS)
        k.op("dve", lambda e: e.reciprocal(rsd2[:], rsd2[:]), reads=[rsd2], writes=[rsd2])
        for ct in range(8):
            k.stt(yn[:, ct, tsl], cv[:, ct, tsl], vT[:, V_GC + ct:V_GC + ct + 1], rsd2[:], ALU.mult, ALU.mult,
                  [cv, vT, rsd2], [yn])
    k.release(0)

    h = k.sbuf("h", [128, 8, D], F32)
    mW = k.mark()
    xres = [k.sbuf(f"xres{i}", [128, D], F32) for i in range(2)]
    d_xr = [k.dsem(f"xr{i}") for i in range(2)]
    for t in range(8):
        xr = xres[t % 2]
        k.dma("sp", xr[:], xs_d[TALL - TOWN + t * 128:TALL - TOWN + (t + 1) * 128, :], d_xr[t % 2], writes=[xr])
        for n in range(4):
            pb = ps[n % 2]
            for kt in range(KT):
                k.mm(pb[:], yn[:, kt, t * 128:(t + 1) * 128], wbuf[:, kt, n * 512:(n + 1) * 512], kt == 0, kt == KT - 1,
                     [yn, wbuf], [pb])
            k.tt(h[:, t, n * 512:(n + 1) * 512], pb[:], xr[:, n * 512:(n + 1) * 512], ALU.add, [pb, xr], [h])
    k.release(mW)
    k.release(mark_yn, "right")

    def dump_h(dst_list):
        for t in range(8):
            for dd in dst_list:
                k.dma("sp", dd[t * 128:(t + 1) * 128, :], h[:, t, :], d_out, reads=[h])

    if stage == "h1":
        dump_h([dbg_d, y_d])
        k.finish("sp")
        print("instructions", k.n_inst, "waits", k.n_wait)
        return nc

    mE = k.mark()
    KTb = k.sbuf("KTb", [128, 16, 256], BF16)
    Vb = k.sbuf("Vb", [128, 2, D], BF16)
    wch = [k.sbuf(f"wch{i}", [128, KT, 512], BF16) for i in range(2)]
    d_wc = [k.dsem(f"wc{i}") for i in range(2)]
    wcnt = {"n": 0}

    def load_wchunk(w_d, c0, g_off=None):
        i = wcnt["n"] % 2
        wcnt["n"] += 1
        for q4 in range(4):
            src = w_d[q4 * 512:(q4 + 1) * 512, c0:c0 + 512].rearrange("(kt p) c -> p kt c", p=128)
            k.dma("pool", wch[i][:, q4 * 4:(q4 + 1) * 4, :], src, d_wc[i], writes=[wch[i]])
        k.group_end(d_wc[i], [wch[i]])
        if g_off is not None:
            fold_g(wch[i], g_off, 512)
        return wch[i]

    mE1 = k.mark()
    alloc_norm_bufs()
    xtile = NB["xtile"]
    memT = k.sbuf("memT", [128, KT, 256], BF16)
    def mem_src(mt):
        def f():
            i = load_tok_tile(mem_d[mt * 128:(mt + 1) * 128, :])
            return xtile[i], xtile[i][:]
        return f
    norm_pipeline([(mem_d[mt * 128:(mt + 1) * 128, :], memT,
                    lambda half, mt=mt: memT.ap(half * 8 * 256 + mt * 128, [[256, 8], [1, 128]])) for mt in range(2)])
    for c in range(8):
        wc = load_wchunk(w_kv_d, c * 512, G_MEMN)
        if c < 4:
            for j in range(4):
                pb = ps[2 + j % 2]
                for kt in range(KT):
                    k.mm(pb[:, 0:256], wc[:, kt, j * 128:(j + 1) * 128], memT[:, kt, :], kt == 0, kt == KT - 1,
                         [wc, memT], [pb])
                k.act(KTb[:, c * 4 + j, :], pb[:, 0:256], AF.Copy, [pb], [KTb])
        else:
            for mt in range(2):
                pb = ps[2 + mt]
                for kt in range(KT):
                    k.mm(pb[:], memT[:, kt, mt * 128:(mt + 1) * 128], wc[:, kt, :], kt == 0, kt == KT - 1,
                         [wc, memT], [pb])
                k.act(Vb[:, mt, (c - 4) * 512:(c - 3) * 512], pb[:], AF.Copy, [pb], [Vb])
    k.release(mE1)
    qT = k.sbuf("qT", [128, 16, TOWN], BF16)
    hnT = k.sbuf("hnT", [128, 16, TOWN], BF16)
    mE2 = k.mark()
    alloc_norm_bufs(False)
    norm_pipeline([((h, h[:, t, :]), hnT,
                    lambda half, t=t: hnT.ap(half * 8 * TOWN + t * 128, [[TOWN, 8], [1, 128]])) for t in range(8)])
    k.release(mE2)
    qscale = 1.0 / math.sqrt(512.0)
    for c in range(4):
        wc = load_wchunk(w_q_d, c * 512, G_MEM)
        for j in range(4):
            for blk in range(2):
                pb = ps[2 + (2 * j + blk) % 2]
                for kt in range(KT):
                    k.mm(pb[:], wc[:, kt, j * 128:(j + 1) * 128], hnT[:, kt, blk * BLK:(blk + 1) * BLK], kt == 0,
                         kt == KT - 1, [wc, hnT], [pb])
                k.act(qT[:, c * 4 + j, blk * BLK:(blk + 1) * BLK], pb[:], AF.Copy, [pb], [qT], scale=qscale)
    PT = k.sbuf("PT", [128, 8, TOWN], BF16)
    pex = [k.sbuf(f"pex{i}", [128, 256], F32) for i in range(2)]
    pnb = [k.sbuf(f"pnb{i}", [128, 256], BF16) for i in range(2)]
    sst = [k.sbuf(f"sst{i}", [128, 4], F32) for i in range(2)]
    def att_scores(it_):
        hd, t = divmod(it_, 8)
        pb = ps[2 + it_ % 2]
        for j in range(4):
            k.mm(pb[:, 0:256], qT[:, hd * 4 + j, t * 128:(t + 1) * 128], KTb[:, hd * 4 + j, :], j == 0, j == 3,
                 [qT, KTb], [pb])

    def att_softmax(it_):
        hd, t = divmod(it_, 8)
        i = it_ % 2
        pb = ps[2 + i]
        st = sst[i]
        k.op("dve", lambda e: e.tensor_reduce(out=st[:, 0:1], in_=pb[:, 0:256], axis=AX.X, op=ALU.max),
             reads=[pb], writes=[st])
        k.ts(st[:, 1:2], st[:, 0:1], -1.0, None, ALU.mult, None, [st], [st])
        k.act(pex[i][:], pb[:, 0:256], AF.Exp, [pb, st], [pex[i], st], bias=st[:, 1:2], accum_out=st[:, 2:3])
        k.op("dve", lambda e: e.reciprocal(st[:, 3:4], st[:, 2:3]), reads=[st], writes=[st])
        k.ts(pnb[i][:], pex[i][:], st[:, 3:4], None, ALU.mult, None, [pex[i], st], [pnb[i]])
        pt = ps[4 + i]
        ptv = pt[:].bitcast(BF16)
        for mt in range(2):
            k.tr(ptv[:, mt * 128:(mt + 1) * 128], pnb[i][:, mt * 128:(mt + 1) * 128], ident_b[:], [pnb[i], ident_b], [pt])
        k.copy(PT.ap(hd * 2 * TOWN + t * 128, [[TOWN, 2], [1, 128]]),
               ptv[:, 0:256].rearrange("p (m t) -> p m t", m=2), [pt], [PT])

    att_scores(0)
    for it_ in range(32):
        if it_ + 1 < 32:
            att_scores(it_ + 1)
        att_softmax(it_)
    OT = hnT
    for hd in range(4):
        for dvt in range(4):
            for blk in range(2):
                pb = ps[2 + (dvt * 2 + blk) % 2]
                for mt in range(2):
                    k.mm(pb[:], Vb[:, mt, (hd * 4 + dvt) * 128:(hd * 4 + dvt + 1) * 128],
                         PT[:, hd * 2 + mt, blk * BLK:(blk + 1) * BLK], mt == 0, mt == 1, [Vb, PT], [pb])
                k.act(OT[:, hd * 4 + dvt, blk * BLK:(blk + 1) * BLK], pb[:], AF.Copy, [pb], [OT])
    for n in range(4):
        wc = load_wchunk(w_o_d, n * 512)
        for t in range(8):
            pb = ps[2 + t % 2]
            for ct in range(KT):
                k.mm(pb[:], OT[:, ct, t * 128:(t + 1) * 128], wc[:, ct, :], ct == 0, ct == KT - 1, [OT, wc], [pb])
            k.tt(h[:, t, n * 512:(n + 1) * 512], h[:, t, n * 512:(n + 1) * 512], pb[:], ALU.add, [h, pb], [h])
    k.release(mE)

    if stage == "h2":
        dump_h([dbg_d, y_d])
        k.finish("sp")
        print("instructions", k.n_inst, "waits", k.n_wait)
        return nc

    idxT = k.sbuf("idxT", [128, TOWN], U32)
    gT = k.sbuf("gT", [128, TOWN], F32)
    hn3b = k.sbuf("hn3b", [128, 8, D], BF16)
    rstd3 = k.sbuf("rstd3", [128, 16], F32)
    mF0 = k.mark()
    gffn_b = k.sbuf("gffn_b", [128, D], F32)
    k.dma("sp", gffn_b[:], bass.AP(tensor=vecs_d.tensor, offset=G_FFN * 128, ap=[[0, 128], [1, D]]), d_setup,
          writes=[gffn_b])
    k.group_end(d_setup, [gffn_b])
    xs0 = k.sbuf("xs0", [128, D], F32)
    for t in range(8):
        k.act(xs0[:], h[:, t, :], AF.Square, [h], [xs0, rstd3], accum_out=rstd3[:, 8 + t:9 + t])
        pool_rsqrt(rstd3, rstd3[:, t:t + 1], rstd3[:, 8 + t:9 + t], 1.0 / D, [128, 1])
        k.stt(hn3b[:, t, :], h[:, t, :], rstd3[:, t:t + 1], gffn_b[:], ALU.mult, ALU.mult, [h, rstd3, gffn_b], [hn3b])
    k.release(mF0)
    mF = k.mark()
    import os
    RL = float(os.environ.get("RL", "99")) if stage == "route" else 99

    def route_body():
        idxf = k.sbuf("idxf", [128, 8, 128], F32)
        gtf = k.sbuf("gtf", [128, 8, 128], F32)
        keysT = k.sbuf("keysT", [128, 8, 128], F32)
        kraw = k.sbuf("kraw", [128, 128], F32)
        d_kr = k.dsem("kr")
        for hq in range(8):
            k.dma("sp", kraw[:].rearrange("n (s c) -> n s c", s=2), pkeys_d[hq].rearrange("s n c -> n s c"), d_kr,
                  writes=[kraw])
            k.tr(ps[0][:, 0:128], kraw[:], ident_f[:], [kraw, ident_f], [ps[0]])
            k.copy(keysT[:, hq, :], ps[0][:, 0:128], [ps[0]], [keysT])
        if RL <= 1:
            return
        iota16 = k.sbuf("iota16", [128, 16], F32)
        k.op("pool", lambda e: e.iota(iota16[:], pattern=[[1, 16]], base=0, channel_multiplier=0,
                                      allow_small_or_imprecise_dtypes=True), writes=[iota16])
        pwq = k.sbuf("pwq", [128, KT, 512], F32R)
        wsts = [k.sbuf(f"wst{i}", [128, 512], F32) for i in range(2)]
        d_wss = [k.dsem(f"ws{i}") for i in range(2)]
        xs32 = k.sbuf("xs32", [128, D], F32)
        hnT3s = [k.sbuf(f"hnT3{i}", [128, KT, 128], F32R) for i in range(2)]
        qTts = [k.sbuf(f"qTt{i}", [128, 4, 128], F32) for i in range(2)]
        scss = [k.sbuf(f"scs{i}", [128, 8, 128], F32) for i in range(2)]
        NCH = 4
        SC = []
        for c_ in range(NCH):
            SC.append(dict(
                scw=k.sbuf(f"scw{c_}", [128, 256], F32), cand=k.sbuf(f"cand{c_}", [128, 256], F32),
                s1v=k.sbuf(f"s1v{c_}", [128, 2, 16], F32), s1i=k.sbuf(f"s1i{c_}", [128, 2, 16], F32),
                iu=k.sbuf(f"iu{c_}", [128, 16], U32), tv=k.sbuf(f"tv{c_}", [128, 16], F32),
                posf=k.sbuf(f"posf{c_}", [128, 16], F32), pai=k.sbuf(f"pai{c_}", [128, 16], I32),
                paf=k.sbuf(f"paf{c_}", [128, 16], F32), pbf=k.sbuf(f"pbf{c_}", [128, 16], F32),
                eq=k.sbuf(f"eq{c_}", [128, 16, 16], F32), sel=k.sbuf(f"sel{c_}", [128, 2, 16], F32),
                rst=k.sbuf(f"rst{c_}", [128, 8], F32), ex=k.sbuf(f"ex{c_}", [128, 16], F32)))

        def top16(B, vals_buf, vals_ap, src_buf, src_ap, idx_buf, idx_ap, n):
            scw, iu = B["scw"], B["iu"]
            k.op("dve", lambda e: e.max(out=vals_ap[:, 0:8], in_=src_ap), reads=[src_buf], writes=[vals_buf])
            yield
            k.op("dve", lambda e: e.match_replace(out=scw[:, 0:n], in_to_replace=vals_ap[:, 0:8], in_values=src_ap,
                                                  imm_value=-1e30), reads=[src_buf, vals_buf], writes=[scw])
            yield
            k.op("dve", lambda e: e.max(out=vals_ap[:, 8:16], in_=scw[:, 0:n]), reads=[scw], writes=[vals_buf])
            yield
            k.op("dve", lambda e: e.max_index(out=iu[:, 0:8], in_max=vals_ap[:, 0:8], in_values=src_ap),
                 reads=[src_buf, vals_buf], writes=[iu])
            yield
            k.op("dve", lambda e: e.max_index(out=iu[:, 8:16], in_max=vals_ap[:, 8:16], in_values=src_ap),
                 reads=[src_buf, vals_buf], writes=[iu])
            yield
            k.copy(idx_ap, iu[:], [iu], [idx_buf])
            yield

        def head_chain(B, scs, hh, hq, t):
            s1v, s1i, cand, tv, posf = B["s1v"], B["s1i"], B["cand"], B["tv"], B["posf"]
            rst, ex, paf, pai, pbf, eq, sel = B["rst"], B["ex"], B["paf"], B["pai"], B["pbf"], B["eq"], B["sel"]
            for sd in range(2):
                yield from top16(B, s1v, s1v[:, sd, :], scs, scs[:, hh * 2 + sd, :], s1i, s1i[:, sd, :], 128)
            k.tt(cand[:].rearrange("p (a b) -> p a b", a=16), s1v.ap(0, [[1, 16], [0, 16]]),
                 s1v.ap(16, [[0, 16], [1, 16]]), ALU.add, [s1v], [cand])
            yield
            yield from top16(B, tv, tv[:], cand, cand[:], posf, posf[:], 256)
            k.ts(rst[:, 2:3], tv[:, 0:1], -1.0, None, ALU.mult, None, [tv], [rst])
            yield
            k.act(ex[:], tv[:], AF.Exp, [tv, rst], [ex, rst], bias=rst[:, 2:3], accum_out=rst[:, 3:4])
            yield
            k.op("dve", lambda e: e.reciprocal(rst[:, 4:5], rst[:, 3:4]), reads=[rst], writes=[rst])
            yield
            k.ts(gtf[:, t, hq * 16:(hq + 1) * 16], ex[:], rst[:, 4:5], None, ALU.mult, None, [ex, rst], [gtfB[hq]])
            yield
            k.ts(paf[:], posf[:], 0.0625, -0.46875, ALU.mult, ALU.add, [posf], [paf])
            yield
            k.copy(pai[:], paf[:], [paf], [pai])
            yield
            k.copy(paf[:], pai[:], [pai], [paf])
            yield
            k.stt(pbf[:], paf[:], -16.0, posf[:], ALU.mult, ALU.add, [paf, posf], [pbf])
            yield
            for sd, pos_b in ((0, paf), (1, pbf)):
                k.tt(eq[:], pos_b.ap(0, [[1, 16], [0, 16]]), iota16.ap(0, [[0, 16], [1, 16]]), ALU.is_equal,
                     [pos_b, iota16], [eq])
                yield
                k.tt(eq[:], eq[:], s1i.ap(sd * 16, [[0, 16], [1, 16]]), ALU.mult, [eq, s1i], [eq])
                yield
                k.op("dve", lambda e, sd=sd: e.tensor_reduce(out=sel[:, sd, :], in_=eq[:], axis=AX.X, op=ALU.add),
                     reads=[eq], writes=[sel])
                yield
            k.stt(idxf[:, t, hq * 16:(hq + 1) * 16], sel[:, 0, :], 128.0, sel[:, 1, :], ALU.mult, ALU.add,
                  [sel], [idxfB[hq]])
            yield

        idxfB = [Buf(f"idxf_h{i}") for i in range(8)]
        gtfB = [Buf(f"gtf_h{i}") for i in range(8)]

        for half in range(2):
            for kt in range(KT):
                wst, d_ws = wsts[kt % 2], d_wss[kt % 2]
                k.dma("sp", wst[:], pwq_d[kt * 128:(kt + 1) * 128, half * 512:(half + 1) * 512], d_ws, writes=[wst])
                k.act(pwq[:, kt, :], wst[:], AF.Copy, [wst, vT], [pwq], scale=vT[:, G_FFN + kt:G_FFN + kt + 1])
            if RL <= 1.4:
                return
            def front(t):
                hnT3, qTt, scs = hnT3s[t % 2], qTts[t % 2], scss[t % 2]
                k.act(xs32[:], h[:, t, :], AF.Copy, [h, rstd3], [xs32], scale=rstd3[:, t:t + 1])
                for q4 in range(4):
                    pb = ps[q4]
                    for j in range(4):
                        kt = q4 * 4 + j
                        k.tr(pb[:, j * 128:(j + 1) * 128], xs32[:, kt * 128:(kt + 1) * 128], ident_f[:], [xs32, ident_f], [pb])
                    k.act(hnT3[:, q4 * 4:(q4 + 1) * 4, :], pb[:].rearrange("p (j t) -> p j t", j=4), AF.Copy, [pb], [hnT3])
                for hh in range(4):
                    pb = ps[4 + hh % 2]
                    for kt in range(KT):
                        k.mm(pb[:, 0:128], pwq[:, kt, hh * 128:(hh + 1) * 128], hnT3[:, kt, :], kt == 0, kt == KT - 1,
                             [pwq, hnT3], [pb])
                    k.act(qTt[:, hh, :], pb[:, 0:128], AF.Copy, [pb], [qTt])
                for hh in range(4):
                    hq = half * 4 + hh
                    for sd in range(2):
                        pb = ps[6 + sd]
                        k.mm(pb[:, hh * 128:(hh + 1) * 128], qTt[sd * 64:(sd + 1) * 64, hh, :],
                             keysT[sd * 64:(sd + 1) * 64, hq, :], True, True, [qTt, keysT], [pb])
                for sd in range(2):
                    k.act(scs.ap(sd * 128, [[256, 4], [1, 128]]), ps[6 + sd][:].rearrange("p (h n) -> p h n", h=4),
                          AF.Copy, [ps[6 + sd]], [scs])

            front(0)
            for t in range(8):
                if t + 1 < 8:
                    front(t + 1)
                scs = scss[t % 2]
                chains = [head_chain(SC[hh], scs, hh, half * 4 + hh, t) for hh in range(4)]
                live = list(chains)
                while live:
                    for g_ in list(live):
                        try:
                            next(g_)
                        except StopIteration:
                            live.remove(g_)
        if RL <= 6:
            return
        for t in range(8):
            k.tr(ps[0][:, 0:128], idxf[:, t, :], ident_f[:], [idxf, ident_f] + idxfB, [ps[0]])
            k.copy(idxT[:, t * 128:(t + 1) * 128], ps[0][:, 0:128], [ps[0]], [idxT])
            k.tr(ps[1][:, 0:128], gtf[:, t, :], ident_f[:], [gtf, ident_f] + gtfB, [ps[1]])
            k.copy(gT[:, t * 128:(t + 1) * 128], ps[1][:, 0:128], [ps[1]], [gT])

    route_body()
    k.release(mF)

    if stage == "route":
        obuf = k.sbuf("obuf", [128, D], F32)
        for t in range(8):
            k.memset(obuf[:], 0.0, [obuf])
            k.tr(ps[0][:, 0:128], gT[:, t * 128:(t + 1) * 128], ident_f[:], [gT, ident_f], [ps[0]])
            k.copy(obuf[:, 0:128], ps[0][:, 0:128], [ps[0]], [obuf])
            k.copy(gT[:, t * 128:(t + 1) * 128], idxT[:, t * 128:(t + 1) * 128], [idxT, ps[0]], [gT])
            k.tr(ps[1][:, 0:128], gT[:, t * 128:(t + 1) * 128], ident_f[:], [gT, ident_f], [ps[1]])
            k.copy(obuf[:, 128:256], ps[1][:, 0:128], [ps[1]], [obuf])
            for dd in (dbg_d, y_d):
                k.dma("sp", dd[t * 128:(t + 1) * 128, :], obuf[:], d_out, reads=[obuf])
        k.finish("sp")
        print("instructions", k.n_inst, "waits", k.n_wait)
        return nc

    mF3 = k.mark()
    Ebig = k.sbuf("Ebig", [128, 256], F32)
    k.memset(Ebig[:], 0.0, [Ebig])
    k.memset(Ebig[:, 127:128], 1.0, [Ebig])
    fng_b = k.sbuf("fng_b", [128, D], F32)
    k.dma("sp", fng_b[:], bass.AP(tensor=fng_d.tensor, offset=0, ap=[[0, 128], [1, D]]), d_setup, writes=[fng_b])
    k.group_end(d_setup, [fng_b])
    NG = 9
    UV = [k.sbuf(f"UV{i}", [128, 2 * D], BF16) for i in range(NG)]
    d_u = [k.dsem(f"u{i}") for i in range(NG)]
    lw = [k.sbuf(f"lw{i}", [128, 128], BF16) for i in range(4)]
    accs = [k.sbuf(f"accs{i}", [128, 8], F32) for i in range(2)]
    gact = [k.sbuf(f"gact{i}", [128, 1], F32) for i in range(2)]
    junkd = k.sbuf("junkd", [128, 1024], BF16)
    obuf = k.sbuf("obuf", [128, D], F32)
    fst = k.sbuf("fst", [128, 4], F32)
    d_o = k.dsem("o")
    NTOK = TOWN

    def issue_gather(tg):
        i = tg % NG
        k.dma_custom("pool", lambda e: e.indirect_dma_start(
            out=UV[i][:], out_offset=None, in_=puv_bf,
            in_offset=bass.IndirectOffsetOnAxis(ap=idxT[:, tg:tg + 1], axis=0)), d_u[i], reads=[idxT, tblU], writes=[UV[i]])

    def issue_bcast(tg):
        t, tl = tg // 128, tg % 128
        for n in range(4):
            k.mm(ps[n][:], ident_b[:, tl:tl + 1].to_broadcast([128, 128]), hn3b[:, t, n * 512:(n + 1) * 512], True, True,
                 [ident_b, hn3b], [ps[n]])

    def issue_dots(tg):
        i, ac = tg % NG, accs[tg % 2]
        for n in range(2):
            k.op("dve", lambda e, n=n: e.scalar_tensor_tensor(
                out=junkd[:], in0=UV[i][:, n * 1024:(n + 1) * 1024], scalar=1.0, in1=psd[n][:],
                op0=ALU.mult, op1=ALU.mult, accum_out=ac[:, n:n + 1]), reads=[UV[i], ps[2 * n], ps[2 * n + 1]],
                writes=[junkd, ac])
        k.act(gact[tg % 2][:], ac[:, 0:1], AF.Gelu_apprx_tanh, [ac], [gact[tg % 2]], bias=ac[:, 1:2])

    def issue_combine(tg):
        t, tl = tg // 128, tg % 128
        i, j = tg % NG, tg % 2
        lwj = lw[tg % 4]
        k.ts(lwj[:], Ebig[:, 127 - tl:255 - tl], gact[j][:, 0:1], gT[:, tg:tg + 1], ALU.mult, ALU.mult,
             [Ebig, gact[j], gT], [lwj], q="pool")
        for n in range(4):
            k.mm(ps[4 + n][:], lwj[:], UV[i][:, D + n * 512:D + (n + 1) * 512], tl == 0, tl == 127, [lwj, UV[i]],
                 [ps[4 + n]])
        if tl == 127:
            for n in range(4):
                k.tt(h[:, t, n * 512:(n + 1) * 512], h[:, t, n * 512:(n + 1) * 512], ps[4 + n][:], ALU.add,
                     [h, ps[4 + n]], [h])
            k.act(obuf[:], h[:, t, :], AF.Square, [h], [obuf, fst], accum_out=fst[:, 0:1])
            pool_rsqrt(fst, fst[:, 1:2], fst[:, 0:1], 1.0 / D, [128, 1])
            k.stt(obuf[:], h[:, t, :], fst[:, 1:2], fng_b[:], ALU.mult, ALU.mult, [h, fst, fng_b], [obuf])
            k.dma("sp", y_d[t * 128:(t + 1) * 128, :], obuf[:], d_o, reads=[obuf])
            if dbg_d is not None:
                k.dma("sp", dbg_d[t * 128:(t + 1) * 128, :], obuf[:], d_o, reads=[obuf])

    for tg in range(min(NG - 1, NTOK)):
        issue_gather(tg)
    issue_bcast(0)
    issue_dots(0)
    issue_bcast(1)
    for tg in range(NTOK):
        if tg + NG - 1 < NTOK:
            issue_gather(tg + NG - 1)
        if tg + 1 < NTOK:
            issue_dots(tg + 1)
        if tg + 2 < NTOK:
            issue_bcast(tg + 2)
        issue_combine(tg)
    k.finish("sp")
    print("instructions", k.n_inst, "waits", k.n_wait)
    return nc


def make_in_maps(inp):
    f = lambda a: np.ascontiguousarray(np.asarray(a, dtype=np.float32))
    x = f(inp["x"])
    mem = f(inp["mem"])
    vec_names = ["norm_mix_g", "norm_mem_g", "mem_norm_g", "norm_ffn_g"]
    rows = [f(inp[n]).reshape(16, 128) for n in vec_names]
    for n in ["conv_b", "conv_ln_g", "conv_ln_b", "grp_norm_conv_g", "grp_norm_ssm_g", "ssm_d"]:
        rows.append(f(inp[n]).reshape(8, 128))
    vecs = np.ascontiguousarray(np.concatenate(rows, axis=0))
    shared = {
        "vecs": vecs,
        "conv_w": f(inp["conv_w"]).reshape(31, 1024),
        "lam_re": f(inp["ssm_lam_re"]).reshape(32, 128),
        "lam_im": f(inp["ssm_lam_im"]).reshape(32, 128),
        "log_dt": f(inp["ssm_log_dt"]).reshape(32, 2),
        "b_re": f(inp["ssm_b_re"]).reshape(32, 2048),
        "b_im": f(inp["ssm_b_im"]).reshape(32, 2048),
        "c_re": f(inp["ssm_c_re"]).reshape(32, 2048),
        "c_im": f(inp["ssm_c_im"]).reshape(32, 2048),
        "w_in": f(inp["w_in"]).reshape(D, 3072),
        "glu_w": f(inp["ssm_glu_w"]).reshape(1024, 1024),
        "w_out": f(inp["w_out"]).reshape(D, D),
        "w_q": f(inp["w_q_mem"]).reshape(D, D),
        "w_kv": f(inp["w_kv_mem"]).reshape(D, 2 * D),
        "w_o": f(inp["w_o_mem"]).reshape(D, D),
        "peer_w_q": f(inp["peer_w_q"]).reshape(D, 1024),
        "peer_keys": f(inp["peer_keys"]).reshape(8, 2, 128, 64),
        "peer_u": f(inp["peer_u"]).reshape(16384, D),
        "peer_v": f(inp["peer_v"]).reshape(16384, D),
        "final_g": f(inp["final_norm_g"]).reshape(1, D),
    }
    maps = []
    for c in range(NCORES):
        b, seg = c // 4, c % 4
        xs = np.zeros((TALL, D), np.float32)
        n_have = (seg + 1) * TOWN
        xs[TALL - n_have:] = x[b, :n_have]
        m = dict(shared)
        m["xs"] = xs
        m["mem"] = np.ascontiguousarray(mem[b])
        maps.append(m)
    return maps


def kernel(**inputs):
    nc = build_program("full")
    in_maps = make_in_maps(inputs)
    res = run_bass_kernel_spmd(nc, in_maps, core_ids=list(range(NCORES)))
    out = np.zeros((2, 4096, D), np.float32)
    for c in range(NCORES):
        b, seg = c // 4, c % 4
        out[b, seg * TOWN:(seg + 1) * TOWN] = res.results[c]["y"]
    return out
```

```python
import math
import numpy as np
import concourse.bass as bass
import concourse.mybir as mybir
from concourse.bass_utils import run_bass_kernel_spmd

F32 = mybir.dt.float32
F32R = mybir.dt.float32r
BF16 = mybir.dt.bfloat16
U32 = mybir.dt.uint32
I32 = mybir.dt.int32
ALU = mybir.AluOpType
AF = mybir.ActivationFunctionType
AX = mybir.AxisListType

NCORES = 8
D = 2048
KT = 16
TOWN = 1024
TALL = 4096
BLK = 512
NBLK = TALL // BLK
EPS = 1e-6
TWO_PI_S = 6.28318


class Sem:
    def __init__(self, nc, name, step):
        self.h = nc.semaphore(name).__enter__()
        self.name = name
        self.cnt = 0
        self.step = step


class Buf:
    def __init__(self, name, t=None):
        self.name = name
        self.t = t
        self.last_w = None
        self.readers = {}

    def __getitem__(self, k):
        return self.t[k]

    def ap(self, offset, dims):
        full = self.t[:]
        pstep = full.ap[0][0]
        return bass.AP(tensor=self.t, offset=offset, ap=[[pstep, full.ap[0][1]]] + [list(d) for d in dims])

    def app(self, p0, pn, offset, dims):
        full = self.t[:]
        pstep = full.ap[0][0]
        return bass.AP(tensor=self.t, offset=p0 * pstep + offset, ap=[[pstep, pn]] + [list(d) for d in dims])


class K:
    def __init__(self, nc):
        self.nc = nc
        self.eng = {"pe": nc.tensor, "act": nc.scalar, "dve": nc.vector,
                    "pool": nc.gpsimd, "sp": nc.sync}
        self.qsem = {q: Sem(nc, "q_" + q, 1) for q in self.eng}
        self.seen = {q: {} for q in self.eng}
        self.n_inst = 0
        self.n_wait = 0
        self.dsems = []
        self._stack = []
        self._rstack = []
        self.fence = {}

    def sbuf(self, name, shape, dt=F32, side="left"):
        self._uid = getattr(self, "_uid", 0) + 1
        cm = self.nc.sbuf_tensor(f"sb{self._uid}_{name}", list(shape), dt, side=side)
        t = cm.__enter__()
        b = Buf(name, t)
        b.readers = dict(self.fence)
        (self._stack if side == "left" else self._rstack).append((cm, b))
        return b

    def psum(self, name, shape, dt=F32):
        cm = self.nc.psum_tensor(name, list(shape), dt)
        t = cm.__enter__()
        b = Buf(name, t)
        self._stack.append((cm, b))
        return b

    def mark(self, side="left"):
        return len(self._stack if side == "left" else self._rstack)

    def release(self, mark, side="left"):
        st = self._stack if side == "left" else self._rstack
        while len(st) > mark:
            cm, b = st.pop()
            deps = list(b.readers.items()) + ([b.last_w] if b.last_w else [])
            for sm, tick in deps:
                if self.fence.get(sm, 0) < tick:
                    self.fence[sm] = tick
            cm.__exit__(None, None, None)

    def dsem(self, name):
        s = Sem(self.nc, "d_" + name, 16)
        self.dsems.append(s)
        return s

    def _waits(self, q, reads, writes, skip=None):
        need = {}
        myq = self.qsem[q]

        def add(dep, same_ok):
            if dep is None:
                return
            s, tick = dep
            if s is skip:
                return
            if s is myq and not same_ok:
                return
            if need.get(s, 0) < tick:
                need[s] = tick

        same = q != "pe"
        for b in reads:
            add(b.last_w, same)
        for b in writes:
            add(b.last_w, same)
            for s, tick in b.readers.items():
                add((s, tick), same)
        eng = self.eng[q]
        seen = self.seen[q]
        for s, tick in need.items():
            if seen.get(s, 0) >= tick:
                continue
            eng.wait_ge(s.h, tick)
            self.n_wait += 1
            seen[s] = tick

    def _record(self, sem, reads, writes):
        tick = sem.cnt
        for b in reads:
            if b.readers.get(sem, 0) < tick:
                b.readers[sem] = tick
        for b in writes:
            b.last_w = (sem, tick)
            b.readers = {}

    def op(self, q, fn, reads=(), writes=()):
        self._waits(q, reads, writes)
        ins = fn(self.eng[q])
        s = self.qsem[q]
        s.cnt += 1
        ins.then_inc(s.h, 1)
        self.n_inst += 1
        self._record(s, reads, writes)
        return ins

    def dma(self, q, out, in_, dsem, reads=(), writes=(), **kw):
        self._waits(q, reads, writes, skip=dsem)
        ins = self.eng[q].dma_start(out=out, in_=in_, **kw)
        dsem.cnt += 16
        ins.then_inc(dsem.h, 16)
        self.n_inst += 1
        self._record(dsem, reads, writes)
        return ins

    def dma_custom(self, q, fn, dsem, reads=(), writes=()):
        self._waits(q, reads, writes)
        ins = fn(self.eng[q])
        dsem.cnt += 16
        ins.then_inc(dsem.h, 16)
        self.n_inst += 1
        self._record(dsem, reads, writes)
        return ins

    def group_end(self, dsem, bufs):
        for b in bufs:
            b.last_w = (dsem, dsem.cnt)

    def finish(self, q="sp"):
        eng = self.eng[q]
        for s in list(self.qsem.values()) + self.dsems:
            if s.cnt > 0 and s is not self.qsem[q]:
                eng.wait_ge(s.h, s.cnt)

    def mm(self, out, lhsT, rhs, start, stop, reads, writes):
        return self.op("pe", lambda e: e.matmul(out, lhsT=lhsT, rhs=rhs, start=start, stop=stop),
                       reads=reads, writes=writes)

    def tr(self, out, in_, ident, reads, writes):
        return self.op("pe", lambda e: e.transpose(out, in_, ident), reads=reads, writes=writes)

    def act(self, out, in_, func, reads, writes, **kw):
        return self.op("act", lambda e: e.activation(out=out, in_=in_, func=func, **kw), reads=reads, writes=writes)

    def tt(self, out, in0, in1, op, reads, writes, q="dve"):
        return self.op(q, lambda e: e.tensor_tensor(out=out, in0=in0, in1=in1, op=op), reads=reads, writes=writes)

    def ts(self, out, in0, s1, s2, op0, op1, reads, writes, q="dve"):
        if s2 is None:
            return self.op(q, lambda e: e.tensor_scalar(out, in0, s1, None, op0=op0), reads=reads, writes=writes)
        return self.op(q, lambda e: e.tensor_scalar(out, in0, s1, s2, op0=op0, op1=op1), reads=reads, writes=writes)

    def stt(self, out, in0, scalar, in1, op0, op1, reads, writes):
        return self.op("dve", lambda e: e.scalar_tensor_tensor(out=out, in0=in0, scalar=scalar, in1=in1,
                                                               op0=op0, op1=op1), reads=reads, writes=writes)

    def copy(self, out, in_, reads, writes, q="dve"):
        return self.op(q, lambda e: e.tensor_copy(out, in_), reads=reads, writes=writes)

    def memset(self, ap, val, writes, q="dve"):
        return self.op(q, lambda e: e.memset(ap, val), writes=writes)


def build_program(stage="full"):
    nc = bass.Bass("TRN2", target_bir_lowering=False)

    def din(name, shape, dt=F32):
        return nc.dram_tensor(name, list(shape), dt, kind="ExternalInput").ap()

    xs_d = din("xs", [TALL, D])
    mem_d = din("mem", [256, D])
    vecs_d = din("vecs", [112, 128])
    convw_d = din("conv_w", [31, 1024])
    lre_d = din("lam_re", [32, 128])
    lim_d = din("lam_im", [32, 128])
    ldt_d = din("log_dt", [32, 2])
    bre_d = din("b_re", [32, 2048])
    bim_d = din("b_im", [32, 2048])
    cre_d = din("c_re", [32, 2048])
    cim_d = din("c_im", [32, 2048])
    w_in_d = din("w_in", [D, 3072])
    glu_d = din("glu_w", [1024, 1024])
    w_out_d = din("w_out", [D, D])
    w_q_d = din("w_q", [D, D])
    w_kv_d = din("w_kv", [D, 2 * D])
    w_o_d = din("w_o", [D, D])
    pwq_d = din("peer_w_q", [D, 1024])
    pkeys_d = din("peer_keys", [8, 2, 128, 64])
    pu_d = din("peer_u", [16384, D])
    pv_d = din("peer_v", [16384, D])
    fng_d = din("final_g", [1, D])
    y_d = nc.dram_tensor("y", [TOWN, D], F32, kind="ExternalOutput").ap()
    dbg_d = None
    if stage != "full":
        dbg_d = nc.dram_tensor("dbg", [TOWN, D], F32, kind="ExternalOutput").ap()

    puv_bf = nc.dram_tensor("puv_bf", [16384, 2 * D], BF16, kind="Internal").ap()
    tblU = Buf("tblUV")

    k = K(nc)

    ident_f = k.sbuf("ident_f", [128, 128], F32, side="right")
    ident_b = k.sbuf("ident_b", [128, 128], BF16, side="right")
    ones_r = k.sbuf("ones_r", [128, 128], F32R, side="right")
    iot = k.sbuf("iot", [128, 128], F32, side="right")
    vT = k.sbuf("vT", [128, 112], F32, side="right")
    pidx = k.sbuf("pidx", [128, 1], F32, side="right")
    negh = k.sbuf("negh", [128, 1], F32, side="right")
    psd = [k.psum(f"psd{i}", [128, 1024], F32) for i in range(4)]
    ps = [Buf(f"ps{i}", psd[i // 2][:, (i % 2) * 512:(i % 2 + 1) * 512]) for i in range(8)]
    d_setup = k.dsem("setup")
    d_out = k.dsem("out")

    k.op("pool", lambda e: e.iota(iot[:], pattern=[[1, 128]], base=0, channel_multiplier=-1,
                                  allow_small_or_imprecise_dtypes=True), writes=[iot])
    k.ts(ident_f[:], iot[:], 0.0, None, ALU.is_equal, None, [iot], [ident_f])
    k.copy(ident_b[:], ident_f[:], [ident_f], [ident_b])
    k.memset(iot[:], 1.0, [iot])
    k.copy(ones_r[:], iot[:], [iot], [ones_r])
    k.op("pool", lambda e: e.iota(pidx[:], pattern=[[0, 1]], base=0, channel_multiplier=1,
                                  allow_small_or_imprecise_dtypes=True), writes=[pidx])

    k.memset(negh[:], -0.5, [negh])
    m0 = k.mark()
    vraw = k.sbuf("vraw", [112, 128], F32)
    k.dma("sp", vraw[:], vecs_d, d_setup, writes=[vraw])
    k.tr(ps[0][:, 0:112], vraw[:], ident_f[0:112, 0:112], [vraw, ident_f], [ps[0]])
    k.copy(vT[:], ps[0][:, 0:112], [ps[0]], [vT])
    k.release(m0)
    G_MIX, G_MEM, G_MEMN, G_FFN = 0, 16, 32, 48
    V_CB, V_LNG, V_LNB, V_GC, V_GS, V_SD = 64, 72, 80, 88, 96, 104

    NB = {}
    NXB = 4
    d_x = [k.dsem(f"x{i}") for i in range(NXB)]
    cnt = {"x": 0}

    def alloc_norm_bufs(with_x=True):
        if with_x:
            NB["xtile"] = [k.sbuf(f"xtile{i}", [128, D], F32) for i in range(NXB)]
        NB["xsb"] = [k.sbuf(f"xsb{i}", [128, D], BF16) for i in range(2)]
        NB["junk"] = k.sbuf("junk", [128, D], BF16)
        NB["stat"] = [k.sbuf(f"stat{i}", [128, 4], F32) for i in range(2)]

    def load_tok_tile(src_ap):
        i = cnt["x"] % 2
        cnt["x"] += 1
        k.dma("sp", NB["xtile"][i][:], src_ap, d_x[i], writes=[NB["xtile"][i]])
        return i

    def pool_rsqrt(buf, out_ap, in_ap, scale, shape):
        k.ts(out_ap, in_ap, scale, EPS, ALU.mult, ALU.add, [buf], [buf], q="pool")
        nh = negh[:, 0:1] if shape[1] == 1 else negh[:, 0:1].to_broadcast(list(shape))
        k.tt(out_ap, out_ap, nh, ALU.pow, [buf, negh], [buf], q="pool")

    def rstd_of(src_buf, src_ap, st, n):
        junk = NB["junk"]
        k.act(junk[:, 0:n], src_ap, AF.Square, [src_buf], [junk, st], accum_out=st[:, 0:1])
        pool_rsqrt(st, st[:, 1:2], st[:, 0:1], 1.0 / n, [128, 1])

    def fold_g(wb, g_off, ncols):
        for kt in range(KT):
            k.ts(wb[:, kt, 0:ncols], wb[:, kt, 0:ncols], vT[:, g_off + kt:g_off + kt + 1], None, ALU.mult, None,
                 [wb, vT], [wb])

    def norm_A(src_buf, src_ap, i):
        st = NB["stat"][i]
        rstd_of(src_buf, src_ap, st, D)
        k.ts(NB["xsb"][i][:], src_ap, st[:, 1:2], None, ALU.mult, None, [src_buf, st], [NB["xsb"][i]])

    def norm_B(dstT_buf, dst_ap_fn, pbank, i):
        xsb = NB["xsb"]
        for half in range(2):
            pb = ps[pbank + half]
            pv = pb[:].bitcast(BF16)
            for j in range(8):
                kt = half * 8 + j
                k.tr(pv[:, j * 128:(j + 1) * 128], xsb[i][:, kt * 128:(kt + 1) * 128], ident_b[:],
                     [xsb[i], ident_b], [pb])
            pin = pv.rearrange("p (j t) -> p j t", j=8)
            if half == 0:
                k.act(dst_ap_fn(half), pin, AF.Copy, [pb], [dstT_buf])
            else:
                k.copy(dst_ap_fn(half), pin, [pb], [dstT_buf])

    def norm_pipeline(items, post=None):
        n = len(items)

        def is_dram(t):
            return not isinstance(items[t][0], tuple)

        def issue_load(t):
            if is_dram(t):
                i = t % NXB
                k.dma("sp", NB["xtile"][i][:], items[t][0], d_x[i], writes=[NB["xtile"][i]])

        def doA(t):
            if is_dram(t):
                xb = NB["xtile"][t % NXB]
                norm_A(xb, xb[:], t % 2)
            else:
                norm_A(items[t][0][0], items[t][0][1], t % 2)
        for t in range(min(NXB - 1, n)):
            issue_load(t)
        doA(0)
        for t in range(n):
            if t + NXB - 1 < n:
                issue_load(t + NXB - 1)
            if t + 1 < n:
                doA(t + 1)
            norm_B(items[t][1], items[t][2], 0, t % 2)
            if post is not None:
                post(t)

    uT = k.sbuf("uT", [128, 8, TALL], BF16)
    mA = k.mark()
    alloc_norm_bufs()
    xtile = NB["xtile"]
    w_ssm = k.sbuf("w_ssm", [128, KT, 1024], BF16)
    d_w = k.dsem("w0")
    for kt in range(KT):
        k.dma("pool", w_ssm[:, kt, :], w_in_d[kt * 128:(kt + 1) * 128, 2048:3072], d_w, writes=[w_ssm])
    k.group_end(d_w, [w_ssm])
    fold_g(w_ssm, G_MIX, 1024)
    xnTs = [k.sbuf(f"xnT{i}", [128, KT, BLK], BF16) for i in range(2)]
    def a1_src(r0):
        def f():
            i = load_tok_tile(xs_d[r0:r0 + 128, :])
            return xtile[i], xtile[i][:]
        return f

    def a1_post(t):
        if t % 4 != 3:
            return
        b = t // 4
        xnT = xnTs[b % 2]
        for ct in range(8):
            pb = ps[2 + ct % 2]
            for kt in range(KT):
                k.mm(pb[:], w_ssm[:, kt, ct * 128:(ct + 1) * 128], xnT[:, kt, :], kt == 0, kt == KT - 1,
                     [w_ssm, xnT], [pb])
            k.act(uT[:, ct, b * BLK:(b + 1) * BLK], pb[:], AF.Copy, [pb], [uT])

    items = []
    for b in range(NBLK):
        for tl in range(4):
            xnT = xnTs[b % 2]
            r0 = b * BLK + tl * 128
            items.append((xs_d[r0:r0 + 128, :], xnT,
                          lambda half, tl=tl, xnT=xnT: xnT.ap(half * 8 * BLK + tl * 128, [[BLK, 8], [1, 128]])))
    norm_pipeline(items, a1_post)
    k.release(mA)

    mS = k.mark()
    s5w = {nm: k.sbuf(nm, [128, 32, 128], BF16) for nm in ("Bre_pad", "Bim_pad", "WA", "NWA", "NWB")}
    parT = k.sbuf("parT", [128, 18, 32], F32)
    jvec = k.sbuf("jvec", [128, BLK], F32)
    k.op("pool", lambda e: e.iota(jvec[:], pattern=[[1, BLK]], base=0, channel_multiplier=0,
                                  allow_small_or_imprecise_dtypes=True), writes=[jvec])
    jrev = k.sbuf("jrev", [128, BLK], F32)
    k.op("pool", lambda e: e.iota(jrev[:], pattern=[[-1, BLK]], base=BLK - 1, channel_multiplier=0,
                                  allow_small_or_imprecise_dtypes=True), writes=[jrev])
    mP = k.mark()
    la = {nm: k.sbuf("la_" + nm, [32, 128], F32) for nm in
          ("lre", "lim", "dtb", "ar", "th", "rho", "phi", "f", "fa", "sinv", "cosv", "lbr", "lbi", "nr", "den",
           "t1", "t2", "cr", "ci", "f512", "g512", "ga512")}
    lai = k.sbuf("la_i", [32, 128], I32)
    par = k.sbuf("par", [32, 18, 128], F32)
    ldt = k.sbuf("ldt", [32, 2], F32)
    dt2 = k.sbuf("dt2", [32, 2], F32)
    k.dma("sp", la["lre"][:], lre_d, d_setup, writes=[la["lre"]])
    k.dma("sp", la["lim"][:], lim_d, d_setup, writes=[la["lim"]])
    k.dma("sp", ldt[:], ldt_d, d_setup, writes=[ldt])
    k.group_end(d_setup, [la["lre"], la["lim"], ldt, vraw])
    k.act(dt2[:], ldt[:], AF.Exp, [ldt], [dt2])
    k.copy(la["dtb"].ap(0, [[64, 2], [1, 64]]), dt2.ap(0, [[1, 2], [0, 64]]), [dt2], [la["dtb"]])

    def L(n):
        return la[n][:]
    k.tt(L("ar"), L("lre"), L("dtb"), ALU.mult, [la["lre"], la["dtb"]], [la["ar"]])
    k.tt(L("th"), L("lim"), L("dtb"), ALU.mult, [la["lim"], la["dtb"]], [la["th"]])
    k.act(L("rho"), L("ar"), AF.Exp, [la["ar"]], [la["rho"]])
    k.ts(L("phi"), L("th"), 1.0 / (2 * math.pi), None, ALU.mult, None, [la["th"]], [la["phi"]])

    def frac(dst, src):
        k.copy(lai[:], la[src][:], [la[src]], [lai])
        k.tt(la[dst][:], la[src][:], lai[:], ALU.subtract, [la[src], lai], [la[dst]])

    def sincos(sin_dst, cos_dst, fsrc, tmp):
        k.act(la[sin_dst][:], la[fsrc][:], AF.Sin, [la[fsrc]], [la[sin_dst]], scale=TWO_PI_S)
        k.act(la[tmp][:], la[fsrc][:], AF.Abs, [la[fsrc]], [la[tmp]])
        k.act(la[cos_dst][:], la[tmp][:], AF.Sin, [la[tmp]], [la[cos_dst]], scale=-TWO_PI_S, bias=math.pi / 2)

    frac("f", "phi")
    sincos("sinv", "cosv", "f", "fa")
    k.tt(L("lbr"), L("rho"), L("cosv"), ALU.mult, [la["rho"], la["cosv"]], [la["lbr"]])
    k.tt(L("lbi"), L("rho"), L("sinv"), ALU.mult, [la["rho"], la["sinv"]], [la["lbi"]])
    k.ts(L("nr"), L("lbr"), -1.0, None, ALU.add, None, [la["lbr"]], [la["nr"]])
    k.tt(L("t1"), L("lre"), L("lre"), ALU.mult, [la["lre"]], [la["t1"]])
    k.tt(L("t2"), L("lim"), L("lim"), ALU.mult, [la["lim"]], [la["t2"]])
    k.tt(L("den"), L("t1"), L("t2"), ALU.add, [la["t1"], la["t2"]], [la["den"]])
    k.op("dve", lambda e: e.reciprocal(L("den"), L("den")), reads=[la["den"]], writes=[la["den"]])
    k.tt(L("t1"), L("nr"), L("lre"), ALU.mult, [la["nr"], la["lre"]], [la["t1"]])
    k.tt(L("t2"), L("lbi"), L("lim"), ALU.mult, [la["lbi"], la["lim"]], [la["t2"]])
    k.tt(L("cr"), L("t1"), L("t2"), ALU.add, [la["t1"], la["t2"]], [la["cr"]])
    k.tt(L("cr"), L("cr"), L("den"), ALU.mult, [la["cr"], la["den"]], [la["cr"]])
    k.tt(L("t1"), L("lbi"), L("lre"), ALU.mult, [la["lbi"], la["lre"]], [la["t1"]])
    k.tt(L("t2"), L("nr"), L("lim"), ALU.mult, [la["nr"], la["lim"]], [la["t2"]])
    k.tt(L("ci"), L("t1"), L("t2"), ALU.subtract, [la["t1"], la["t2"]], [la["ci"]])
    k.tt(L("ci"), L("ci"), L("den"), ALU.mult, [la["ci"], la["den"]], [la["ci"]])
    k.ts(L("f512"), L("f"), float(BLK), None, ALU.mult, None, [la["f"]], [la["f512"]])
    frac("g512", "f512")
    k.copy(par[:, 0, :], L("f"), [la["f"]], [par])
    k.copy(par[:, 1, :], L("rho"), [la["rho"]], [par])
    sincos("t1", "t2", "g512", "ga512")
    k.copy(par[:, 2, :], L("t2"), [la["t2"]], [par])
    k.copy(par[:, 3, :], L("t1"), [la["t1"]], [par])
    k.ts(par[:, 4, :], L("t1"), -1.0, None, ALU.mult, None, [la["t1"]], [par])
    k.copy(par[:, 5, :], L("ar"), [la["ar"]], [par])
    for b_ in range(6):
        kk = 5 - b_
        k.ts(L("f512"), L("g512"), float(kk), None, ALU.mult, None, [la["g512"]], [la["f512"]])
        frac("ga512", "f512")
        sincos("t1", "t2", "ga512", "fa")
        k.act(L("lbr"), L("ar"), AF.Exp, [la["ar"]], [la["lbr"]], scale=float(BLK * kk))
        k.tt(par[:, 6 + b_, :], L("lbr"), L("t2"), ALU.mult, [la["lbr"], la["t2"]], [par])
        k.tt(par[:, 12 + b_, :], L("lbr"), L("t1"), ALU.mult, [la["lbr"], la["t1"]], [par])
    for i in range(18):
        pb_ = ps[i // 16]
        k.tr(pb_[:, (i % 16) * 32:(i % 16 + 1) * 32], par[:, i, :], ident_f[0:32, 0:32], [par, ident_f], [pb_])
    k.copy(parT[:, 0:16, :].rearrange("p a b -> p (a b)"), ps[0][:, 0:512], [ps[0]], [parT])
    k.copy(parT[:, 16:18, :].rearrange("p a b -> p (a b)"), ps[1][:, 0:64], [ps[1]], [parT])

    XT = {nm: k.sbuf("XT_" + nm, [128, 32, 16], F32) for nm in ("bre", "bim", "cre", "cim")}
    mP1 = k.mark()
    raw_re = k.sbuf("raw_re", [32, 2048], F32)
    raw_im = k.sbuf("raw_im", [32, 2048], F32)
    Bre = k.sbuf("Bre", [32, 2048], F32)
    Bim = k.sbuf("Bim", [32, 2048], F32)
    tb = k.sbuf("tb", [32, 2048], F32)
    d_raw = k.dsem("raw")
    k.dma("sp", raw_re[:], bre_d, d_raw, writes=[raw_re])
    k.dma("sp", raw_im[:], bim_d, d_raw, writes=[raw_im])
    k.group_end(d_raw, [raw_re, raw_im])

    def v3(buf):
        return buf.ap(0, [[16, 128], [1, 16]])

    def bc(buf):
        return buf.ap(0, [[1, 128], [0, 16]])
    k.tt(v3(Bre), v3(raw_re), bc(la["cr"]), ALU.mult, [raw_re, la["cr"]], [Bre])
    k.tt(v3(tb), v3(raw_im), bc(la["ci"]), ALU.mult, [raw_im, la["ci"]], [tb])
    k.tt(Bre[:], Bre[:], tb[:], ALU.subtract, [Bre, tb], [Bre])
    k.tt(v3(Bim), v3(raw_im), bc(la["cr"]), ALU.mult, [raw_im, la["cr"]], [Bim])
    k.tt(v3(tb), v3(raw_re), bc(la["ci"]), ALU.mult, [raw_re, la["ci"]], [tb])
    k.tt(Bim[:], Bim[:], tb[:], ALU.add, [Bim, tb], [Bim])

    def to_XT(nm, in_fn, rd):
        pb = ps[1]
        for h in range(16):
            k.tr(pb[:, h * 32:(h + 1) * 32], in_fn(h), ident_f[0:32, 0:32], rd + [ident_f], [pb])
        k.copy(XT[nm].ap(0, [[1, 16], [16, 32]]), pb[:].rearrange("p (h a) -> p h a", h=16), [pb], [XT[nm]])

    to_XT("bre", lambda h: Bre.ap(h, [[16, 128]]), [Bre])
    to_XT("bim", lambda h: Bim.ap(h, [[16, 128]]), [Bim])
    k.dma("sp", raw_re[:], cre_d, d_raw, writes=[raw_re])
    k.dma("sp", raw_im[:], cim_d, d_raw, writes=[raw_im])
    k.group_end(d_raw, [raw_re, raw_im])
    for nm, src in (("cre", raw_re), ("cim", raw_im)):
        k.copy(tb.ap(0, [[128, 16], [64, 2], [1, 64]]), src.ap(0, [[64, 16], [1024, 2], [1, 64]]), [src], [tb])
        to_XT(nm, lambda h: tb[:, h * 128:(h + 1) * 128], [tb])
    k.release(mP1)

    mask2 = k.sbuf("mask2", [128, 4], F32)
    k.ts(mask2[:, 0:1], pidx[:], 64.0, None, ALU.is_lt, None, [pidx], [mask2])
    k.ts(mask2[:, 1:2], pidx[:], 64.0, None, ALU.is_ge, None, [pidx], [mask2])
    k.ts(mask2[:, 2:4], mask2[:, 0:2], -1.0, None, ALU.mult, None, [mask2], [mask2])

    padf = k.sbuf("padf", [128, 32, 128], F32)

    def build_padded(dst_buf, src, neg, dt_direct):
        tgt = dst_buf if dt_direct else padf
        k.memset(tgt[:], 0.0, [tgt])
        for g2 in range(2):
            for b4 in range(4):
                o_ap = tgt.ap(b4 * 128 + b4 * 32 + 16 * g2, [[512, 8], [1, 16]])
                i_ap = src.ap(b4 * 16, [[64, 8], [1, 16]])
                mcol = mask2[:, (2 if neg else 0) + g2:(2 if neg else 0) + g2 + 1]
                k.ts(o_ap, i_ap, mcol, None, ALU.mult, None, [src, mask2], [tgt])

    build_padded(s5w["WA"], XT["cre"], False, True)
    build_padded(s5w["NWA"], XT["cre"], True, True)
    build_padded(s5w["NWB"], XT["cim"], True, True)
    for nm, dst in (("bre", "Bre_pad"), ("bim", "Bim_pad")):
        build_padded(None, XT[nm], False, False)
        for a in range(8):
            pb = ps[2 + a % 2]
            for b4 in range(4):
                k.tr(pb[:, b4 * 128:(b4 + 1) * 128], padf[:, a * 4 + b4, :], ident_f[:], [padf, ident_f], [pb])
            k.copy(s5w[dst][:, a * 4:(a + 1) * 4, :].rearrange("p a b -> p (a b)"), pb[:], [pb], [s5w[dst]])
    k.release(mP)


    mark_yn = k.mark("right")
    yn = k.sbuf("yn", [128, 16, TOWN], BF16, side="right")
    mZ = k.mark("right")
    zb = k.sbuf("zb", [128, 8, TOWN], BF16, side="right")
    tabC = [k.sbuf(f"tabC{i}", [128, BLK], F32) for i in range(2)]
    tabS = [k.sbuf(f"tabS{i}", [128, BLK], F32) for i in range(2)]
    tabWCS = [k.sbuf("tWCS", [128, 2 * BLK], F32)] * 2
    tabNSC = [k.sbuf("tNSC", [128, 2 * BLK], F32)] * 2
    g_fr = k.sbuf("g_fr", [128, BLK], F32)
    g_wt = g_fr
    pacc = k.sbuf("pacc", [128, 6, 4], F32)
    psr = k.sbuf("psr", [128, 2, 6], F32)
    pcc = k.sbuf("pcc", [128, 8], F32)
    j6 = k.sbuf("j6", [128, 6], F32)
    print("S5 loop: sbuf bytes remaining", nc.sbuf_bytes_remaining)
    angi = k.sbuf("angi", [128, BLK], I32)
    tmp = [k.sbuf(f"s5t{i}", [128, BLK], F32) for i in range(4)]
    ang, fr, fab = tmp[0], tmp[1], tmp[2]
    g_ang = tmp[1]
    g_fab = g_ang
    bh0 = [k.sbuf(f"bh{j}", [128, BLK], F32) for j in range(2)]
    zz0 = [k.sbuf(f"zz{j}", [128, BLK], F32) for j in range(2)]
    bh = [bh0, bh0]
    zz = [zz0, zz0]
    pp = [[k.sbuf(f"pp{i}{j}", [128, BLK], BF16) for j in range(4)] for i in range(2)]
    init = [k.sbuf(f"init{i}", [128, 2], F32) for i in range(2)]
    tini = k.sbuf("tini", [128, 2], F32)
    ytmp = tmp[0]
    d_cv = k.dsem("cv")
    CR = 512
    for c in range(16384 // CR):
        k.dma("pool", puv_bf[c * CR:(c + 1) * CR, 0:D], pu_d[c * CR:(c + 1) * CR, :], d_cv, writes=[tblU])
        k.dma("pool", puv_bf[c * CR:(c + 1) * CR, D:2 * D], pv_d[c * CR:(c + 1) * CR, :], d_cv, writes=[tblU])
    k.group_end(d_cv, [tblU])
    it = 0
    def gen_tables_front(P):
        sl = P % 2
        k.ts(g_ang[:], jvec[:], parT[:, 0, P:P + 1], None, ALU.mult, None, [jvec, parT], [g_ang])
        k.copy(angi[:], g_ang[:], [g_ang], [angi])
        k.tt(g_fr[:], g_ang[:], angi[:], ALU.subtract, [g_ang, angi], [g_fr])
        k.act(tabS[sl][:], g_fr[:], AF.Sin, [g_fr], [tabS[sl]], scale=TWO_PI_S)
        k.act(g_fab[:], g_fr[:], AF.Abs, [g_fr], [g_fab])
        k.act(tabC[sl][:], g_fab[:], AF.Sin, [g_fab], [tabC[sl]], scale=-TWO_PI_S, bias=math.pi / 2)
        k.act(g_wt[:], jrev[:], AF.Exp, [jrev, parT], [g_wt], scale=parT[:, 5, P:P + 1])

    def gen_tables_back(P):
        sl = P % 2
        wcs, nsc = tabWCS[sl], tabNSC[sl]
        k.tt(wcs[:, 0:BLK], tabC[sl][:], g_wt[:], ALU.mult, [tabC[sl], g_wt], [wcs])
        k.tt(wcs[:, BLK:2 * BLK], tabS[sl][:], g_wt[:], ALU.mult, [tabS[sl], g_wt], [wcs])
        k.ts(nsc[:, 0:BLK], wcs[:, BLK:2 * BLK], -1.0, None, ALU.mult, None, [wcs], [nsc])
        k.copy(nsc[:, BLK:2 * BLK], wcs[:, 0:BLK], [wcs], [nsc])

    gen_tables_front(0)
    gen_tables_back(0)
    for P in range(32):
        ct = P // 4
        tC, tS, tWCS, tNSC = tabC[P % 2], tabS[P % 2], tabWCS[P % 2], tabNSC[P % 2]
        if P + 1 < 32:
            gen_tables_front(P + 1)
        rho_b = parT[:, 1, P:P + 1].to_broadcast([128, BLK])
        NPB = NBLK - 2
        for b in range(NBLK):
            s = it % 2
            it += 1
            pre, pim = ps[4 + 2 * s], ps[5 + 2 * s]
            rhs = uT[:, ct, b * BLK:(b + 1) * BLK]
            k.mm(pre[:], s5w["Bre_pad"][:, P, :], rhs, True, True, [s5w["Bre_pad"], uT], [pre])
            k.mm(pim[:], s5w["Bim_pad"][:, P, :], rhs, True, True, [s5w["Bim_pad"], uT], [pim])
            if b < NPB:
                for q_, tw in enumerate((tWCS, tNSC)):
                    k.op("dve", lambda e, tw=tw, q_=q_: e.scalar_tensor_tensor(
                        out=tmp[0][:].bitcast(BF16), in0=tw[:], scalar=1.0, in1=psd[2 + s][:], op0=ALU.mult,
                        op1=ALU.mult, accum_out=psr[:, q_, b:b + 1]), reads=[tw, pre, pim], writes=[tmp[0], psr])
                if b == NPB - 1:
                    mre = parT.ap(6 * 32 + P, [[32, 6]])
                    mim = parT.ap(12 * 32 + P, [[32, 6]])
                    for q_, (sv, mw) in enumerate(((0, mre), (1, mim), (1, mre), (0, mim))):
                        k.op("dve", lambda e, sv=sv, mw=mw, q_=q_: e.scalar_tensor_tensor(
                            out=j6[:], in0=psr[:, sv, :], scalar=1.0, in1=mw, op0=ALU.mult, op1=ALU.mult,
                            accum_out=pcc[:, q_:q_ + 1]), reads=[psr, parT], writes=[j6, pcc])
                    k.tt(pcc[:, 4:5], pcc[:, 0:1], pcc[:, 1:2], ALU.subtract, [pcc], [pcc])
                    k.tt(pcc[:, 5:6], pcc[:, 2:3], pcc[:, 3:4], ALU.add, [pcc], [pcc])
                    nin = init[(b + 1) % 2]
                    c5, s5c, ns5 = parT[:, 2, P:P + 1], parT[:, 3, P:P + 1], parT[:, 4, P:P + 1]
                    k.tt(tini[:, 0:1], pcc[:, 4:5], c5, ALU.mult, [pcc, parT], [tini])
                    k.tt(tini[:, 1:2], pcc[:, 5:6], c5, ALU.mult, [pcc, parT], [tini])
                    k.stt(nin[:, 0:1], pcc[:, 5:6], ns5, tini[:, 0:1], ALU.mult, ALU.add, [pcc, parT, tini], [nin])
                    k.stt(nin[:, 1:2], pcc[:, 4:5], s5c, tini[:, 1:2], ALU.mult, ALU.add, [pcc, parT, tini], [nin])
                continue
            k.tt(tmp[0][:], tC[:], pre[:], ALU.mult, [tC, pre], [tmp[0]])
            k.tt(tmp[1][:], tS[:], pim[:], ALU.mult, [tS, pim], [tmp[1]])
            k.tt(bh[s][0][:], tmp[0][:], tmp[1][:], ALU.add, [tmp[0], tmp[1]], [bh[s][0]])
            k.tt(tmp[2][:], tC[:], pim[:], ALU.mult, [tC, pim], [tmp[2]])
            k.tt(tmp[3][:], tS[:], pre[:], ALU.mult, [tS, pre], [tmp[3]])
            k.tt(bh[s][1][:], tmp[2][:], tmp[3][:], ALU.subtract, [tmp[2], tmp[3]], [bh[s][1]])
            ini = init[b % 2]
            for c in range(2):
                iv = ini[:, c:c + 1]
                rd = [parT, bh[s][c], ini]
                k.op("dve", lambda e, c=c, iv=iv: e.tensor_tensor_scan(
                    out=zz[s][c][:], data0=rho_b, data1=bh[s][c][:], initial=iv, op0=ALU.mult, op1=ALU.add),
                    reads=rd, writes=[zz[s][c]])
            if b < NBLK - 1:
                nin = init[(b + 1) % 2]
                e_re, e_im = zz[s][0][:, BLK - 1:BLK], zz[s][1][:, BLK - 1:BLK]
                c5, s5c, ns5 = parT[:, 2, P:P + 1], parT[:, 3, P:P + 1], parT[:, 4, P:P + 1]
                k.tt(tini[:, 0:1], e_re, c5, ALU.mult, [zz[s][0], parT], [tini])
                k.tt(tini[:, 1:2], e_im, c5, ALU.mult, [zz[s][1], parT], [tini])
                k.stt(nin[:, 0:1], e_im, ns5, tini[:, 0:1], ALU.mult, ALU.add, [zz[s][1], parT, tini], [nin])
                k.stt(nin[:, 1:2], e_re, s5c, tini[:, 1:2], ALU.mult, ALU.add, [zz[s][0], parT, tini], [nin])
            if b >= NBLK - 2:
                ob = b - (NBLK - 2)
                yb = ps[2 + ob]
                k.tt(pp[s][0][:], tC[:], zz[s][0][:], ALU.mult, [tC, zz[s][0]], [pp[s][0]])
                k.tt(pp[s][1][:], tS[:], zz[s][1][:], ALU.mult, [tS, zz[s][1]], [pp[s][1]])
                k.tt(pp[s][2][:], tS[:], zz[s][0][:], ALU.mult, [tS, zz[s][0]], [pp[s][2]])
                k.tt(pp[s][3][:], tC[:], zz[s][1][:], ALU.mult, [tC, zz[s][1]], [pp[s][3]])
                wl = ["WA", "NWA", "NWB", "NWB"]
                for j in range(4):
                    k.mm(yb[:], s5w[wl[j]][:, P, :], pp[s][j][:], (P % 4 == 0 and j == 0), (P % 4 == 3 and j == 3),
                         [s5w[wl[j]], pp[s][j]], [yb])
                if P % 4 == 3:
                    k.stt(ytmp[:], uT[:, ct, b * BLK:(b + 1) * BLK], vT[:, V_SD + ct:V_SD + ct + 1], yb[:],
                          ALU.mult, ALU.add, [uT, vT, yb], [ytmp])
                    k.act(zb[:, ct, ob * BLK:(ob + 1) * BLK], ytmp[:], AF.Gelu_apprx_tanh, [ytmp], [zb])

        if P + 1 < 32:
            gen_tables_back(P + 1)

    if stage == "s5":
        obuf = k.sbuf("obuf", [128, D], F32)
        for tt_ in range(8):
            k.memset(obuf[:], 0.0, [obuf])
            for ct in range(8):
                pvb = ps[0][:].bitcast(BF16)
                k.tr(pvb[:, 0:128], zb[:, ct, tt_ * 128:(tt_ + 1) * 128], ident_b[:], [zb, ident_b], [ps[0]])
                k.copy(obuf[:, ct * 128:(ct + 1) * 128], pvb[:, 0:128], [ps[0]], [obuf])
            k.dma("sp", dbg_d[tt_ * 128:(tt_ + 1) * 128, :], obuf[:], d_out, reads=[obuf])
            k.dma("sp", y_d[tt_ * 128:(tt_ + 1) * 128, :], obuf[:], d_out, reads=[obuf])
        k.finish("sp")
        print("instructions", k.n_inst, "waits", k.n_wait)
        return nc

    k.release(mS)
    k.release(0)

    def chan_rstd(dst, src_sq_fn, nct, blk, bank, extra_mean=None):
        pb = ps[bank]
        for ct in range(nct):
            sqb = src_sq_fn(ct)
            k.mm(pb[:], ones_r[:], sqb[:], ct == 0, ct == nct - 1, [ones_r, sqb], [pb])

    glu_w = k.sbuf("glu_w", [128, 8, 1024], BF16, side="right")
    d_w1 = k.dsem("w1")
    for kt in range(8):
        k.dma("pool", glu_w[:, kt, :], glu_d[kt * 128:(kt + 1) * 128, :], d_w1, writes=[glu_w])
    k.group_end(d_w1, [glu_w])
    w_ag = k.sbuf("w_ag", [128, KT, 2048], BF16)
    d_w2 = k.dsem("w2")
    for kt in range(KT):
        k.dma("pool", w_ag[:, kt, :], w_in_d[kt * 128:(kt + 1) * 128, 0:2048], d_w2, writes=[w_ag])
    k.group_end(d_w2, [w_ag])
    mG = k.mark()
    ysf = k.sbuf("ysf", [128, 8, TOWN], F32)
    sg = [k.sbuf(f"sg{i}", [128, BLK], F32) for i in range(2)]
    sq = [k.sbuf(f"sq{i}", [128, BLK], F32R) for i in range(2)]
    rsd = k.sbuf("rsd", [128, BLK], F32)
    tmpn = k.sbuf("tmpn", [128, BLK], F32)
    for blk in range(2):
        tsl = slice(blk * BLK, (blk + 1) * BLK)
        for ct in range(8):
            pb = ps[ct % 2]
            for kt in range(8):
                k.mm(pb[:], glu_w[:, kt, ct * 128:(ct + 1) * 128], zb[:, kt, tsl], kt == 0, kt == 7, [glu_w, zb], [pb])
            k.act(sg[ct % 2][:], pb[:], AF.Sigmoid, [pb], [sg[ct % 2]])
            k.tt(ysf[:, ct, tsl], zb[:, ct, tsl], sg[ct % 2][:], ALU.mult, [zb, sg[ct % 2]], [ysf])
        pb = ps[2]
        for ct in range(8):
            k.act(sq[ct % 2][:], ysf[:, ct, tsl], AF.Square, [ysf], [sq[ct % 2]])
            k.mm(pb[:], ones_r[:], sq[ct % 2][:], ct == 0, ct == 7, [ones_r, sq[ct % 2]], [pb])
        k.act(rsd[:], pb[:], AF.Sqrt, [pb], [rsd], scale=1.0 / 1024, bias=EPS)
        k.op("dve", lambda e: e.reciprocal(rsd[:], rsd[:]), reads=[rsd], writes=[rsd])
        for ct in range(8):
            k.stt(yn[:, 8 + ct, tsl], ysf[:, ct, tsl], vT[:, V_GS + ct:V_GS + ct + 1], rsd[:], ALU.mult, ALU.mult,
                  [ysf, vT, rsd], [yn])
    k.release(mG)
    k.release(mZ, "right")

    mCr = k.mark("right")
    HAL = 30
    cbuf = k.sbuf("cbuf", [128, 8, HAL + TOWN + 2], BF16, side="right")
    cw = k.sbuf("cw", [128, 8, 31], F32, side="right")
    cwraw = k.sbuf("cwraw", [31, 1024], F32)
    k.dma("sp", cwraw[:], convw_d, d_setup, writes=[cwraw])
    k.group_end(d_setup, [cwraw])
    for ct in range(8):
        k.tr(ps[0][:, ct * 32:ct * 32 + 31], cwraw[:, ct * 128:(ct + 1) * 128], ident_f[0:31, 0:31], [cwraw, ident_f], [ps[0]])
    k.copy(cw[:], ps[0][:, 0:256].rearrange("p (c k) -> p c k", c=8)[:, :, 0:31], [ps[0]], [cw])
    alloc_norm_bufs()
    xtile = NB["xtile"]
    fold_g(w_ag, G_MIX, 2048)
    xnT = k.sbuf("xnT2", [128, KT, BLK], BF16)
    sgc = [k.sbuf(f"sgc{i}", [128, BLK], F32) for i in range(2)]
    def cv_post(t):
        if t % 4 != 3:
            return
        b = 5 + t // 4
        for ct in range(8):
            pa, pg = ps[2 + 2 * (ct % 2)], ps[3 + 2 * (ct % 2)]
            for kt in range(KT):
                k.mm(pg[:], w_ag[:, kt, 1024 + ct * 128:1024 + (ct + 1) * 128], xnT[:, kt, :], kt == 0, kt == KT - 1,
                     [w_ag, xnT], [pg])
            for kt in range(KT):
                k.mm(pa[:], w_ag[:, kt, ct * 128:(ct + 1) * 128], xnT[:, kt, :], kt == 0, kt == KT - 1,
                     [w_ag, xnT], [pa])
            k.act(sgc[ct % 2][:], pg[:], AF.Sigmoid, [pg], [sgc[ct % 2]])
            if b == 5:
                k.tt(cbuf[:, ct, 0:HAL], pa[:, BLK - HAL:BLK], sgc[ct % 2][:, BLK - HAL:BLK], ALU.mult,
                     [pa, sgc[ct % 2]], [cbuf])
            else:
                o = HAL + (b - 6) * BLK
                k.tt(cbuf[:, ct, o:o + BLK], pa[:], sgc[ct % 2][:], ALU.mult, [pa, sgc[ct % 2]], [cbuf])

    items = []
    for b in range(5, 8):
        for tl in range(4):
            r0 = b * BLK + tl * 128
            items.append((xs_d[r0:r0 + 128, :], xnT,
                          lambda half, tl=tl: xnT.ap(half * 8 * BLK + tl * 128, [[BLK, 8], [1, 128]])))
    norm_pipeline(items, cv_post)
    k.release(0)
    cv = k.sbuf("cv", [128, 8, TOWN], F32)
    dg = k.sbuf("dg", [128, 8, 31, 128], BF16, side="right")
    for ct in range(8):
        for kk in range(31):
            k.ts(dg[:, ct, kk, :], ident_f[:], cw[:, ct, kk:kk + 1], None, ALU.mult, None, [ident_f, cw], [dg],
                 q="dve")
    for ct in range(8):
        for blk in range(2):
            pb = ps[2 + (ct * 2 + blk) % 4]
            for kk in range(31):
                k.mm(pb[:], dg[:, ct, kk, :], cbuf[:, ct, blk * BLK + kk:blk * BLK + kk + BLK], kk == 0, kk == 30,
                     [dg, cbuf], [pb])
            k.act(cv[:, ct, blk * BLK:(blk + 1) * BLK], pb[:], AF.Identity, [pb, vT], [cv],
                  bias=vT[:, V_CB + ct:V_CB + ct + 1])
    k.release(mCr, "right")
    wbuf = k.sbuf("wbuf", [128, KT, D], BF16, side="right")
    d_w3 = k.dsem("w3")
    for kt in range(KT):
        k.dma("pool", wbuf[:, kt, :], w_out_d[kt * 128:(kt + 1) * 128, :], d_w3, writes=[wbuf])
    k.group_end(d_w3, [wbuf])
    sqc = [k.sbuf(f"sqc{i}", [128, BLK], F32R) for i in range(2)]
    cvr = [k.sbuf(f"cvr{i}", [128, BLK], F32R) for i in range(2)]
    mean = k.sbuf("mean", [128, BLK], F32)
    var = k.sbuf("var", [128, BLK], F32)
    m2 = k.sbuf("m2", [128, BLK], F32)
    rsd2 = k.sbuf("rsd2", [128, BLK], F32)
    for blk in range(2):
        tsl = slice(blk * BLK, (blk + 1) * BLK)
        p1, p2 = ps[0], ps[1]
        for ct in range(8):
            k.copy(cvr[ct % 2][:], cv[:, ct, tsl], [cv], [cvr[ct % 2]], q="pool")
            k.act(sqc[ct % 2][:], cv[:, ct, tsl], AF.Square, [cv], [sqc[ct % 2]])
            k.mm(p1[:], ones_r[:], cvr[ct % 2][:], ct == 0, ct == 7, [ones_r, cvr[ct % 2]], [p1])
            k.mm(p2[:], ones_r[:], sqc[ct % 2][:], ct == 0, ct == 7, [ones_r, sqc[ct % 2]], [p2])
        k.act(mean[:], p1[:], AF.Copy, [p1], [mean], scale=1.0 / 1024)
        k.tt(m2[:], mean[:], mean[:], ALU.mult, [mean], [m2])
        k.stt(var[:], p2[:], 1.0 / 1024, m2[:], ALU.mult, ALU.subtract, [p2, m2], [var])
        k.act(var[:], var[:], AF.Sqrt, [var], [var], bias=EPS)
        k.op("dve", lambda e: e.reciprocal(var[:], var[:]), reads=[var], writes=[var])
        for ct in range(8):
            k.tt(cv[:, ct, tsl], cv[:, ct, tsl], mean[:], ALU.subtract, [cv, mean], [cv])
            k.tt(cv[:, ct, tsl], cv[:, ct, tsl], var[:], ALU.mult, [cv, var], [cv])
            k.act(cv[:, ct, tsl], cv[:, ct, tsl], AF.Silu, [cv, vT], [cv],
                  scale=vT[:, V_LNG + ct:V_LNG + ct + 1], bias=vT[:, V_LNB + ct:V_LNB + ct + 1])
        p3 = ps[2]
        for ct in range(8):
            k.act(sqc[ct % 2][:], cv[:, ct, tsl], AF.Square, [cv], [sqc[ct % 2]])
            k.mm(p3[:], ones_r[:], sqc[ct % 2][:], ct == 0, ct == 7, [ones_r, sqc[ct % 2]], [p3])
        k.act(rsd2[:], p3[:], AF.Sqrt, [p3], [rsd2], scale=1.0 / 1024, bias=EPS)
        k.op("dve", lambda e: e.reciprocal(rsd2[:], rsd2[:]), reads=[rsd2], writes=[rsd2])
        for ct in range(8):
            k.stt(yn[:, ct, tsl], cv[:, ct, tsl], vT[:, V_GC + ct:V_GC + ct + 1], rsd2[:], ALU.mult, ALU.mult,
                  [cv, vT, rsd2], [yn])
    k.release(0)

    h = k.sbuf("h", [128, 8, D], F32)
    mW = k.mark()
    xres = [k.sbuf(f"xres{i}", [128, D], F32) for i in range(2)]
    d_xr = [k.dsem(f"xr{i}") for i in range(2)]
    for t in range(8):
        xr = xres[t % 2]
        k.dma("sp", xr[:], xs_d[TALL - TOWN + t * 128:TALL - TOWN + (t + 1) * 128, :], d_xr[t % 2], writes=[xr])
        for n in range(4):
            pb = ps[n % 2]
            for kt in range(KT):
                k.mm(pb[:], yn[:, kt, t * 128:(t + 1) * 128], wbuf[:, kt, n * 512:(n + 1) * 512], kt == 0, kt == KT - 1,
                     [yn, wbuf], [pb])
            k.tt(h[:, t, n * 512:(n + 1) * 512], pb[:], xr[:, n * 512:(n + 1) * 512], ALU.add, [pb, xr], [h])
    k.release(mW)
    k.release(mark_yn, "right")

    def dump_h(dst_list):
        for t in range(8):
            for dd in dst_list:
                k.dma("sp", dd[t * 128:(t + 1) * 128, :], h[:, t, :], d_out, reads=[h])

    if stage == "h1":
        dump_h([dbg_d, y_d])
        k.finish("sp")
        print("instructions", k.n_inst, "waits", k.n_wait)
        return nc

    mE = k.mark()
    KTb = k.sbuf("KTb", [128, 16, 256], BF16)
    Vb = k.sbuf("Vb", [128, 2, D], BF16)
    wch = [k.sbuf(f"wch{i}", [128, KT, 512], BF16) for i in range(2)]
    d_wc = [k.dsem(f"wc{i}") for i in range(2)]
    wcnt = {"n": 0}

    def load_wchunk(w_d, c0, g_off=None):
        i = wcnt["n"] % 2
        wcnt["n"] += 1
        for q4 in range(4):
            src = w_d[q4 * 512:(q4 + 1) * 512, c0:c0 + 512].rearrange("(kt p) c -> p kt c", p=128)
            k.dma("pool", wch[i][:, q4 * 4:(q4 + 1) * 4, :], src, d_wc[i], writes=[wch[i]])
        k.group_end(d_wc[i], [wch[i]])
        if g_off is not None:
            fold_g(wch[i], g_off, 512)
        return wch[i]

    mE1 = k.mark()
    alloc_norm_bufs()
    xtile = NB["xtile"]
    memT = k.sbuf("memT", [128, KT, 256], BF16)
    def mem_src(mt):
        def f():
            i = load_tok_tile(mem_d[mt * 128:(mt + 1) * 128, :])
            return xtile[i], xtile[i][:]
        return f
    norm_pipeline([(mem_d[mt * 128:(mt + 1) * 128, :], memT,
                    lambda half, mt=mt: memT.ap(half * 8 * 256 + mt * 128, [[256, 8], [1, 128]])) for mt in range(2)])
    for c in range(8):
        wc = load_wchunk(w_kv_d, c * 512, G_MEMN)
        if c < 4:
            for j in range(4):
                pb = ps[2 + j % 2]
                for kt in range(KT):
                    k.mm(pb[:, 0:256], wc[:, kt, j * 128:(j + 1) * 128], memT[:, kt, :], kt == 0, kt == KT - 1,
                         [wc, memT], [pb])
                k.act(KTb[:, c * 4 + j, :], pb[:, 0:256], AF.Copy, [pb], [KTb])
        else:
            for mt in range(2):
                pb = ps[2 + mt]
                for kt in range(KT):
                    k.mm(pb[:], memT[:, kt, mt * 128:(mt + 1) * 128], wc[:, kt, :], kt == 0, kt == KT - 1,
                         [wc, memT], [pb])
                k.act(Vb[:, mt, (c - 4) * 512:(c - 3) * 512], pb[:], AF.Copy, [pb], [Vb])
    k.release(mE1)
    qT = k.sbuf("qT", [128, 16, TOWN], BF16)
    hnT = k.sbuf("hnT", [128, 16, TOWN], BF16)
    mE2 = k.mark()
    alloc_norm_bufs(False)
    norm_pipeline([((h, h[:, t, :]), hnT,
                    lambda half, t=t: hnT.ap(half * 8 * TOWN + t * 128, [[TOWN, 8], [1, 128]])) for t in range(8)])
    k.release(mE2)
    qscale = 1.0 / math.sqrt(512.0)
    for c in range(4):
        wc = load_wchunk(w_q_d, c * 512, G_MEM)
        for j in range(4):
            for blk in range(2):
                pb = ps[2 + (2 * j + blk) % 2]
                for kt in range(KT):
                    k.mm(pb[:], wc[:, kt, j * 128:(j + 1) * 128], hnT[:, kt, blk * BLK:(blk + 1) * BLK], kt == 0,
                         kt == KT - 1, [wc, hnT], [pb])
                k.act(qT[:, c * 4 + j, blk * BLK:(blk + 1) * BLK], pb[:], AF.Copy, [pb], [qT], scale=qscale)
    PT = k.sbuf("PT", [128, 8, TOWN], BF16)
    pex = [k.sbuf(f"pex{i}", [128, 256], F32) for i in range(2)]
    pnb = [k.sbuf(f"pnb{i}", [128, 256], BF16) for i in range(2)]
    sst = [k.sbuf(f"sst{i}", [128, 4], F32) for i in range(2)]
    def att_scores(it_):
        hd, t = divmod(it_, 8)
        pb = ps[2 + it_ % 2]
        for j in range(4):
            k.mm(pb[:, 0:256], qT[:, hd * 4 + j, t * 128:(t + 1) * 128], KTb[:, hd * 4 + j, :], j == 0, j == 3,
                 [qT, KTb], [pb])

    def att_softmax(it_):
        hd, t = divmod(it_, 8)
        i = it_ % 2
        pb = ps[2 + i]
        st = sst[i]
        k.op("dve", lambda e: e.tensor_reduce(out=st[:, 0:1], in_=pb[:, 0:256], axis=AX.X, op=ALU.max),
             reads=[pb], writes=[st])
        k.ts(st[:, 1:2], st[:, 0:1], -1.0, None, ALU.mult, None, [st], [st])
        k.act(pex[i][:], pb[:, 0:256], AF.Exp, [pb, st], [pex[i], st], bias=st[:, 1:2], accum_out=st[:, 2:3])
        k.op("dve", lambda e: e.reciprocal(st[:, 3:4], st[:, 2:3]), reads=[st], writes=[st])
        k.ts(pnb[i][:], pex[i][:], st[:, 3:4], None, ALU.mult, None, [pex[i], st], [pnb[i]])
        pt = ps[4 + i]
        ptv = pt[:].bitcast(BF16)
        for mt in range(2):
            k.tr(ptv[:, mt * 128:(mt + 1) * 128], pnb[i][:, mt * 128:(mt + 1) * 128], ident_b[:], [pnb[i], ident_b], [pt])
        k.copy(PT.ap(hd * 2 * TOWN + t * 128, [[TOWN, 2], [1, 128]]),
               ptv[:, 0:256].rearrange("p (m t) -> p m t", m=2), [pt], [PT])

    att_scores(0)
    for it_ in range(32):
        if it_ + 1 < 32:
            att_scores(it_ + 1)
        att_softmax(it_)
    OT = hnT
    for hd in range(4):
        for dvt in range(4):
            for blk in range(2):
                pb = ps[2 + (dvt * 2 + blk) % 2]
                for mt in range(2):
                    k.mm(pb[:], Vb[:, mt, (hd * 4 + dvt) * 128:(hd * 4 + dvt + 1) * 128],
                         PT[:, hd * 2 + mt, blk * BLK:(blk + 1) * BLK], mt == 0, mt == 1, [Vb, PT], [pb])
                k.act(OT[:, hd * 4 + dvt, blk * BLK:(blk + 1) * BLK], pb[:], AF.Copy, [pb], [OT])
    for n in range(4):
        wc = load_wchunk(w_o_d, n * 512)
        for t in range(8):
            pb = ps[2 + t % 2]
            for ct in range(KT):
                k.mm(pb[:], OT[:, ct, t * 128:(t + 1) * 128], wc[:, ct, :], ct == 0, ct == KT - 1, [OT, wc], [pb])
            k.tt(h[:, t, n * 512:(n + 1) * 512], h[:, t, n * 512:(n + 1) * 512], pb[:], ALU.add, [h, pb], [h])
    k.release(mE)

    if stage == "h2":
        dump_h([dbg_d, y_d])
        k.finish("sp")
        print("instructions", k.n_inst, "waits", k.n_wait)
        return nc

    idxT = k.sbuf("idxT", [128, TOWN], U32)
    gT = k.sbuf("gT", [128, TOWN], F32)
    hn3b = k.sbuf("hn3b", [128, 8, D], BF16)
    rstd3 = k.sbuf("rstd3", [128, 16], F32)
    mF0 = k.mark()
    gffn_b = k.sbuf("gffn_b", [128, D], F32)
    k.dma("sp", gffn_b[:], bass.AP(tensor=vecs_d.tensor, offset=G_FFN * 128, ap=[[0, 128], [1, D]]), d_setup,
          writes=[gffn_b])
    k.group_end(d_setup, [gffn_b])
    xs0 = k.sbuf("xs0", [128, D], F32)
    for t in range(8):
        k.act(xs0[:], h[:, t, :], AF.Square, [h], [xs0, rstd3], accum_out=rstd3[:, 8 + t:9 + t])
        pool_rsqrt(rstd3, rstd3[:, t:t + 1], rstd3[:, 8 + t:9 + t], 1.0 / D, [128, 1])
        k.stt(hn3b[:, t, :], h[:, t, :], rstd3[:, t:t + 1], gffn_b[:], ALU.mult, ALU.mult, [h, rstd3, gffn_b], [hn3b])
    k.release(mF0)
    mF = k.mark()
    import os
    RL = float(os.environ.get("RL", "99")) if stage == "route" else 99

    def route_body():
        idxf = k.sbuf("idxf", [128, 8, 128], F32)
        gtf = k.sbuf("gtf", [128, 8, 128], F32)
        keysT = k.sbuf("keysT", [128, 8, 128], F32)
        kraw = k.sbuf("kraw", [128, 128], F32)
        d_kr = k.dsem("kr")
        for hq in range(8):
            k.dma("sp", kraw[:].rearrange("n (s c) -> n s c", s=2), pkeys_d[hq].rearrange("s n c -> n s c"), d_kr,
                  writes=[kraw])
            k.tr(ps[0][:, 0:128], kraw[:], ident_f[:], [kraw, ident_f], [ps[0]])
            k.copy(keysT[:, hq, :], ps[0][:, 0:128], [ps[0]], [keysT])
        if RL <= 1:
            return
        iota16 = k.sbuf("iota16", [128, 16], F32)
        k.op("pool", lambda e: e.iota(iota16[:], pattern=[[1, 16]], base=0, channel_multiplier=0,
                                      allow_small_or_imprecise_dtypes=True), writes=[iota16])
        pwq = k.sbuf("pwq", [128, KT, 512], F32R)
        wsts = [k.sbuf(f"wst{i}", [128, 512], F32) for i in range(2)]
        d_wss = [k.dsem(f"ws{i}") for i in range(2)]
        xs32 = k.sbuf("xs32", [128, D], F32)
        hnT3s = [k.sbuf(f"hnT3{i}", [128, KT, 128], F32R) for i in range(2)]
        qTts = [k.sbuf(f"qTt{i}", [128, 4, 128], F32) for i in range(2)]
        scss = [k.sbuf(f"scs{i}", [128, 8, 128], F32) for i in range(2)]
        NCH = 4
        SC = []
        for c_ in range(NCH):
            SC.append(dict(
                scw=k.sbuf(f"scw{c_}", [128, 256], F32), cand=k.sbuf(f"cand{c_}", [128, 256], F32),
                s1v=k.sbuf(f"s1v{c_}", [128, 2, 16], F32), s1i=k.sbuf(f"s1i{c_}", [128, 2, 16], F32),
                iu=k.sbuf(f"iu{c_}", [128, 16], U32), tv=k.sbuf(f"tv{c_}", [128, 16], F32),
                posf=k.sbuf(f"posf{c_}", [128, 16], F32), pai=k.sbuf(f"pai{c_}", [128, 16], I32),
                paf=k.sbuf(f"paf{c_}", [128, 16], F32), pbf=k.sbuf(f"pbf{c_}", [128, 16], F32),
                eq=k.sbuf(f"eq{c_}", [128, 16, 16], F32), sel=k.sbuf(f"sel{c_}", [128, 2, 16], F32),
                rst=k.sbuf(f"rst{c_}", [128, 8], F32), ex=k.sbuf(f"ex{c_}", [128, 16], F32)))

        def top16(B, vals_buf, vals_ap, src_buf, src_ap, idx_buf, idx_ap, n):
            scw, iu = B["scw"], B["iu"]
            k.op("dve", lambda e: e.max(out=vals_ap[:, 0:8], in_=src_ap), reads=[src_buf], writes=[vals_buf])
            yield
            k.op("dve", lambda e: e.match_replace(out=scw[:, 0:n], in_to_replace=vals_ap[:, 0:8], in_values=src_ap,
                                                  imm_value=-1e30), reads=[src_buf, vals_buf], writes=[scw])
            yield
            k.op("dve", lambda e: e.max(out=vals_ap[:, 8:16], in_=scw[:, 0:n]), reads=[scw], writes=[vals_buf])
            yield
            k.op("dve", lambda e: e.max_index(out=iu[:, 0:8], in_max=vals_ap[:, 0:8], in_values=src_ap),
                 reads=[src_buf, vals_buf], writes=[iu])
            yield
            k.op("dve", lambda e: e.max_index(out=iu[:, 8:16], in_max=vals_ap[:, 8:16], in_values=src_ap),
                 reads=[src_buf, vals_buf], writes=[iu])
            yield
            k.copy(idx_ap, iu[:], [iu], [idx_buf])
            yield

        def head_chain(B, scs, hh, hq, t):
            s1v, s1i, cand, tv, posf = B["s1v"], B["s1i"], B["cand"], B["tv"], B["posf"]
            rst, ex, paf, pai, pbf, eq, sel = B["rst"], B["ex"], B["paf"], B["pai"], B["pbf"], B["eq"], B["sel"]
            for sd in range(2):
                yield from top16(B, s1v, s1v[:, sd, :], scs, scs[:, hh * 2 + sd, :], s1i, s1i[:, sd, :], 128)
            k.tt(cand[:].rearrange("p (a b) -> p a b", a=16), s1v.ap(0, [[1, 16], [0, 16]]),
                 s1v.ap(16, [[0, 16], [1, 16]]), ALU.add, [s1v], [cand])
            yield
            yield from top16(B, tv, tv[:], cand, cand[:], posf, posf[:], 256)
            k.ts(rst[:, 2:3], tv[:, 0:1], -1.0, None, ALU.mult, None, [tv], [rst])
            yield
            k.act(ex[:], tv[:], AF.Exp, [tv, rst], [ex, rst], bias=rst[:, 2:3], accum_out=rst[:, 3:4])
            yield
            k.op("dve", lambda e: e.reciprocal(rst[:, 4:5], rst[:, 3:4]), reads=[rst], writes=[rst])
            yield
            k.ts(gtf[:, t, hq * 16:(hq + 1) * 16], ex[:], rst[:, 4:5], None, ALU.mult, None, [ex, rst], [gtfB[hq]])
            yield
            k.ts(paf[:], posf[:], 0.0625, -0.46875, ALU.mult, ALU.add, [posf], [paf])
            yield
            k.copy(pai[:], paf[:], [paf], [pai])
            yield
            k.copy(paf[:], pai[:], [pai], [paf])
            yield
            k.stt(pbf[:], paf[:], -16.0, posf[:], ALU.mult, ALU.add, [paf, posf], [pbf])
            yield
            for sd, pos_b in ((0, paf), (1, pbf)):
                k.tt(eq[:], pos_b.ap(0, [[1, 16], [0, 16]]), iota16.ap(0, [[0, 16], [1, 16]]), ALU.is_equal,
                     [pos_b, iota16], [eq])
                yield
                k.tt(eq[:], eq[:], s1i.ap(sd * 16, [[0, 16], [1, 16]]), ALU.mult, [eq, s1i], [eq])
                yield
                k.op("dve", lambda e, sd=sd: e.tensor_reduce(out=sel[:, sd, :], in_=eq[:], axis=AX.X, op=ALU.add),
                     reads=[eq], writes=[sel])
                yield
            k.stt(idxf[:, t, hq * 16:(hq + 1) * 16], sel[:, 0, :], 128.0, sel[:, 1, :], ALU.mult, ALU.add,
                  [sel], [idxfB[hq]])
            yield

        idxfB = [Buf(f"idxf_h{i}") for i in range(8)]
        gtfB = [Buf(f"gtf_h{i}") for i in range(8)]

        for half in range(2):
            for kt in range(KT):
                wst, d_ws = wsts[kt % 2], d_wss[kt % 2]
                k.dma("sp", wst[:], pwq_d[kt * 128:(kt + 1) * 128, half * 512:(half + 1) * 512], d_ws, writes=[wst])
                k.act(pwq[:, kt, :], wst[:], AF.Copy, [wst, vT], [pwq], scale=vT[:, G_FFN + kt:G_FFN + kt + 1])
            if RL <= 1.4:
                return
            def front(t):
                hnT3, qTt, scs = hnT3s[t % 2], qTts[t % 2], scss[t % 2]
                k.act(xs32[:], h[:, t, :], AF.Copy, [h, rstd3], [xs32], scale=rstd3[:, t:t + 1])
                for q4 in range(4):
                    pb = ps[q4]
                    for j in range(4):
                        kt = q4 * 4 + j
                        k.tr(pb[:, j * 128:(j + 1) * 128], xs32[:, kt * 128:(kt + 1) * 128], ident_f[:], [xs32, ident_f], [pb])
                    k.act(hnT3[:, q4 * 4:(q4 + 1) * 4, :], pb[:].rearrange("p (j t) -> p j t", j=4), AF.Copy, [pb], [hnT3])
                for hh in range(4):
                    pb = ps[4 + hh % 2]
                    for kt in range(KT):
                        k.mm(pb[:, 0:128], pwq[:, kt, hh * 128:(hh + 1) * 128], hnT3[:, kt, :], kt == 0, kt == KT - 1,
                             [pwq, hnT3], [pb])
                    k.act(qTt[:, hh, :], pb[:, 0:128], AF.Copy, [pb], [qTt])
                for hh in range(4):
                    hq = half * 4 + hh
                    for sd in range(2):
                        pb = ps[6 + sd]
                        k.mm(pb[:, hh * 128:(hh + 1) * 128], qTt[sd * 64:(sd + 1) * 64, hh, :],
                             keysT[sd * 64:(sd + 1) * 64, hq, :], True, True, [qTt, keysT], [pb])
                for sd in range(2):
                    k.act(scs.ap(sd * 128, [[256, 4], [1, 128]]), ps[6 + sd][:].rearrange("p (h n) -> p h n", h=4),
                          AF.Copy, [ps[6 + sd]], [scs])

            front(0)
            for t in range(8):
                if t + 1 < 8:
                    front(t + 1)
                scs = scss[t % 2]
                chains = [head_chain(SC[hh], scs, hh, half * 4 + hh, t) for hh in range(4)]
                live = list(chains)
                while live:
                    for g_ in list(live):
                        try:
                            next(g_)
                        except StopIteration:
                            live.remove(g_)
        if RL <= 6:
            return
        for t in range(8):
            k.tr(ps[0][:, 0:128], idxf[:, t, :], ident_f[:], [idxf, ident_f] + idxfB, [ps[0]])
            k.copy(idxT[:, t * 128:(t + 1) * 128], ps[0][:, 0:128], [ps[0]], [idxT])
            k.tr(ps[1][:, 0:128], gtf[:, t, :], ident_f[:], [gtf, ident_f] + gtfB, [ps[1]])
            k.copy(gT[:, t * 128:(t + 1) * 128], ps[1][:, 0:128], [ps[1]], [gT])

    route_body()
    k.release(mF)

    if stage == "route":
        obuf = k.sbuf("obuf", [128, D], F32)
        for t in range(8):
            k.memset(obuf[:], 0.0, [obuf])
            k.tr(ps[0][:, 0:128], gT[:, t * 128:(t + 1) * 128], ident_f[:], [gT, ident_f], [ps[0]])
            k.copy(obuf[:, 0:128], ps[0][:, 0:128], [ps[0]], [obuf])
            k.copy(gT[:, t * 128:(t + 1) * 128], idxT[:, t * 128:(t + 1) * 128], [idxT, ps[0]], [gT])
            k.tr(ps[1][:, 0:128], gT[:, t * 128:(t + 1) * 128], ident_f[:], [gT, ident_f], [ps[1]])
            k.copy(obuf[:, 128:256], ps[1][:, 0:128], [ps[1]], [obuf])
            for dd in (dbg_d, y_d):
                k.dma("sp", dd[t * 128:(t + 1) * 128, :], obuf[:], d_out, reads=[obuf])
        k.finish("sp")
        print("instructions", k.n_inst, "waits", k.n_wait)
        return nc

    mF3 = k.mark()
    Ebig = k.sbuf("Ebig", [128, 256], F32)
    k.memset(Ebig[:], 0.0, [Ebig])
    k.memset(Ebig[:, 127:128], 1.0, [Ebig])
    fng_b = k.sbuf("fng_b", [128, D], F32)
    k.dma("sp", fng_b[:], bass.AP(tensor=fng_d.tensor, offset=0, ap=[[0, 128], [1, D]]), d_setup, writes=[fng_b])
    k.group_end(d_setup, [fng_b])
    NG = 9
    UV = [k.sbuf(f"UV{i}", [128, 2 * D], BF16) for i in range(NG)]
    d_u = [k.dsem(f"u{i}") for i in range(NG)]
    lw = [k.sbuf(f"lw{i}", [128, 128], BF16) for i in range(4)]
    accs = [k.sbuf(f"accs{i}", [128, 8], F32) for i in range(2)]
    gact = [k.sbuf(f"gact{i}", [128, 1], F32) for i in range(2)]
    junkd = k.sbuf("junkd", [128, 1024], BF16)
    obuf = k.sbuf("obuf", [128, D], F32)
    fst = k.sbuf("fst", [128, 4], F32)
    d_o = k.dsem("o")
    NTOK = TOWN

    def issue_gather(tg):
        i = tg % NG
        k.dma_custom("pool", lambda e: e.indirect_dma_start(
            out=UV[i][:], out_offset=None, in_=puv_bf,
            in_offset=bass.IndirectOffsetOnAxis(ap=idxT[:, tg:tg + 1], axis=0)), d_u[i], reads=[idxT, tblU], writes=[UV[i]])

    def issue_bcast(tg):
        t, tl = tg // 128, tg % 128
        for n in range(4):
            k.mm(ps[n][:], ident_b[:, tl:tl + 1].to_broadcast([128, 128]), hn3b[:, t, n * 512:(n + 1) * 512], True, True,
                 [ident_b, hn3b], [ps[n]])

    def issue_dots(tg):
        i, ac = tg % NG, accs[tg % 2]
        for n in range(2):
            k.op("dve", lambda e, n=n: e.scalar_tensor_tensor(
                out=junkd[:], in0=UV[i][:, n * 1024:(n + 1) * 1024], scalar=1.0, in1=psd[n][:],
                op0=ALU.mult, op1=ALU.mult, accum_out=ac[:, n:n + 1]), reads=[UV[i], ps[2 * n], ps[2 * n + 1]],
                writes=[junkd, ac])
        k.act(gact[tg % 2][:], ac[:, 0:1], AF.Gelu_apprx_tanh, [ac], [gact[tg % 2]], bias=ac[:, 1:2])

    def issue_combine(tg):
        t, tl = tg // 128, tg % 128
        i, j = tg % NG, tg % 2
        lwj = lw[tg % 4]
        k.ts(lwj[:], Ebig[:, 127 - tl:255 - tl], gact[j][:, 0:1], gT[:, tg:tg + 1], ALU.mult, ALU.mult,
             [Ebig, gact[j], gT], [lwj], q="pool")
        for n in range(4):
            k.mm(ps[4 + n][:], lwj[:], UV[i][:, D + n * 512:D + (n + 1) * 512], tl == 0, tl == 127, [lwj, UV[i]],
                 [ps[4 + n]])
        if tl == 127:
            for n in range(4):
                k.tt(h[:, t, n * 512:(n + 1) * 512], h[:, t, n * 512:(n + 1) * 512], ps[4 + n][:], ALU.add,
                     [h, ps[4 + n]], [h])
            k.act(obuf[:], h[:, t, :], AF.Square, [h], [obuf, fst], accum_out=fst[:, 0:1])
            pool_rsqrt(fst, fst[:, 1:2], fst[:, 0:1], 1.0 / D, [128, 1])
            k.stt(obuf[:], h[:, t, :], fst[:, 1:2], fng_b[:], ALU.mult, ALU.mult, [h, fst, fng_b], [obuf])
            k.dma("sp", y_d[t * 128:(t + 1) * 128, :], obuf[:], d_o, reads=[obuf])
            if dbg_d is not None:
                k.dma("sp", dbg_d[t * 128:(t + 1) * 128, :], obuf[:], d_o, reads=[obuf])

    for tg in range(min(NG - 1, NTOK)):
        issue_gather(tg)
    issue_bcast(0)
    issue_dots(0)
    issue_bcast(1)
    for tg in range(NTOK):
        if tg + NG - 1 < NTOK:
            issue_gather(tg + NG - 1)
        if tg + 1 < NTOK:
            issue_dots(tg + 1)
        if tg + 2 < NTOK:
            issue_bcast(tg + 2)
        issue_combine(tg)
    k.finish("sp")
    print("instructions", k.n_inst, "waits", k.n_wait)
    return nc


def make_in_maps(inp):
    f = lambda a: np.ascontiguousarray(np.asarray(a, dtype=np.float32))
    x = f(inp["x"])
    mem = f(inp["mem"])
    vec_names = ["norm_mix_g", "norm_mem_g", "mem_norm_g", "norm_ffn_g"]
    rows = [f(inp[n]).reshape(16, 128) for n in vec_names]
    for n in ["conv_b", "conv_ln_g", "conv_ln_b", "grp_norm_conv_g", "grp_norm_ssm_g", "ssm_d"]:
        rows.append(f(inp[n]).reshape(8, 128))
    vecs = np.ascontiguousarray(np.concatenate(rows, axis=0))
    shared = {
        "vecs": vecs,
        "conv_w": f(inp["conv_w"]).reshape(31, 1024),
        "lam_re": f(inp["ssm_lam_re"]).reshape(32, 128),
        "lam_im": f(inp["ssm_lam_im"]).reshape(32, 128),
        "log_dt": f(inp["ssm_log_dt"]).reshape(32, 2),
        "b_re": f(inp["ssm_b_re"]).reshape(32, 2048),
        "b_im": f(inp["ssm_b_im"]).reshape(32, 2048),
        "c_re": f(inp["ssm_c_re"]).reshape(32, 2048),
        "c_im": f(inp["ssm_c_im"]).reshape(32, 2048),
        "w_in": f(inp["w_in"]).reshape(D, 3072),
        "glu_w": f(inp["ssm_glu_w"]).reshape(1024, 1024),
        "w_out": f(inp["w_out"]).reshape(D, D),
        "w_q": f(inp["w_q_mem"]).reshape(D, D),
        "w_kv": f(inp["w_kv_mem"]).reshape(D, 2 * D),
        "w_o": f(inp["w_o_mem"]).reshape(D, D),
        "peer_w_q": f(inp["peer_w_q"]).reshape(D, 1024),
        "peer_keys": f(inp["peer_keys"]).reshape(8, 2, 128, 64),
        "peer_u": f(inp["peer_u"]).reshape(16384, D),
        "peer_v": f(inp["peer_v"]).reshape(16384, D),
        "final_g": f(inp["final_norm_g"]).reshape(1, D),
    }
    maps = []
    for c in range(NCORES):
        b, seg = c // 4, c % 4
        xs = np.zeros((TALL, D), np.float32)
        n_have = (seg + 1) * TOWN
        xs[TALL - n_have:] = x[b, :n_have]
        m = dict(shared)
        m["xs"] = xs
        m["mem"] = np.ascontiguousarray(mem[b])
        maps.append(m)
    return maps


def kernel(**inputs):
    nc = build_program("full")
    in_maps = make_in_maps(inputs)
    res = run_bass_kernel_spmd(nc, in_maps, core_ids=list(range(NCORES)))
    out = np.zeros((2, 4096, D), np.float32)
    for c in range(NCORES):
        b, seg = c // 4, c % 4
        out[b, seg * TOWN:(seg + 1) * TOWN] = res.results[c]["y"]
    return out
```

```python
import math
import numpy as np
import concourse.bass as bass
import concourse.mybir as mybir
from concourse.bass_utils import run_bass_kernel_spmd

F32 = mybir.dt.float32
F32R = mybir.dt.float32r
BF16 = mybir.dt.bfloat16
U32 = mybir.dt.uint32
I32 = mybir.dt.int32
ALU = mybir.AluOpType
AF = mybir.ActivationFunctionType
AX = mybir.AxisListType

NCORES = 8
D = 2048
KT = 16
TOWN = 1024
TALL = 4096
BLK = 512
NBLK = TALL // BLK
EPS = 1e-6
TWO_PI_S = 6.28318


class Sem:
    def __init__(self, nc, name, step):
        self.h = nc.semaphore(name).__enter__()
        self.name = name
        self.cnt = 0
        self.step = step


class Buf:
    def __init__(self, name, t=None):
        self.name = name
        self.t = t
        self.last_w = None
        self.readers = {}

    def __getitem__(self, k):
        return self.t[k]

    def ap(self, offset, dims):
        full = self.t[:]
        pstep = full.ap[0][0]
        return bass.AP(tensor=self.t, offset=offset, ap=[[pstep, full.ap[0][1]]] + [list(d) for d in dims])

    def app(self, p0, pn, offset, dims):
        full = self.t[:]
        pstep = full.ap[0][0]
        return bass.AP(tensor=self.t, offset=p0 * pstep + offset, ap=[[pstep, pn]] + [list(d) for d in dims])


class K:
    def __init__(self, nc):
        self.nc = nc
        self.eng = {"pe": nc.tensor, "act": nc.scalar, "dve": nc.vector,
                    "pool": nc.gpsimd, "sp": nc.sync}
        self.qsem = {q: Sem(nc, "q_" + q, 1) for q in self.eng}
        self.seen = {q: {} for q in self.eng}
        self.n_inst = 0
        self.n_wait = 0
        self.dsems = []
        self._stack = []
        self._rstack = []
        self.fence = {}

    def sbuf(self, name, shape, dt=F32, side="left"):
        self._uid = getattr(self, "_uid", 0) + 1
        cm = self.nc.sbuf_tensor(f"sb{self._uid}_{name}", list(shape), dt, side=side)
        t = cm.__enter__()
        b = Buf(name, t)
        b.readers = dict(self.fence)
        (self._stack if side == "left" else self._rstack).append((cm, b))
        return b

    def psum(self, name, shape, dt=F32):
        cm = self.nc.psum_tensor(name, list(shape), dt)
        t = cm.__enter__()
        b = Buf(name, t)
        self._stack.append((cm, b))
        return b

    def mark(self, side="left"):
        return len(self._stack if side == "left" else self._rstack)

    def release(self, mark, side="left"):
        st = self._stack if side == "left" else self._rstack
        while len(st) > mark:
            cm, b = st.pop()
            deps = list(b.readers.items()) + ([b.last_w] if b.last_w else [])
            for sm, tick in deps:
                if self.fence.get(sm, 0) < tick:
                    self.fence[sm] = tick
            cm.__exit__(None, None, None)

    def dsem(self, name):
        s = Sem(self.nc, "d_" + name, 16)
        self.dsems.append(s)
        return s

    def _waits(self, q, reads, writes, skip=None):
        need = {}
        myq = self.qsem[q]

        def add(dep, same_ok):
            if dep is None:
                return
            s, tick = dep
            if s is skip:
                return
            if s is myq and not same_ok:
                return
            if need.get(s, 0) < tick:
                need[s] = tick

        same = q != "pe"
        for b in reads:
            add(b.last_w, same)
        for b in writes:
            add(b.last_w, same)
            for s, tick in b.readers.items():
                add((s, tick), same)
        eng = self.eng[q]
        seen = self.seen[q]
        for s, tick in need.items():
            if seen.get(s, 0) >= tick:
                continue
            eng.wait_ge(s.h, tick)
            self.n_wait += 1
            seen[s] = tick

    def _record(self, sem, reads, writes):
        tick = sem.cnt
        for b in reads:
            if b.readers.get(sem, 0) < tick:
                b.readers[sem] = tick
        for b in writes:
            b.last_w = (sem, tick)
            b.readers = {}

    def op(self, q, fn, reads=(), writes=()):
        self._waits(q, reads, writes)
        ins = fn(self.eng[q])
        s = self.qsem[q]
        s.cnt += 1
        ins.then_inc(s.h, 1)
        self.n_inst += 1
        self._record(s, reads, writes)
        return ins

    def dma(self, q, out, in_, dsem, reads=(), writes=(), **kw):
        self._waits(q, reads, writes, skip=dsem)
        ins = self.eng[q].dma_start(out=out, in_=in_, **kw)
        dsem.cnt += 16
        ins.then_inc(dsem.h, 16)
        self.n_inst += 1
        self._record(dsem, reads, writes)
        return ins

    def dma_custom(self, q, fn, dsem, reads=(), writes=()):
        self._waits(q, reads, writes)
        ins = fn(self.eng[q])
        dsem.cnt += 16
        ins.then_inc(dsem.h, 16)
        self.n_inst += 1
        self._record(dsem, reads, writes)
        return ins

    def group_end(self, dsem, bufs):
        for b in bufs:
            b.last_w = (dsem, dsem.cnt)

    def finish(self, q="sp"):
        eng = self.eng[q]
        for s in list(self.qsem.values()) + self.dsems:
            if s.cnt > 0 and s is not self.qsem[q]:
                eng.wait_ge(s.h, s.cnt)

    def mm(self, out, lhsT, rhs, start, stop, reads, writes):
        return self.op("pe", lambda e: e.matmul(out, lhsT=lhsT, rhs=rhs, start=start, stop=stop),
                       reads=reads, writes=writes)

    def tr(self, out, in_, ident, reads, writes):
        return self.op("pe", lambda e: e.transpose(out, in_, ident), reads=reads, writes=writes)

    def act(self, out, in_, func, reads, writes, **kw):
        return self.op("act", lambda e: e.activation(out=out, in_=in_, func=func, **kw), reads=reads, writes=writes)

    def tt(self, out, in0, in1, op, reads, writes, q="dve"):
        return self.op(q, lambda e: e.tensor_tensor(out=out, in0=in0, in1=in1, op=op), reads=reads, writes=writes)

    def ts(self, out, in0, s1, s2, op0, op1, reads, writes, q="dve"):
        if s2 is None:
            return self.op(q, lambda e: e.tensor_scalar(out, in0, s1, None, op0=op0), reads=reads, writes=writes)
        return self.op(q, lambda e: e.tensor_scalar(out, in0, s1, s2, op0=op0, op1=op1), reads=reads, writes=writes)

    def stt(self, out, in0, scalar, in1, op0, op1, reads, writes):
        return self.op("dve", lambda e: e.scalar_tensor_tensor(out=out, in0=in0, scalar=scalar, in1=in1,
                                                               op0=op0, op1=op1), reads=reads, writes=writes)

    def copy(self, out, in_, reads, writes, q="dve"):
        return self.op(q, lambda e: e.tensor_copy(out, in_), reads=reads, writes=writes)

    def memset(self, ap, val, writes, q="dve"):
        return self.op(q, lambda e: e.memset(ap, val), writes=writes)


def build_program(stage="full"):
    nc = bass.Bass("TRN2", target_bir_lowering=False)

    def din(name, shape, dt=F32):
        return nc.dram_tensor(name, list(shape), dt, kind="ExternalInput").ap()

    xs_d = din("xs", [TALL, D])
    mem_d = din("mem", [256, D])
    vecs_d = din("vecs", [112, 128])
    convw_d = din("conv_w", [31, 1024])
    lre_d = din("lam_re", [32, 128])
    lim_d = din("lam_im", [32, 128])
    ldt_d = din("log_dt", [32, 2])
    bre_d = din("b_re", [32, 2048])
    bim_d = din("b_im", [32, 2048])
    cre_d = din("c_re", [32, 2048])
    cim_d = din("c_im", [32, 2048])
    w_in_d = din("w_in", [D, 3072])
    glu_d = din("glu_w", [1024, 1024])
    w_out_d = din("w_out", [D, D])
    w_q_d = din("w_q", [D, D])
    w_kv_d = din("w_kv", [D, 2 * D])
    w_o_d = din("w_o", [D, D])
    pwq_d = din("peer_w_q", [D, 1024])
    pkeys_d = din("peer_keys", [8, 2, 128, 64])
    pu_d = din("peer_u", [16384, D])
    pv_d = din("peer_v", [16384, D])
    fng_d = din("final_g", [1, D])
    y_d = nc.dram_tensor("y", [TOWN, D], F32, kind="ExternalOutput").ap()
    dbg_d = None
    if stage != "full":
        dbg_d = nc.dram_tensor("dbg", [TOWN, D], F32, kind="ExternalOutput").ap()

    puv_bf = nc.dram_tensor("puv_bf", [16384, 2 * D], BF16, kind="Internal").ap()
    tblU = Buf("tblUV")

    k = K(nc)

    ident_f = k.sbuf("ident_f", [128, 128], F32, side="right")
    ident_b = k.sbuf("ident_b", [128, 128], BF16, side="right")
    ones_r = k.sbuf("ones_r", [128, 128], F32R, side="right")
    iot = k.sbuf("iot", [128, 128], F32, side="right")
    vT = k.sbuf("vT", [128, 112], F32, side="right")
    pidx = k.sbuf("pidx", [128, 1], F32, side="right")
    negh = k.sbuf("negh", [128, 1], F32, side="right")
    psd = [k.psum(f"psd{i}", [128, 1024], F32) for i in range(4)]
    ps = [Buf(f"ps{i}", psd[i // 2][:, (i % 2) * 512:(i % 2 + 1) * 512]) for i in range(8)]
    d_setup = k.dsem("setup")
    d_out = k.dsem("out")

    k.op("pool", lambda e: e.iota(iot[:], pattern=[[1, 128]], base=0, channel_multiplier=-1,
                                  allow_small_or_imprecise_dtypes=True), writes=[iot])
    k.ts(ident_f[:], iot[:], 0.0, None, ALU.is_equal, None, [iot], [ident_f])
    k.copy(ident_b[:], ident_f[:], [ident_f], [ident_b])
    k.memset(iot[:], 1.0, [iot])
    k.copy(ones_r[:], iot[:], [iot], [ones_r])
    k.op("pool", lambda e: e.iota(pidx[:], pattern=[[0, 1]], base=0, channel_multiplier=1,
                                  allow_small_or_imprecise_dtypes=True), writes=[pidx])

    k.memset(negh[:], -0.5, [negh])
    m0 = k.mark()
    vraw = k.sbuf("vraw", [112, 128], F32)
    k.dma("sp", vraw[:], vecs_d, d_setup, writes=[vraw])
    k.tr(ps[0][:, 0:112], vraw[:], ident_f[0:112, 0:112], [vraw, ident_f], [ps[0]])
    k.copy(vT[:], ps[0][:, 0:112], [ps[0]], [vT])
    k.release(m0)
    G_MIX, G_MEM, G_MEMN, G_FFN = 0, 16, 32, 48
    V_CB, V_LNG, V_LNB, V_GC, V_GS, V_SD = 64, 72, 80, 88, 96, 104

    NB = {}
    NXB = 4
    d_x = [k.dsem(f"x{i}") for i in range(NXB)]
    cnt = {"x": 0}

    def alloc_norm_bufs(with_x=True):
        if with_x:
            NB["xtile"] = [k.sbuf(f"xtile{i}", [128, D], F32) for i in range(NXB)]
        NB["xsb"] = [k.sbuf(f"xsb{i}", [128, D], BF16) for i in range(2)]
        NB["junk"] = k.sbuf("junk", [128, D], BF16)
        NB["stat"] = [k.sbuf(f"stat{i}", [128, 4], F32) for i in range(2)]

    def load_tok_tile(src_ap):
        i = cnt["x"] % 2
        cnt["x"] += 1
        k.dma("sp", NB["xtile"][i][:], src_ap, d_x[i], writes=[NB["xtile"][i]])
        return i

    def pool_rsqrt(buf, out_ap, in_ap, scale, shape):
        k.ts(out_ap, in_ap, scale, EPS, ALU.mult, ALU.add, [buf], [buf], q="pool")
        nh = negh[:, 0:1] if shape[1] == 1 else negh[:, 0:1].to_broadcast(list(shape))
        k.tt(out_ap, out_ap, nh, ALU.pow, [buf, negh], [buf], q="pool")

    def rstd_of(src_buf, src_ap, st, n):
        junk = NB["junk"]
        k.act(junk[:, 0:n], src_ap, AF.Square, [src_buf], [junk, st], accum_out=st[:, 0:1])
        pool_rsqrt(st, st[:, 1:2], st[:, 0:1], 1.0 / n, [128, 1])

    def fold_g(wb, g_off, ncols):
        for kt in range(KT):
            k.ts(wb[:, kt, 0:ncols], wb[:, kt, 0:ncols], vT[:, g_off + kt:g_off + kt + 1], None, ALU.mult, None,
                 [wb, vT], [wb])

    def norm_A(src_buf, src_ap, i):
        st = NB["stat"][i]
        rstd_of(src_buf, src_ap, st, D)
        k.ts(NB["xsb"][i][:], src_ap, st[:, 1:2], None, ALU.mult, None, [src_buf, st], [NB["xsb"][i]])

    def norm_B(dstT_buf, dst_ap_fn, pbank, i):
        xsb = NB["xsb"]
        for half in range(2):
            pb = ps[pbank + half]
            pv = pb[:].bitcast(BF16)
            for j in range(8):
                kt = half * 8 + j
                k.tr(pv[:, j * 128:(j + 1) * 128], xsb[i][:, kt * 128:(kt + 1) * 128], ident_b[:],
                     [xsb[i], ident_b], [pb])
            pin = pv.rearrange("p (j t) -> p j t", j=8)
            if half == 0:
                k.act(dst_ap_fn(half), pin, AF.Copy, [pb], [dstT_buf])
            else:
                k.copy(dst_ap_fn(half), pin, [pb], [dstT_buf])

    def norm_pipeline(items, post=None):
        n = len(items)

        def is_dram(t):
            return not isinstance(items[t][0], tuple)

        def issue_load(t):
            if is_dram(t):
                i = t % NXB
                k.dma("sp", NB["xtile"][i][:], items[t][0], d_x[i], writes=[NB["xtile"][i]])

        def doA(t):
            if is_dram(t):
                xb = NB["xtile"][t % NXB]
                norm_A(xb, xb[:], t % 2)
            else:
                norm_A(items[t][0][0], items[t][0][1], t % 2)
        for t in range(min(NXB - 1, n)):
            issue_load(t)
        doA(0)
        for t in range(n):
            if t + NXB - 1 < n:
                issue_load(t + NXB - 1)
            if t + 1 < n:
                doA(t + 1)
            norm_B(items[t][1], items[t][2], 0, t % 2)
            if post is not None:
                post(t)

    uT = k.sbuf("uT", [128, 8, TALL], BF16)
    mA = k.mark()
    alloc_norm_bufs()
    xtile = NB["xtile"]
    w_ssm = k.sbuf("w_ssm", [128, KT, 1024], BF16)
    d_w = k.dsem("w0")
    for kt in range(KT):
        k.dma("pool", w_ssm[:, kt, :], w_in_d[kt * 128:(kt + 1) * 128, 2048:3072], d_w, writes=[w_ssm])
    k.group_end(d_w, [w_ssm])
    fold_g(w_ssm, G_MIX, 1024)
    xnTs = [k.sbuf(f"xnT{i}", [128, KT, BLK], BF16) for i in range(2)]
    def a1_src(r0):
        def f():
            i = load_tok_tile(xs_d[r0:r0 + 128, :])
            return xtile[i], xtile[i][:]
        return f

    a1_pending = []

    def a1_post(t):
        if t % 4 == 3:
            a1_pending.extend((t // 4, ct) for ct in range(8))
        n_issue = len(a1_pending) if t == 4 * NBLK - 1 else min(2, len(a1_pending))
        for _ in range(n_issue):
            b, ct = a1_pending.pop(0)
            xnT = xnTs[b % 2]
            pb = ps[2 + ct % 2]
            for kt in range(KT):
                k.mm(pb[:], w_ssm[:, kt, ct * 128:(ct + 1) * 128], xnT[:, kt, :], kt == 0, kt == KT - 1,
                     [w_ssm, xnT], [pb])
            k.act(uT[:, ct, b * BLK:(b + 1) * BLK], pb[:], AF.Copy, [pb], [uT])

    items = []
    for b in range(NBLK):
        for tl in range(4):
            xnT = xnTs[b % 2]
            r0 = b * BLK + tl * 128
            items.append((xs_d[r0:r0 + 128, :], xnT,
                          lambda half, tl=tl, xnT=xnT: xnT.ap(half * 8 * BLK + tl * 128, [[BLK, 8], [1, 128]])))
    norm_pipeline(items, a1_post)
    k.release(mA)

    mS = k.mark()
    s5w = {nm: k.sbuf(nm, [128, 32, 128], BF16) for nm in ("Bre_pad", "Bim_pad", "WA", "NWA", "NWB")}
    parT = k.sbuf("parT", [128, 18, 32], F32)
    jvec = k.sbuf("jvec", [128, BLK], F32)
    k.op("pool", lambda e: e.iota(jvec[:], pattern=[[1, BLK]], base=0, channel_multiplier=0,
                                  allow_small_or_imprecise_dtypes=True), writes=[jvec])
    jrev = k.sbuf("jrev", [128, BLK], F32)
    k.op("pool", lambda e: e.iota(jrev[:], pattern=[[-1, BLK]], base=BLK - 1, channel_multiplier=0,
                                  allow_small_or_imprecise_dtypes=True), writes=[jrev])
    mP = k.mark()
    la = {nm: k.sbuf("la_" + nm, [32, 128], F32) for nm in
          ("lre", "lim", "dtb", "ar", "th", "rho", "phi", "f", "fa", "sinv", "cosv", "lbr", "lbi", "nr", "den",
           "t1", "t2", "cr", "ci", "f512", "g512", "ga512")}
    lai = k.sbuf("la_i", [32, 128], I32)
    par = k.sbuf("par", [32, 18, 128], F32)
    ldt = k.sbuf("ldt", [32, 2], F32)
    dt2 = k.sbuf("dt2", [32, 2], F32)
    k.dma("sp", la["lre"][:], lre_d, d_setup, writes=[la["lre"]])
    k.dma("sp", la["lim"][:], lim_d, d_setup, writes=[la["lim"]])
    k.dma("sp", ldt[:], ldt_d, d_setup, writes=[ldt])
    k.group_end(d_setup, [la["lre"], la["lim"], ldt, vraw])
    k.act(dt2[:], ldt[:], AF.Exp, [ldt], [dt2])
    k.copy(la["dtb"].ap(0, [[64, 2], [1, 64]]), dt2.ap(0, [[1, 2], [0, 64]]), [dt2], [la["dtb"]])

    def L(n):
        return la[n][:]
    k.tt(L("ar"), L("lre"), L("dtb"), ALU.mult, [la["lre"], la["dtb"]], [la["ar"]])
    k.tt(L("th"), L("lim"), L("dtb"), ALU.mult, [la["lim"], la["dtb"]], [la["th"]])
    k.act(L("rho"), L("ar"), AF.Exp, [la["ar"]], [la["rho"]])
    k.ts(L("phi"), L("th"), 1.0 / (2 * math.pi), None, ALU.mult, None, [la["th"]], [la["phi"]])

    def frac(dst, src):
        k.copy(lai[:], la[src][:], [la[src]], [lai])
        k.tt(la[dst][:], la[src][:], lai[:], ALU.subtract, [la[src], lai], [la[dst]])

    def sincos(sin_dst, cos_dst, fsrc, tmp):
        k.act(la[sin_dst][:], la[fsrc][:], AF.Sin, [la[fsrc]], [la[sin_dst]], scale=TWO_PI_S)
        k.act(la[tmp][:], la[fsrc][:], AF.Abs, [la[fsrc]], [la[tmp]])
        k.act(la[cos_dst][:], la[tmp][:], AF.Sin, [la[tmp]], [la[cos_dst]], scale=-TWO_PI_S, bias=math.pi / 2)

    frac("f", "phi")
    sincos("sinv", "cosv", "f", "fa")
    k.tt(L("lbr"), L("rho"), L("cosv"), ALU.mult, [la["rho"], la["cosv"]], [la["lbr"]])
    k.tt(L("lbi"), L("rho"), L("sinv"), ALU.mult, [la["rho"], la["sinv"]], [la["lbi"]])
    k.ts(L("nr"), L("lbr"), -1.0, None, ALU.add, None, [la["lbr"]], [la["nr"]])
    k.tt(L("t1"), L("lre"), L("lre"), ALU.mult, [la["lre"]], [la["t1"]])
    k.tt(L("t2"), L("lim"), L("lim"), ALU.mult, [la["lim"]], [la["t2"]])
    k.tt(L("den"), L("t1"), L("t2"), ALU.add, [la["t1"], la["t2"]], [la["den"]])
    k.op("dve", lambda e: e.reciprocal(L("den"), L("den")), reads=[la["den"]], writes=[la["den"]])
    k.tt(L("t1"), L("nr"), L("lre"), ALU.mult, [la["nr"], la["lre"]], [la["t1"]])
    k.tt(L("t2"), L("lbi"), L("lim"), ALU.mult, [la["lbi"], la["lim"]], [la["t2"]])
    k.tt(L("cr"), L("t1"), L("t2"), ALU.add, [la["t1"], la["t2"]], [la["cr"]])
    k.tt(L("cr"), L("cr"), L("den"), ALU.mult, [la["cr"], la["den"]], [la["cr"]])
    k.tt(L("t1"), L("lbi"), L("lre"), ALU.mult, [la["lbi"], la["lre"]], [la["t1"]])
    k.tt(L("t2"), L("nr"), L("lim"), ALU.mult, [la["nr"], la["lim"]], [la["t2"]])
    k.tt(L("ci"), L("t1"), L("t2"), ALU.subtract, [la["t1"], la["t2"]], [la["ci"]])
    k.tt(L("ci"), L("ci"), L("den"), ALU.mult, [la["ci"], la["den"]], [la["ci"]])
    k.ts(L("f512"), L("f"), float(BLK), None, ALU.mult, None, [la["f"]], [la["f512"]])
    frac("g512", "f512")
    k.copy(par[:, 0, :], L("f"), [la["f"]], [par])
    k.copy(par[:, 1, :], L("rho"), [la["rho"]], [par])
    sincos("t1", "t2", "g512", "ga512")
    k.copy(par[:, 2, :], L("t2"), [la["t2"]], [par])
    k.copy(par[:, 3, :], L("t1"), [la["t1"]], [par])
    k.ts(par[:, 4, :], L("t1"), -1.0, None, ALU.mult, None, [la["t1"]], [par])
    k.copy(par[:, 5, :], L("ar"), [la["ar"]], [par])
    for b_ in range(6):
        kk = 5 - b_
        k.ts(L("f512"), L("g512"), float(kk), None, ALU.mult, None, [la["g512"]], [la["f512"]])
        frac("ga512", "f512")
        sincos("t1", "t2", "ga512", "fa")
        k.act(L("lbr"), L("ar"), AF.Exp, [la["ar"]], [la["lbr"]], scale=float(BLK * kk))
        k.tt(par[:, 6 + b_, :], L("lbr"), L("t2"), ALU.mult, [la["lbr"], la["t2"]], [par])
        k.tt(par[:, 12 + b_, :], L("lbr"), L("t1"), ALU.mult, [la["lbr"], la["t1"]], [par])
    for i in range(18):
        pb_ = ps[i // 16]
        k.tr(pb_[:, (i % 16) * 32:(i % 16 + 1) * 32], par[:, i, :], ident_f[0:32, 0:32], [par, ident_f], [pb_])
    k.copy(parT[:, 0:16, :].rearrange("p a b -> p (a b)"), ps[0][:, 0:512], [ps[0]], [parT])
    k.copy(parT[:, 16:18, :].rearrange("p a b -> p (a b)"), ps[1][:, 0:64], [ps[1]], [parT])

    XT = {nm: k.sbuf("XT_" + nm, [128, 32, 16], F32) for nm in ("bre", "bim", "cre", "cim")}
    mP1 = k.mark()
    raw_re = k.sbuf("raw_re", [32, 2048], F32)
    raw_im = k.sbuf("raw_im", [32, 2048], F32)
    Bre = k.sbuf("Bre", [32, 2048], F32)
    Bim = k.sbuf("Bim", [32, 2048], F32)
    tb = k.sbuf("tb", [32, 2048], F32)
    d_raw = k.dsem("raw")
    k.dma("sp", raw_re[:], bre_d, d_raw, writes=[raw_re])
    k.dma("sp", raw_im[:], bim_d, d_raw, writes=[raw_im])
    k.group_end(d_raw, [raw_re, raw_im])

    def v3(buf):
        return buf.ap(0, [[16, 128], [1, 16]])

    def bc(buf):
        return buf.ap(0, [[1, 128], [0, 16]])
    k.tt(v3(Bre), v3(raw_re), bc(la["cr"]), ALU.mult, [raw_re, la["cr"]], [Bre])
    k.tt(v3(tb), v3(raw_im), bc(la["ci"]), ALU.mult, [raw_im, la["ci"]], [tb])
    k.tt(Bre[:], Bre[:], tb[:], ALU.subtract, [Bre, tb], [Bre])
    k.tt(v3(Bim), v3(raw_im), bc(la["cr"]), ALU.mult, [raw_im, la["cr"]], [Bim])
    k.tt(v3(tb), v3(raw_re), bc(la["ci"]), ALU.mult, [raw_re, la["ci"]], [tb])
    k.tt(Bim[:], Bim[:], tb[:], ALU.add, [Bim, tb], [Bim])

    def to_XT(nm, in_fn, rd):
        pb = ps[1]
        for h in range(16):
            k.tr(pb[:, h * 32:(h + 1) * 32], in_fn(h), ident_f[0:32, 0:32], rd + [ident_f], [pb])
        k.copy(XT[nm].ap(0, [[1, 16], [16, 32]]), pb[:].rearrange("p (h a) -> p h a", h=16), [pb], [XT[nm]])

    to_XT("bre", lambda h: Bre.ap(h, [[16, 128]]), [Bre])
    to_XT("bim", lambda h: Bim.ap(h, [[16, 128]]), [Bim])
    k.dma("sp", raw_re[:], cre_d, d_raw, writes=[raw_re])
    k.dma("sp", raw_im[:], cim_d, d_raw, writes=[raw_im])
    k.group_end(d_raw, [raw_re, raw_im])
    for nm, src in (("cre", raw_re), ("cim", raw_im)):
        k.copy(tb.ap(0, [[128, 16], [64, 2], [1, 64]]), src.ap(0, [[64, 16], [1024, 2], [1, 64]]), [src], [tb])
        to_XT(nm, lambda h: tb[:, h * 128:(h + 1) * 128], [tb])
    k.release(mP1)

    mask2 = k.sbuf("mask2", [128, 4], F32)
    k.ts(mask2[:, 0:1], pidx[:], 64.0, None, ALU.is_lt, None, [pidx], [mask2])
    k.ts(mask2[:, 1:2], pidx[:], 64.0, None, ALU.is_ge, None, [pidx], [mask2])
    k.ts(mask2[:, 2:4], mask2[:, 0:2], -1.0, None, ALU.mult, None, [mask2], [mask2])

    padf = k.sbuf("padf", [128, 32, 128], F32)

    def build_padded(dst_buf, src, neg, dt_direct):
        tgt = dst_buf if dt_direct else padf
        k.memset(tgt[:], 0.0, [tgt])
        for g2 in range(2):
            for b4 in range(4):
                o_ap = tgt.ap(b4 * 128 + b4 * 32 + 16 * g2, [[512, 8], [1, 16]])
                i_ap = src.ap(b4 * 16, [[64, 8], [1, 16]])
                mcol = mask2[:, (2 if neg else 0) + g2:(2 if neg else 0) + g2 + 1]
                k.ts(o_ap, i_ap, mcol, None, ALU.mult, None, [src, mask2], [tgt])

    build_padded(s5w["WA"], XT["cre"], False, True)
    build_padded(s5w["NWA"], XT["cre"], True, True)
    build_padded(s5w["NWB"], XT["cim"], True, True)
    for nm, dst in (("bre", "Bre_pad"), ("bim", "Bim_pad")):
        build_padded(None, XT[nm], False, False)
        for a in range(8):
            pb = ps[2 + a % 2]
            for b4 in range(4):
                k.tr(pb[:, b4 * 128:(b4 + 1) * 128], padf[:, a * 4 + b4, :], ident_f[:], [padf, ident_f], [pb])
            k.copy(s5w[dst][:, a * 4:(a + 1) * 4, :].rearrange("p a b -> p (a b)"), pb[:], [pb], [s5w[dst]])
    k.release(mP)


    mark_yn = k.mark("right")
    yn = k.sbuf("yn", [128, 16, TOWN], BF16, side="right")
    mZ = k.mark("right")
    zb = k.sbuf("zb", [128, 8, TOWN], BF16, side="right")
    tabC = [k.sbuf(f"tabC{i}", [128, BLK], F32) for i in range(2)]
    tabS = [k.sbuf(f"tabS{i}", [128, BLK], F32) for i in range(2)]
    tabWCS = [k.sbuf("tWCS", [128, 2 * BLK], F32)] * 2
    tabNSC = [k.sbuf("tNSC", [128, 2 * BLK], F32)] * 2
    g_fr = k.sbuf("g_fr", [128, BLK], F32)
    g_wt = g_fr
    pacc = k.sbuf("pacc", [128, 6, 4], F32)
    psr = k.sbuf("psr", [128, 2, 6], F32)
    pcc = k.sbuf("pcc", [128, 8], F32)
    j6 = k.sbuf("j6", [128, 6], F32)
    print("S5 loop: sbuf bytes remaining", nc.sbuf_bytes_remaining)
    angi = k.sbuf("angi", [128, BLK], I32)
    tmp = [k.sbuf(f"s5t{i}", [128, BLK], F32) for i in range(4)]
    ang, fr, fab = tmp[0], tmp[1], tmp[2]
    g_ang = tmp[1]
    g_fab = g_ang
    bh0 = [k.sbuf(f"bh{j}", [128, BLK], F32) for j in range(2)]
    zz0 = [k.sbuf(f"zz{j}", [128, BLK], F32) for j in range(2)]
    bh = [bh0, bh0]
    zz = [zz0, zz0]
    pp = [[k.sbuf(f"pp{i}{j}", [128, BLK], BF16) for j in range(4)] for i in range(2)]
    init = [k.sbuf(f"init{i}", [128, 2], F32) for i in range(2)]
    tini = k.sbuf("tini", [128, 2], F32)
    ytmp = tmp[0]
    d_cv = k.dsem("cv")
    CR = 512
    for c in range(16384 // CR):
        k.dma("pool", puv_bf[c * CR:(c + 1) * CR, 0:D], pu_d[c * CR:(c + 1) * CR, :], d_cv, writes=[tblU])
        k.dma("pool", puv_bf[c * CR:(c + 1) * CR, D:2 * D], pv_d[c * CR:(c + 1) * CR, :], d_cv, writes=[tblU])
    k.group_end(d_cv, [tblU])
    it = 0
    def gen_tables_front(P):
        sl = P % 2
        k.ts(g_ang[:], jvec[:], parT[:, 0, P:P + 1], None, ALU.mult, None, [jvec, parT], [g_ang])
        k.copy(angi[:], g_ang[:], [g_ang], [angi])
        k.tt(g_fr[:], g_ang[:], angi[:], ALU.subtract, [g_ang, angi], [g_fr])
        k.act(tabS[sl][:], g_fr[:], AF.Sin, [g_fr], [tabS[sl]], scale=TWO_PI_S)
        k.act(g_fab[:], g_fr[:], AF.Abs, [g_fr], [g_fab])
        k.act(tabC[sl][:], g_fab[:], AF.Sin, [g_fab], [tabC[sl]], scale=-TWO_PI_S, bias=math.pi / 2)
        k.act(g_wt[:], jrev[:], AF.Exp, [jrev, parT], [g_wt], scale=parT[:, 5, P:P + 1])

    def gen_tables_back(P):
        sl = P % 2
        wcs, nsc = tabWCS[sl], tabNSC[sl]
        k.tt(wcs[:, 0:BLK], tabC[sl][:], g_wt[:], ALU.mult, [tabC[sl], g_wt], [wcs])
        k.tt(wcs[:, BLK:2 * BLK], tabS[sl][:], g_wt[:], ALU.mult, [tabS[sl], g_wt], [wcs])
        k.ts(nsc[:, 0:BLK], wcs[:, BLK:2 * BLK], -1.0, None, ALU.mult, None, [wcs], [nsc])
        k.copy(nsc[:, BLK:2 * BLK], wcs[:, 0:BLK], [wcs], [nsc])

    gen_tables_front(0)
    gen_tables_back(0)
    for P in range(32):
        ct = P // 4
        tC, tS, tWCS, tNSC = tabC[P % 2], tabS[P % 2], tabWCS[P % 2], tabNSC[P % 2]
        if P + 1 < 32:
            gen_tables_front(P + 1)
        rho_b = parT[:, 1, P:P + 1].to_broadcast([128, BLK])
        NPB = NBLK - 2
        for b in range(NBLK):
            s = it % 2
            it += 1
            pre, pim = ps[4 + 2 * s], ps[5 + 2 * s]
            rhs = uT[:, ct, b * BLK:(b + 1) * BLK]
            k.mm(pre[:], s5w["Bre_pad"][:, P, :], rhs, True, True, [s5w["Bre_pad"], uT], [pre])
            k.mm(pim[:], s5w["Bim_pad"][:, P, :], rhs, True, True, [s5w["Bim_pad"], uT], [pim])
            if b < NPB:
                for q_, tw in enumerate((tWCS, tNSC)):
                    k.op("dve", lambda e, tw=tw, q_=q_: e.scalar_tensor_tensor(
                        out=tmp[0][:].bitcast(BF16), in0=tw[:], scalar=1.0, in1=psd[2 + s][:], op0=ALU.mult,
                        op1=ALU.mult, accum_out=psr[:, q_, b:b + 1]), reads=[tw, pre, pim], writes=[tmp[0], psr])
                if b == NPB - 1:
                    mre = parT.ap(6 * 32 + P, [[32, 6]])
                    mim = parT.ap(12 * 32 + P, [[32, 6]])
                    for q_, (sv, mw) in enumerate(((0, mre), (1, mim), (1, mre), (0, mim))):
                        k.op("dve", lambda e, sv=sv, mw=mw, q_=q_: e.scalar_tensor_tensor(
                            out=j6[:], in0=psr[:, sv, :], scalar=1.0, in1=mw, op0=ALU.mult, op1=ALU.mult,
                            accum_out=pcc[:, q_:q_ + 1]), reads=[psr, parT], writes=[j6, pcc])
                    k.tt(pcc[:, 4:5], pcc[:, 0:1], pcc[:, 1:2], ALU.subtract, [pcc], [pcc])
                    k.tt(pcc[:, 5:6], pcc[:, 2:3], pcc[:, 3:4], ALU.add, [pcc], [pcc])
                    nin = init[(b + 1) % 2]
                    c5, s5c, ns5 = parT[:, 2, P:P + 1], parT[:, 3, P:P + 1], parT[:, 4, P:P + 1]
                    k.tt(tini[:, 0:1], pcc[:, 4:5], c5, ALU.mult, [pcc, parT], [tini])
                    k.tt(tini[:, 1:2], pcc[:, 5:6], c5, ALU.mult, [pcc, parT], [tini])
                    k.stt(nin[:, 0:1], pcc[:, 5:6], ns5, tini[:, 0:1], ALU.mult, ALU.add, [pcc, parT, tini], [nin])
                    k.stt(nin[:, 1:2], pcc[:, 4:5], s5c, tini[:, 1:2], ALU.mult, ALU.add, [pcc, parT, tini], [nin])
                continue
            k.tt(tmp[0][:], tC[:], pre[:], ALU.mult, [tC, pre], [tmp[0]])
            k.tt(tmp[1][:], tS[:], pim[:], ALU.mult, [tS, pim], [tmp[1]])
            k.tt(bh[s][0][:], tmp[0][:], tmp[1][:], ALU.add, [tmp[0], tmp[1]], [bh[s][0]])
            k.tt(tmp[2][:], tC[:], pim[:], ALU.mult, [tC, pim], [tmp[2]])
            k.tt(tmp[3][:], tS[:], pre[:], ALU.mult, [tS, pre], [tmp[3]])
            k.tt(bh[s][1][:], tmp[2][:], tmp[3][:], ALU.subtract, [tmp[2], tmp[3]], [bh[s][1]])
            ini = init[b % 2]
            for c in range(2):
                iv = ini[:, c:c + 1]
                rd = [parT, bh[s][c], ini]
                k.op("dve", lambda e, c=c, iv=iv: e.tensor_tensor_scan(
                    out=zz[s][c][:], data0=rho_b, data1=bh[s][c][:], initial=iv, op0=ALU.mult, op1=ALU.add),
                    reads=rd, writes=[zz[s][c]])
            if b < NBLK - 1:
                nin = init[(b + 1) % 2]
                e_re, e_im = zz[s][0][:, BLK - 1:BLK], zz[s][1][:, BLK - 1:BLK]
                c5, s5c, ns5 = parT[:, 2, P:P + 1], parT[:, 3, P:P + 1], parT[:, 4, P:P + 1]
                k.tt(tini[:, 0:1], e_re, c5, ALU.mult, [zz[s][0], parT], [tini])
                k.tt(tini[:, 1:2], e_im, c5, ALU.mult, [zz[s][1], parT], [tini])
                k.stt(nin[:, 0:1], e_im, ns5, tini[:, 0:1], ALU.mult, ALU.add, [zz[s][1], parT, tini], [nin])
                k.stt(nin[:, 1:2], e_re, s5c, tini[:, 1:2], ALU.mult, ALU.add, [zz[s][0], parT, tini], [nin])
            if b >= NBLK - 2:
                ob = b - (NBLK - 2)
                yb = ps[2 + ob]
                k.tt(pp[s][0][:], tC[:], zz[s][0][:], ALU.mult, [tC, zz[s][0]], [pp[s][0]])
                k.tt(pp[s][1][:], tS[:], zz[s][1][:], ALU.mult, [tS, zz[s][1]], [pp[s][1]])
                k.tt(pp[s][2][:], tS[:], zz[s][0][:], ALU.mult, [tS, zz[s][0]], [pp[s][2]])
                k.tt(pp[s][3][:], tC[:], zz[s][1][:], ALU.mult, [tC, zz[s][1]], [pp[s][3]])
                wl = ["WA", "NWA", "NWB", "NWB"]
                for j in range(4):
                    k.mm(yb[:], s5w[wl[j]][:, P, :], pp[s][j][:], (P % 4 == 0 and j == 0), (P % 4 == 3 and j == 3),
                         [s5w[wl[j]], pp[s][j]], [yb])
                if P % 4 == 3:
                    k.stt(ytmp[:], uT[:, ct, b * BLK:(b + 1) * BLK], vT[:, V_SD + ct:V_SD + ct + 1], yb[:],
                          ALU.mult, ALU.add, [uT, vT, yb], [ytmp])
                    k.act(zb[:, ct, ob * BLK:(ob + 1) * BLK], ytmp[:], AF.Gelu_apprx_tanh, [ytmp], [zb])

        if P + 1 < 32:
            gen_tables_back(P + 1)

    if stage == "s5":
        obuf = k.sbuf("obuf", [128, D], F32)
        for tt_ in range(8):
            k.memset(obuf[:], 0.0, [obuf])
            for ct in range(8):
                pvb = ps[0][:].bitcast(BF16)
                k.tr(pvb[:, 0:128], zb[:, ct, tt_ * 128:(tt_ + 1) * 128], ident_b[:], [zb, ident_b], [ps[0]])
                k.copy(obuf[:, ct * 128:(ct + 1) * 128], pvb[:, 0:128], [ps[0]], [obuf])
            k.dma("sp", dbg_d[tt_ * 128:(tt_ + 1) * 128, :], obuf[:], d_out, reads=[obuf])
            k.dma("sp", y_d[tt_ * 128:(tt_ + 1) * 128, :], obuf[:], d_out, reads=[obuf])
        k.finish("sp")
        print("instructions", k.n_inst, "waits", k.n_wait)
        return nc

    k.release(mS)
    k.release(0)

    def chan_rstd(dst, src_sq_fn, nct, blk, bank, extra_mean=None):
        pb = ps[bank]
        for ct in range(nct):
            sqb = src_sq_fn(ct)
            k.mm(pb[:], ones_r[:], sqb[:], ct == 0, ct == nct - 1, [ones_r, sqb], [pb])

    glu_w = k.sbuf("glu_w", [128, 8, 1024], BF16, side="right")
    d_w1 = k.dsem("w1")
    for kt in range(8):
        k.dma("pool", glu_w[:, kt, :], glu_d[kt * 128:(kt + 1) * 128, :], d_w1, writes=[glu_w])
    k.group_end(d_w1, [glu_w])
    w_ag = k.sbuf("w_ag", [128, KT, 2048], BF16)
    d_w2 = k.dsem("w2")
    for kt in range(KT):
        k.dma("pool", w_ag[:, kt, :], w_in_d[kt * 128:(kt + 1) * 128, 0:2048], d_w2, writes=[w_ag])
    k.group_end(d_w2, [w_ag])
    mG = k.mark()
    ysf = k.sbuf("ysf", [128, 8, TOWN], F32)
    sg = [k.sbuf(f"sg{i}", [128, BLK], F32) for i in range(2)]
    sq = [k.sbuf(f"sq{i}", [128, BLK], F32R) for i in range(2)]
    rsd = k.sbuf("rsd", [128, BLK], F32)
    tmpn = k.sbuf("tmpn", [128, BLK], F32)
    for blk in range(2):
        tsl = slice(blk * BLK, (blk + 1) * BLK)
        for ct in range(8):
            pb = ps[ct % 2]
            for kt in range(8):
                k.mm(pb[:], glu_w[:, kt, ct * 128:(ct + 1) * 128], zb[:, kt, tsl], kt == 0, kt == 7, [glu_w, zb], [pb])
            k.act(sg[ct % 2][:], pb[:], AF.Sigmoid, [pb], [sg[ct % 2]])
            k.tt(ysf[:, ct, tsl], zb[:, ct, tsl], sg[ct % 2][:], ALU.mult, [zb, sg[ct % 2]], [ysf])
        pb = ps[2]
        for ct in range(8):
            k.act(sq[ct % 2][:], ysf[:, ct, tsl], AF.Square, [ysf], [sq[ct % 2]])
            k.mm(pb[:], ones_r[:], sq[ct % 2][:], ct == 0, ct == 7, [ones_r, sq[ct % 2]], [pb])
        k.act(rsd[:], pb[:], AF.Sqrt, [pb], [rsd], scale=1.0 / 1024, bias=EPS)
        k.op("dve", lambda e: e.reciprocal(rsd[:], rsd[:]), reads=[rsd], writes=[rsd])
        for ct in range(8):
            k.stt(yn[:, 8 + ct, tsl], ysf[:, ct, tsl], vT[:, V_GS + ct:V_GS + ct + 1], rsd[:], ALU.mult, ALU.mult,
                  [ysf, vT, rsd], [yn])
    k.release(mG)
    k.release(mZ, "right")

    mCr = k.mark("right")
    HAL = 30
    cbuf = k.sbuf("cbuf", [128, 8, HAL + TOWN + 2], BF16, side="right")
    cw = k.sbuf("cw", [128, 8, 31], F32, side="right")
    cwraw = k.sbuf("cwraw", [31, 1024], F32)
    k.dma("sp", cwraw[:], convw_d, d_setup, writes=[cwraw])
    k.group_end(d_setup, [cwraw])
    for ct in range(8):
        k.tr(ps[0][:, ct * 32:ct * 32 + 31], cwraw[:, ct * 128:(ct + 1) * 128], ident_f[0:31, 0:31], [cwraw, ident_f], [ps[0]])
    k.copy(cw[:], ps[0][:, 0:256].rearrange("p (c k) -> p c k", c=8)[:, :, 0:31], [ps[0]], [cw])
    alloc_norm_bufs()
    xtile = NB["xtile"]
    fold_g(w_ag, G_MIX, 2048)
    xnT = k.sbuf("xnT2", [128, KT, BLK], BF16)
    sgc = [k.sbuf(f"sgc{i}", [128, BLK], F32) for i in range(2)]
    def cv_post(t):
        if t % 4 != 3:
            return
        b = 5 + t // 4
        for ct in range(8):
            pa, pg = ps[2 + 2 * (ct % 2)], ps[3 + 2 * (ct % 2)]
            for kt in range(KT):
                k.mm(pg[:], w_ag[:, kt, 1024 + ct * 128:1024 + (ct + 1) * 128], xnT[:, kt, :], kt == 0, kt == KT - 1,
                     [w_ag, xnT], [pg])
            for kt in range(KT):
                k.mm(pa[:], w_ag[:, kt, ct * 128:(ct + 1) * 128], xnT[:, kt, :], kt == 0, kt == KT - 1,
                     [w_ag, xnT], [pa])
            k.act(sgc[ct % 2][:], pg[:], AF.Sigmoid, [pg], [sgc[ct % 2]])
            if b == 5:
                k.tt(cbuf[:, ct, 0:HAL], pa[:, BLK - HAL:BLK], sgc[ct % 2][:, BLK - HAL:BLK], ALU.mult,
                     [pa, sgc[ct % 2]], [cbuf])
            else:
                o = HAL + (b - 6) * BLK
                k.tt(cbuf[:, ct, o:o + BLK], pa[:], sgc[ct % 2][:], ALU.mult, [pa, sgc[ct % 2]], [cbuf])

    items = []
    for b in range(5, 8):
        for tl in range(4):
            r0 = b * BLK + tl * 128
            items.append((xs_d[r0:r0 + 128, :], xnT,
                          lambda half, tl=tl: xnT.ap(half * 8 * BLK + tl * 128, [[BLK, 8], [1, 128]])))
    norm_pipeline(items, cv_post)
    k.release(0)
    cv = k.sbuf("cv", [128, 8, TOWN], F32)
    dg = k.sbuf("dg", [128, 8, 31, 128], BF16, side="right")
    for ct in range(8):
        for kk in range(31):
            k.ts(dg[:, ct, kk, :], ident_f[:], cw[:, ct, kk:kk + 1], None, ALU.mult, None, [ident_f, cw], [dg],
                 q="dve")
    for ct in range(8):
        for blk in range(2):
            pb = ps[2 + (ct * 2 + blk) % 4]
            for kk in range(31):
                k.mm(pb[:], dg[:, ct, kk, :], cbuf[:, ct, blk * BLK + kk:blk * BLK + kk + BLK], kk == 0, kk == 30,
                     [dg, cbuf], [pb])
            k.act(cv[:, ct, blk * BLK:(blk + 1) * BLK], pb[:], AF.Identity, [pb, vT], [cv],
                  bias=vT[:, V_CB + ct:V_CB + ct + 1])
    k.release(mCr, "right")
    wbuf = k.sbuf("wbuf", [128, KT, D], BF16, side="right")
    d_w3 = k.dsem("w3")
    for kt in range(KT):
        k.dma("pool", wbuf[:, kt, :], w_out_d[kt * 128:(kt + 1) * 128, :], d_w3, writes=[wbuf])
    k.group_end(d_w3, [wbuf])
    sqc = [k.sbuf(f"sqc{i}", [128, BLK], F32R) for i in range(2)]
    cvr = [k.sbuf(f"cvr{i}", [128, BLK], F32R) for i in range(2)]
    mean = k.sbuf("mean", [128, BLK], F32)
    var = k.sbuf("var", [128, BLK], F32)
    m2 = k.sbuf("m2", [128, BLK], F32)
    rsd2 = k.sbuf("rsd2", [128, BLK], F32)
    for blk in range(2):
        tsl = slice(blk * BLK, (blk + 1) * BLK)
        p1, p2 = ps[0], ps[1]
        for ct in range(8):
            k.copy(cvr[ct % 2][:], cv[:, ct, tsl], [cv], [cvr[ct % 2]], q="pool")
            k.act(sqc[ct % 2][:], cv[:, ct, tsl], AF.Square, [cv], [sqc[ct % 2]])
            k.mm(p1[:], ones_r[:], cvr[ct % 2][:], ct == 0, ct == 7, [ones_r, cvr[ct % 2]], [p1])
            k.mm(p2[:], ones_r[:], sqc[ct % 2][:], ct == 0, ct == 7, [ones_r, sqc[ct % 2]], [p2])
        k.act(mean[:], p1[:], AF.Copy, [p1], [mean], scale=1.0 / 1024)
        k.tt(m2[:], mean[:], mean[:], ALU.mult, [mean], [m2])
        k.stt(var[:], p2[:], 1.0 / 1024, m2[:], ALU.mult, ALU.subtract, [p2, m2], [var])
        k.act(var[:], var[:], AF.Sqrt, [var], [var], bias=EPS)
        k.op("dve", lambda e: e.reciprocal(var[:], var[:]), reads=[var], writes=[var])
        for ct in range(8):
            k.tt(cv[:, ct, tsl], cv[:, ct, tsl], mean[:], ALU.subtract, [cv, mean], [cv])
            k.tt(cv[:, ct, tsl], cv[:, ct, tsl], var[:], ALU.mult, [cv, var], [cv])
            k.act(cv[:, ct, tsl], cv[:, ct, tsl], AF.Silu, [cv, vT], [cv],
                  scale=vT[:, V_LNG + ct:V_LNG + ct + 1], bias=vT[:, V_LNB + ct:V_LNB + ct + 1])
        p3 = ps[2]
        for ct in range(8):
            k.act(sqc[ct % 2][:], cv[:, ct, tsl], AF.Square, [cv], [sqc[ct % 2]])
            k.mm(p3[:], ones_r[:], sqc[ct % 2][:], ct == 0, ct == 7, [ones_r, sqc[ct % 2]], [p3])
        k.act(rsd2[:], p3[:], AF.Sqrt, [p3], [rsd2], scale=1.0 / 1024, bias=EPS)
        k.op("dve", lambda e: e.reciprocal(rsd2[:], rsd2[:]), reads=[rsd2], writes=[rsd2])
        for ct in range(8):
            k.stt(yn[:, ct, tsl], cv[:, ct, tsl], vT[:, V_GC + ct:V_GC + ct + 1], rsd2[:], ALU.mult, ALU.mult,
                  [cv, vT, rsd2], [yn])
    k.release(0)

    h = k.sbuf("h", [128, 8, D], F32)
    mW = k.mark()
    xres = [k.sbuf(f"xres{i}", [128, D], F32) for i in range(2)]
    d_xr = [k.dsem(f"xr{i}") for i in range(2)]
    for t in range(8):
        xr = xres[t % 2]
        k.dma("sp", xr[:], xs_d[TALL - TOWN + t * 128:TALL - TOWN + (t + 1) * 128, :], d_xr[t % 2], writes=[xr])
        for n in range(4):
            pb = ps[n % 2]
            for kt in range(KT):
                k.mm(pb[:], yn[:, kt, t * 128:(t + 1) * 128], wbuf[:, kt, n * 512:(n + 1) * 512], kt == 0, kt == KT - 1,
                     [yn, wbuf], [pb])
            k.tt(h[:, t, n * 512:(n + 1) * 512], pb[:], xr[:, n * 512:(n + 1) * 512], ALU.add, [pb, xr], [h])
    k.release(mW)
    k.release(mark_yn, "right")

    def dump_h(dst_list):
        for t in range(8):
            for dd in dst_list:
                k.dma("sp", dd[t * 128:(t + 1) * 128, :], h[:, t, :], d_out, reads=[h])

    if stage == "h1":
        dump_h([dbg_d, y_d])
        k.finish("sp")
        print("instructions", k.n_inst, "waits", k.n_wait)
        return nc

    mE = k.mark()
    KTb = k.sbuf("KTb", [128, 16, 256], BF16)
    Vb = k.sbuf("Vb", [128, 2, D], BF16)
    wch = [k.sbuf(f"wch{i}", [128, KT, 512], BF16) for i in range(2)]
    d_wc = [k.dsem(f"wc{i}") for i in range(2)]
    wcnt = {"n": 0}

    def load_wchunk(w_d, c0, g_off=None):
        i = wcnt["n"] % 2
        wcnt["n"] += 1
        for q4 in range(4):
            src = w_d[q4 * 512:(q4 + 1) * 512, c0:c0 + 512].rearrange("(kt p) c -> p kt c", p=128)
            k.dma("pool", wch[i][:, q4 * 4:(q4 + 1) * 4, :], src, d_wc[i], writes=[wch[i]])
        k.group_end(d_wc[i], [wch[i]])
        if g_off is not None:
            fold_g(wch[i], g_off, 512)
        return wch[i]

    mE1 = k.mark()
    alloc_norm_bufs()
    xtile = NB["xtile"]
    memT = k.sbuf("memT", [128, KT, 256], BF16)
    def mem_src(mt):
        def f():
            i = load_tok_tile(mem_d[mt * 128:(mt + 1) * 128, :])
            return xtile[i], xtile[i][:]
        return f
    norm_pipeline([(mem_d[mt * 128:(mt + 1) * 128, :], memT,
                    lambda half, mt=mt: memT.ap(half * 8 * 256 + mt * 128, [[256, 8], [1, 128]])) for mt in range(2)])
    for c in range(8):
        wc = load_wchunk(w_kv_d, c * 512, G_MEMN)
        if c < 4:
            for j in range(4):
                pb = ps[2 + j % 2]
                for kt in range(KT):
                    k.mm(pb[:, 0:256], wc[:, kt, j * 128:(j + 1) * 128], memT[:, kt, :], kt == 0, kt == KT - 1,
                         [wc, memT], [pb])
                k.act(KTb[:, c * 4 + j, :], pb[:, 0:256], AF.Copy, [pb], [KTb])
        else:
            for mt in range(2):
                pb = ps[2 + mt]
                for kt in range(KT):
                    k.mm(pb[:], memT[:, kt, mt * 128:(mt + 1) * 128], wc[:, kt, :], kt == 0, kt == KT - 1,
                         [wc, memT], [pb])
                k.act(Vb[:, mt, (c - 4) * 512:(c - 3) * 512], pb[:], AF.Copy, [pb], [Vb])
    k.release(mE1)
    qT = k.sbuf("qT", [128, 16, TOWN], BF16)
    hnT = k.sbuf("hnT", [128, 16, TOWN], BF16)
    mE2 = k.mark()
    alloc_norm_bufs(False)
    norm_pipeline([((h, h[:, t, :]), hnT,
                    lambda half, t=t: hnT.ap(half * 8 * TOWN + t * 128, [[TOWN, 8], [1, 128]])) for t in range(8)])
    k.release(mE2)
    qscale = 1.0 / math.sqrt(512.0)
    for c in range(4):
        wc = load_wchunk(w_q_d, c * 512, G_MEM)
        for j in range(4):
            for blk in range(2):
                pb = ps[2 + (2 * j + blk) % 2]
                for kt in range(KT):
                    k.mm(pb[:], wc[:, kt, j * 128:(j + 1) * 128], hnT[:, kt, blk * BLK:(blk + 1) * BLK], kt == 0,
                         kt == KT - 1, [wc, hnT], [pb])
                k.act(qT[:, c * 4 + j, blk * BLK:(blk + 1) * BLK], pb[:], AF.Copy, [pb], [qT], scale=qscale)
    PT = k.sbuf("PT", [128, 8, TOWN], BF16)
    pex = [k.sbuf(f"pex{i}", [128, 256], F32) for i in range(2)]
    pnb = [k.sbuf(f"pnb{i}", [128, 256], BF16) for i in range(2)]
    sst = [k.sbuf(f"sst{i}", [128, 4], F32) for i in range(2)]
    def att_scores(it_):
        hd, t = divmod(it_, 8)
        pb = ps[2 + it_ % 2]
        for j in range(4):
            k.mm(pb[:, 0:256], qT[:, hd * 4 + j, t * 128:(t + 1) * 128], KTb[:, hd * 4 + j, :], j == 0, j == 3,
                 [qT, KTb], [pb])

    def att_softmax(it_):
        hd, t = divmod(it_, 8)
        i = it_ % 2
        pb = ps[2 + i]
        st = sst[i]
        k.op("dve", lambda e: e.tensor_reduce(out=st[:, 0:1], in_=pb[:, 0:256], axis=AX.X, op=ALU.max),
             reads=[pb], writes=[st])
        k.ts(st[:, 1:2], st[:, 0:1], -1.0, None, ALU.mult, None, [st], [st])
        k.act(pex[i][:], pb[:, 0:256], AF.Exp, [pb, st], [pex[i], st], bias=st[:, 1:2], accum_out=st[:, 2:3])
        k.op("dve", lambda e: e.reciprocal(st[:, 3:4], st[:, 2:3]), reads=[st], writes=[st])
        k.ts(pnb[i][:], pex[i][:], st[:, 3:4], None, ALU.mult, None, [pex[i], st], [pnb[i]])
        pt = ps[4 + i]
        ptv = pt[:].bitcast(BF16)
        for mt in range(2):
            k.tr(ptv[:, mt * 128:(mt + 1) * 128], pnb[i][:, mt * 128:(mt + 1) * 128], ident_b[:], [pnb[i], ident_b], [pt])
        k.copy(PT.ap(hd * 2 * TOWN + t * 128, [[TOWN, 2], [1, 128]]),
               ptv[:, 0:256].rearrange("p (m t) -> p m t", m=2), [pt], [PT])

    att_scores(0)
    for it_ in range(32):
        if it_ + 1 < 32:
            att_scores(it_ + 1)
        att_softmax(it_)
    OT = hnT
    for hd in range(4):
        for dvt in range(4):
            for blk in range(2):
                pb = ps[2 + (dvt * 2 + blk) % 2]
                for mt in range(2):
                    k.mm(pb[:], Vb[:, mt, (hd * 4 + dvt) * 128:(hd * 4 + dvt + 1) * 128],
                         PT[:, hd * 2 + mt, blk * BLK:(blk + 1) * BLK], mt == 0, mt == 1, [Vb, PT], [pb])
                k.act(OT[:, hd * 4 + dvt, blk * BLK:(blk + 1) * BLK], pb[:], AF.Copy, [pb], [OT])
    for n in range(4):
        wc = load_wchunk(w_o_d, n * 512)
        for t in range(8):
            pb = ps[2 + t % 2]
            for ct in range(KT):
                k.mm(pb[:], OT[:, ct, t * 128:(t + 1) * 128], wc[:, ct, :], ct == 0, ct == KT - 1, [OT, wc], [pb])
            k.tt(h[:, t, n * 512:(n + 1) * 512], h[:, t, n * 512:(n + 1) * 512], pb[:], ALU.add, [h, pb], [h])
    k.release(mE)

    if stage == "h2":
        dump_h([dbg_d, y_d])
        k.finish("sp")
        print("instructions", k.n_inst, "waits", k.n_wait)
        return nc

    idxT = k.sbuf("idxT", [128, TOWN], U32)
    gT = k.sbuf("gT", [128, TOWN], F32)
    hn3b = k.sbuf("hn3b", [128, 8, D], BF16)
    rstd3 = k.sbuf("rstd3", [128, 16], F32)
    mF0 = k.mark()
    gffn_b = k.sbuf("gffn_b", [128, D], F32)
    k.dma("sp", gffn_b[:], bass.AP(tensor=vecs_d.tensor, offset=G_FFN * 128, ap=[[0, 128], [1, D]]), d_setup,
          writes=[gffn_b])
    k.group_end(d_setup, [gffn_b])
    xs0 = k.sbuf("xs0", [128, D], F32)
    for t in range(8):
        k.act(xs0[:], h[:, t, :], AF.Square, [h], [xs0, rstd3], accum_out=rstd3[:, 8 + t:9 + t])
        pool_rsqrt(rstd3, rstd3[:, t:t + 1], rstd3[:, 8 + t:9 + t], 1.0 / D, [128, 1])
        k.stt(hn3b[:, t, :], h[:, t, :], rstd3[:, t:t + 1], gffn_b[:], ALU.mult, ALU.mult, [h, rstd3, gffn_b], [hn3b])
    k.release(mF0)
    mF = k.mark()
    import os
    RL = float(os.environ.get("RL", "99")) if stage == "route" else 99

    def route_body():
        idxf = k.sbuf("idxf", [128, 8, 128], F32)
        gtf = k.sbuf("gtf", [128, 8, 128], F32)
        keysT = k.sbuf("keysT", [128, 8, 128], F32)
        kraw = k.sbuf("kraw", [128, 128], F32)
        d_kr = k.dsem("kr")
        for hq in range(8):
            k.dma("sp", kraw[:].rearrange("n (s c) -> n s c", s=2), pkeys_d[hq].rearrange("s n c -> n s c"), d_kr,
                  writes=[kraw])
            k.tr(ps[0][:, 0:128], kraw[:], ident_f[:], [kraw, ident_f], [ps[0]])
            k.copy(keysT[:, hq, :], ps[0][:, 0:128], [ps[0]], [keysT])
        if RL <= 1:
            return
        iota16 = k.sbuf("iota16", [128, 16], F32)
        k.op("pool", lambda e: e.iota(iota16[:], pattern=[[1, 16]], base=0, channel_multiplier=0,
                                      allow_small_or_imprecise_dtypes=True), writes=[iota16])
        pwq = k.sbuf("pwq", [128, KT, 512], F32R)
        wsts = [k.sbuf(f"wst{i}", [128, 512], F32) for i in range(2)]
        d_wss = [k.dsem(f"ws{i}") for i in range(2)]
        xs32 = k.sbuf("xs32", [128, D], F32)
        hnT3s = [k.sbuf(f"hnT3{i}", [128, KT, 128], F32R) for i in range(2)]
        qTts = [k.sbuf(f"qTt{i}", [128, 4, 128], F32) for i in range(2)]
        scss = [k.sbuf(f"scs{i}", [128, 8, 128], F32) for i in range(2)]
        NCH = 4
        SC = []
        for c_ in range(NCH):
            SC.append(dict(
                scw=k.sbuf(f"scw{c_}", [128, 256], F32), cand=k.sbuf(f"cand{c_}", [128, 256], F32),
                s1v=k.sbuf(f"s1v{c_}", [128, 2, 16], F32), s1i=k.sbuf(f"s1i{c_}", [128, 2, 16], F32),
                iu=k.sbuf(f"iu{c_}", [128, 16], U32), tv=k.sbuf(f"tv{c_}", [128, 16], F32),
                posf=k.sbuf(f"posf{c_}", [128, 16], F32), pai=k.sbuf(f"pai{c_}", [128, 16], I32),
                paf=k.sbuf(f"paf{c_}", [128, 16], F32), pbf=k.sbuf(f"pbf{c_}", [128, 16], F32),
                eq=k.sbuf(f"eq{c_}", [128, 16, 16], F32), sel=k.sbuf(f"sel{c_}", [128, 2, 16], F32),
                rst=k.sbuf(f"rst{c_}", [128, 8], F32), ex=k.sbuf(f"ex{c_}", [128, 16], F32)))

        def top16(B, vals_buf, vals_ap, src_buf, src_ap, idx_buf, idx_ap, n):
            scw, iu = B["scw"], B["iu"]
            k.op("dve", lambda e: e.max(out=vals_ap[:, 0:8], in_=src_ap), reads=[src_buf], writes=[vals_buf])
            yield
            k.op("dve", lambda e: e.match_replace(out=scw[:, 0:n], in_to_replace=vals_ap[:, 0:8], in_values=src_ap,
                                                  imm_value=-1e30), reads=[src_buf, vals_buf], writes=[scw])
            yield
            k.op("dve", lambda e: e.max(out=vals_ap[:, 8:16], in_=scw[:, 0:n]), reads=[scw], writes=[vals_buf])
            yield
            k.op("dve", lambda e: e.max_index(out=iu[:, 0:8], in_max=vals_ap[:, 0:8], in_values=src_ap),
                 reads=[src_buf, vals_buf], writes=[iu])
            yield
            k.op("dve", lambda e: e.max_index(out=iu[:, 8:16], in_max=vals_ap[:, 8:16], in_values=src_ap),
                 reads=[src_buf, vals_buf], writes=[iu])
            yield
            k.copy(idx_ap, iu[:], [iu], [idx_buf])
            yield

        def head_chain(B, scs, hh, hq, t):
            s1v, s1i, cand, tv, posf = B["s1v"], B["s1i"], B["cand"], B["tv"], B["posf"]
            rst, ex, paf, pai, pbf, eq, sel = B["rst"], B["ex"], B["paf"], B["pai"], B["pbf"], B["eq"], B["sel"]
            for sd in range(2):
                yield from top16(B, s1v, s1v[:, sd, :], scs, scs[:, hh * 2 + sd, :], s1i, s1i[:, sd, :], 128)
            k.tt(cand[:].rearrange("p (a b) -> p a b", a=16), s1v.ap(0, [[1, 16], [0, 16]]),
                 s1v.ap(16, [[0, 16], [1, 16]]), ALU.add, [s1v], [cand])
            yield
            yield from top16(B, tv, tv[:], cand, cand[:], posf, posf[:], 256)
            k.ts(rst[:, 2:3], tv[:, 0:1], -1.0, None, ALU.mult, None, [tv], [rst])
            yield
            k.act(ex[:], tv[:], AF.Exp, [tv, rst], [ex, rst], bias=rst[:, 2:3], accum_out=rst[:, 3:4])
            yield
            k.op("dve", lambda e: e.reciprocal(rst[:, 4:5], rst[:, 3:4]), reads=[rst], writes=[rst])
            yield
            k.ts(gtf[:, t, hq * 16:(hq + 1) * 16], ex[:], rst[:, 4:5], None, ALU.mult, None, [ex, rst], [gtfB[hq]])
            yield
            k.ts(paf[:], posf[:], 0.0625, -0.46875, ALU.mult, ALU.add, [posf], [paf])
            yield
            k.copy(pai[:], paf[:], [paf], [pai])
            yield
            k.copy(paf[:], pai[:], [pai], [paf])
            yield
            k.stt(pbf[:], paf[:], -16.0, posf[:], ALU.mult, ALU.add, [paf, posf], [pbf])
            yield
            for sd, pos_b in ((0, paf), (1, pbf)):
                k.tt(eq[:], pos_b.ap(0, [[1, 16], [0, 16]]), iota16.ap(0, [[0, 16], [1, 16]]), ALU.is_equal,
                     [pos_b, iota16], [eq])
                yield
                k.tt(eq[:], eq[:], s1i.ap(sd * 16, [[0, 16], [1, 16]]), ALU.mult, [eq, s1i], [eq])
                yield
                k.op("dve", lambda e, sd=sd: e.tensor_reduce(out=sel[:, sd, :], in_=eq[:], axis=AX.X, op=ALU.add),
                     reads=[eq], writes=[sel])
                yield
            k.stt(idxf[:, t, hq * 16:(hq + 1) * 16], sel[:, 0, :], 128.0, sel[:, 1, :], ALU.mult, ALU.add,
                  [sel], [idxfB[hq]])
            yield

        idxfB = [Buf(f"idxf_h{i}") for i in range(8)]
        gtfB = [Buf(f"gtf_h{i}") for i in range(8)]

        for half in range(2):
            for kt in range(KT):
                wst, d_ws = wsts[kt % 2], d_wss[kt % 2]
                k.dma("sp", wst[:], pwq_d[kt * 128:(kt + 1) * 128, half * 512:(half + 1) * 512], d_ws, writes=[wst])
                k.act(pwq[:, kt, :], wst[:], AF.Copy, [wst, vT], [pwq], scale=vT[:, G_FFN + kt:G_FFN + kt + 1])
            if RL <= 1.4:
                return
            def front(t):
                hnT3, qTt, scs = hnT3s[t % 2], qTts[t % 2], scss[t % 2]
                k.act(xs32[:], h[:, t, :], AF.Copy, [h, rstd3], [xs32], scale=rstd3[:, t:t + 1])
                for q4 in range(4):
                    pb = ps[q4]
                    for j in range(4):
                        kt = q4 * 4 + j
                        k.tr(pb[:, j * 128:(j + 1) * 128], xs32[:, kt * 128:(kt + 1) * 128], ident_f[:], [xs32, ident_f], [pb])
                    k.act(hnT3[:, q4 * 4:(q4 + 1) * 4, :], pb[:].rearrange("p (j t) -> p j t", j=4), AF.Copy, [pb], [hnT3])
                for hh in range(4):
                    pb = ps[4 + hh % 2]
                    for kt in range(KT):
                        k.mm(pb[:, 0:128], pwq[:, kt, hh * 128:(hh + 1) * 128], hnT3[:, kt, :], kt == 0, kt == KT - 1,
                             [pwq, hnT3], [pb])
                    k.act(qTt[:, hh, :], pb[:, 0:128], AF.Copy, [pb], [qTt])
                for hh in range(4):
                    hq = half * 4 + hh
                    for sd in range(2):
                        pb = ps[6 + sd]
                        k.mm(pb[:, hh * 128:(hh + 1) * 128], qTt[sd * 64:(sd + 1) * 64, hh, :],
                             keysT[sd * 64:(sd + 1) * 64, hq, :], True, True, [qTt, keysT], [pb])
                for sd in range(2):
                    k.act(scs.ap(sd * 128, [[256, 4], [1, 128]]), ps[6 + sd][:].rearrange("p (h n) -> p h n", h=4),
                          AF.Copy, [ps[6 + sd]], [scs])

            front(0)
            for t in range(8):
                if t + 1 < 8:
                    front(t + 1)
                scs = scss[t % 2]
                chains = [head_chain(SC[hh], scs, hh, half * 4 + hh, t) for hh in range(4)]
                live = list(chains)
                while live:
                    for g_ in list(live):
                        try:
                            next(g_)
                        except StopIteration:
                            live.remove(g_)
        if RL <= 6:
            return
        for t in range(8):
            k.tr(ps[0][:, 0:128], idxf[:, t, :], ident_f[:], [idxf, ident_f] + idxfB, [ps[0]])
            k.copy(idxT[:, t * 128:(t + 1) * 128], ps[0][:, 0:128], [ps[0]], [idxT])
            k.tr(ps[1][:, 0:128], gtf[:, t, :], ident_f[:], [gtf, ident_f] + gtfB, [ps[1]])
            k.copy(gT[:, t * 128:(t + 1) * 128], ps[1][:, 0:128], [ps[1]], [gT])

    route_body()
    k.release(mF)

    if stage == "route":
        obuf = k.sbuf("obuf", [128, D], F32)
        for t in range(8):
            k.memset(obuf[:], 0.0, [obuf])
            k.tr(ps[0][:, 0:128], gT[:, t * 128:(t + 1) * 128], ident_f[:], [gT, ident_f], [ps[0]])
            k.copy(obuf[:, 0:128], ps[0][:, 0:128], [ps[0]], [obuf])
            k.copy(gT[:, t * 128:(t + 1) * 128], idxT[:, t * 128:(t + 1) * 128], [idxT, ps[0]], [gT])
            k.tr(ps[1][:, 0:128], gT[:, t * 128:(t + 1) * 128], ident_f[:], [gT, ident_f], [ps[1]])
            k.copy(obuf[:, 128:256], ps[1][:, 0:128], [ps[1]], [obuf])
            for dd in (dbg_d, y_d):
                k.dma("sp", dd[t * 128:(t + 1) * 128, :], obuf[:], d_out, reads=[obuf])
        k.finish("sp")
        print("instructions", k.n_inst, "waits", k.n_wait)
        return nc

    mF3 = k.mark()
    Ebig = k.sbuf("Ebig", [128, 256], F32)
    k.memset(Ebig[:], 0.0, [Ebig])
    k.memset(Ebig[:, 127:128], 1.0, [Ebig])
    fng_b = k.sbuf("fng_b", [128, D], F32)
    k.dma("sp", fng_b[:], bass.AP(tensor=fng_d.tensor, offset=0, ap=[[0, 128], [1, D]]), d_setup, writes=[fng_b])
    k.group_end(d_setup, [fng_b])
    NG = 9
    UV = [k.sbuf(f"UV{i}", [128, 2 * D], BF16) for i in range(NG)]
    d_u = [k.dsem(f"u{i}") for i in range(NG)]
    lw = [k.sbuf(f"lw{i}", [128, 128], BF16) for i in range(4)]
    accs = [k.sbuf(f"accs{i}", [128, 8], F32) for i in range(2)]
    gact = [k.sbuf(f"gact{i}", [128, 1], F32) for i in range(2)]
    junkd = k.sbuf("junkd", [128, 1024], BF16)
    obuf = k.sbuf("obuf", [128, D], F32)
    fst = k.sbuf("fst", [128, 4], F32)
    d_o = k.dsem("o")
    NTOK = TOWN

    def issue_gather(tg):
        i = tg % NG
        k.dma_custom("pool", lambda e: e.indirect_dma_start(
            out=UV[i][:], out_offset=None, in_=puv_bf,
            in_offset=bass.IndirectOffsetOnAxis(ap=idxT[:, tg:tg + 1], axis=0)), d_u[i], reads=[idxT, tblU], writes=[UV[i]])

    def issue_bcast(tg):
        t, tl = tg // 128, tg % 128
        for n in range(4):
            k.mm(ps[n][:], ident_b[:, tl:tl + 1].to_broadcast([128, 128]), hn3b[:, t, n * 512:(n + 1) * 512], True, True,
                 [ident_b, hn3b], [ps[n]])

    def issue_dots(tg):
        i, ac = tg % NG, accs[tg % 2]
        for n in range(2):
            k.op("dve", lambda e, n=n: e.scalar_tensor_tensor(
                out=junkd[:], in0=UV[i][:, n * 1024:(n + 1) * 1024], scalar=1.0, in1=psd[n][:],
                op0=ALU.mult, op1=ALU.mult, accum_out=ac[:, n:n + 1]), reads=[UV[i], ps[2 * n], ps[2 * n + 1]],
                writes=[junkd, ac])
        k.act(gact[tg % 2][:], ac[:, 0:1], AF.Gelu_apprx_tanh, [ac], [gact[tg % 2]], bias=ac[:, 1:2])

    def issue_combine(tg):
        t, tl = tg // 128, tg % 128
        i, j = tg % NG, tg % 2
        lwj = lw[tg % 4]
        k.ts(lwj[:], Ebig[:, 127 - tl:255 - tl], gact[j][:, 0:1], gT[:, tg:tg + 1], ALU.mult, ALU.mult,
             [Ebig, gact[j], gT], [lwj], q="pool")
        for n in range(4):
            k.mm(ps[4 + n][:], lwj[:], UV[i][:, D + n * 512:D + (n + 1) * 512], tl == 0, tl == 127, [lwj, UV[i]],
                 [ps[4 + n]])
        if tl == 127:
            for n in range(4):
                k.tt(h[:, t, n * 512:(n + 1) * 512], h[:, t, n * 512:(n + 1) * 512], ps[4 + n][:], ALU.add,
                     [h, ps[4 + n]], [h])
            k.act(obuf[:], h[:, t, :], AF.Square, [h], [obuf, fst], accum_out=fst[:, 0:1])
            pool_rsqrt(fst, fst[:, 1:2], fst[:, 0:1], 1.0 / D, [128, 1])
            k.stt(obuf[:], h[:, t, :], fst[:, 1:2], fng_b[:], ALU.mult, ALU.mult, [h, fst, fng_b], [obuf])
            k.dma("sp", y_d[t * 128:(t + 1) * 128, :], obuf[:], d_o, reads=[obuf])
            if dbg_d is not None:
                k.dma("sp", dbg_d[t * 128:(t + 1) * 128, :], obuf[:], d_o, reads=[obuf])

    for tg in range(min(NG - 1, NTOK)):
        issue_gather(tg)
    issue_bcast(0)
    issue_dots(0)
    issue_bcast(1)
    for tg in range(NTOK):
        if tg + NG - 1 < NTOK:
            issue_gather(tg + NG - 1)
        if tg + 1 < NTOK:
            issue_dots(tg + 1)
        if tg + 2 < NTOK:
            issue_bcast(tg + 2)
        issue_combine(tg)
    k.finish("sp")
    print("instructions", k.n_inst, "waits", k.n_wait)
    return nc


def make_in_maps(inp):
    f = lambda a: np.ascontiguousarray(np.asarray(a, dtype=np.float32))
    x = f(inp["x"])
    mem = f(inp["mem"])
    vec_names = ["norm_mix_g", "norm_mem_g", "mem_norm_g", "norm_ffn_g"]
    rows = [f(inp[n]).reshape(16, 128) for n in vec_names]
    for n in ["conv_b", "conv_ln_g", "conv_ln_b", "grp_norm_conv_g", "grp_norm_ssm_g", "ssm_d"]:
        rows.append(f(inp[n]).reshape(8, 128))
    vecs = np.ascontiguousarray(np.concatenate(rows, axis=0))
    shared = {
        "vecs": vecs,
        "conv_w": f(inp["conv_w"]).reshape(31, 1024),
        "lam_re": f(inp["ssm_lam_re"]).reshape(32, 128),
        "lam_im": f(inp["ssm_lam_im"]).reshape(32, 128),
        "log_dt": f(inp["ssm_log_dt"]).reshape(32, 2),
        "b_re": f(inp["ssm_b_re"]).reshape(32, 2048),
        "b_im": f(inp["ssm_b_im"]).reshape(32, 2048),
        "c_re": f(inp["ssm_c_re"]).reshape(32, 2048),
        "c_im": f(inp["ssm_c_im"]).reshape(32, 2048),
        "w_in": f(inp["w_in"]).reshape(D, 3072),
        "glu_w": f(inp["ssm_glu_w"]).reshape(1024, 1024),
        "w_out": f(inp["w_out"]).reshape(D, D),
        "w_q": f(inp["w_q_mem"]).reshape(D, D),
        "w_kv": f(inp["w_kv_mem"]).reshape(D, 2 * D),
        "w_o": f(inp["w_o_mem"]).reshape(D, D),
        "peer_w_q": f(inp["peer_w_q"]).reshape(D, 1024),
        "peer_keys": f(inp["peer_keys"]).reshape(8, 2, 128, 64),
        "peer_u": f(inp["peer_u"]).reshape(16384, D),
        "peer_v": f(inp["peer_v"]).reshape(16384, D),
        "final_g": f(inp["final_norm_g"]).reshape(1, D),
    }
    maps = []
    for c in range(NCORES):
        b, seg = c // 4, c % 4
        xs = np.zeros((TALL, D), np.float32)
        n_have = (seg + 1) * TOWN
        xs[TALL - n_have:] = x[b, :n_have]
        m = dict(shared)
        m["xs"] = xs
        m["mem"] = np.ascontiguousarray(mem[b])
        maps.append(m)
    return maps


def kernel(**inputs):
    nc = build_program("full")
    in_maps = make_in_maps(inputs)
    res = run_bass_kernel_spmd(nc, in_maps, core_ids=list(range(NCORES)))
    out = np.zeros((2, 4096, D), np.float32)
    for c in range(NCORES):
        b, seg = c // 4, c % 4
        out[b, seg * TOWN:(seg + 1) * TOWN] = res.results[c]["y"]
    return out
```
